# Optimizing a Trainium2 kernel written in Bass

```python
import jax, jax.numpy as jnp
from jax import lax
import numpy as np


D_MODEL = 1024
BATCH = 8
SEQ = 4096
DEPTH = 2

CTX_LEN = 256
GRID_W = 64
HEAD_DIM = 64
NA_HEADS = 6
NA_WIN_H = 8
NA_WIN_W = 16
NA_QBLK_W = 16
NA_KBLK_W = NA_QBLK_W + NA_WIN_W
SW_HEADS = 6
SW_KV_HEADS = 2
SW_WINDOW = 128
SW_BLOCK = 128
CONV_DIM = 256
CONV_WIDTH = 31
D_FF = 2816
ROPE_BASE = 10000.0
EPS = 1e-6
NEG_INF = -1e30
N_BRANCH = 3
N_MOD = 9

A_DIM = NA_HEADS * HEAD_DIM
B_Q_DIM = SW_HEADS * HEAD_DIM
B_KV_DIM = SW_KV_HEADS * HEAD_DIM
IN_DIM = 3 * A_DIM + B_Q_DIM + 2 * B_KV_DIM + 2 * CONV_DIM + N_BRANCH * D_MODEL
SPLIT_POINTS = (A_DIM, 2 * A_DIM, 3 * A_DIM, 3 * A_DIM + B_Q_DIM,
                3 * A_DIM + B_Q_DIM + B_KV_DIM, 3 * A_DIM + B_Q_DIM + 2 * B_KV_DIM,
                3 * A_DIM + B_Q_DIM + 2 * B_KV_DIM + 2 * CONV_DIM)

kernel_name = 'hybrid_natten_swa_conformer_prefix_dit_block'


def rms_norm(x, g):
    xf = x.astype(jnp.float32)
    y = xf * lax.rsqrt(jnp.mean(xf * xf, axis=-1, keepdims=True) + EPS)
    return (y * g.astype(jnp.float32)).astype(x.dtype)


def layer_norm(x, g, b):
    xf = x.astype(jnp.float32)
    mu = jnp.mean(xf, axis=-1, keepdims=True)
    var = jnp.mean(jnp.square(xf - mu), axis=-1, keepdims=True)
    y = (xf - mu) * lax.rsqrt(var + EPS) * g.astype(jnp.float32) + b.astype(jnp.float32)
    return y.astype(x.dtype)


def modulate(h, shift, scale):
    return h * (1 + scale) + shift


def swiglu(h, w_gu, w_down):
    a, b = jnp.split(h @ w_gu, 2, axis=-1)
    return (jax.nn.silu(a) * b) @ w_down


def axial_rope_tables(n_tokens):
    t = jnp.arange(n_tokens, dtype=jnp.int32)
    row = (t // GRID_W).astype(jnp.float32)
    col = (t % GRID_W).astype(jnp.float32)
    n_freq = HEAD_DIM // 4
    inv_freq = ROPE_BASE ** (-jnp.arange(n_freq, dtype=jnp.float32) / n_freq)
    ang = jnp.concatenate([row[:, None] * inv_freq, col[:, None] * inv_freq], axis=-1)
    return jnp.cos(ang), jnp.sin(ang)


def apply_axial_rope(x, cos, sin):
    half = HEAD_DIM // 2
    xf = x.astype(jnp.float32)
    x1, x2 = xf[..., :half], xf[..., half:]
    cs, sn = cos[None, :, None, :], sin[None, :, None, :]
    return jnp.concatenate([x1 * cs - x2 * sn, x1 * sn + x2 * cs], axis=-1).astype(x.dtype)


def split_combined(z):
    bsz, n, _ = z.shape
    qa, ka, va, qb, kb, vb, u, gates = jnp.split(z, SPLIT_POINTS, axis=-1)
    heads = lambda t, h: t.reshape(bsz, n, h, HEAD_DIM)
    return (heads(qa, NA_HEADS), heads(ka, NA_HEADS), heads(va, NA_HEADS),
            heads(qb, SW_HEADS), heads(kb, SW_KV_HEADS), heads(vb, SW_KV_HEADS), u, gates)


def neighborhood_attention(q, k, v, k_ctx, v_ctx, rpb, rows):
    bsz, n, h, d = q.shape
    n_ctx = k_ctx.shape[1]
    kh = min(NA_WIN_H, rows)
    ncb = GRID_W // NA_QBLK_W
    scale = d ** -0.5
    r = jnp.arange(rows)
    key_rows = jnp.clip(r - kh // 2, 0, rows - kh)[:, None] + jnp.arange(kh)[None]
    j = jnp.arange(ncb)
    blk_c0 = jnp.clip(j * NA_QBLK_W - NA_WIN_W // 2, 0, GRID_W - NA_KBLK_W)
    key_cols = blk_c0[:, None] + jnp.arange(NA_KBLK_W)[None]
    qg = q.reshape(bsz, rows, ncb, NA_QBLK_W, h, d)
    kg = k.reshape(bsz, rows, GRID_W, h, d)
    vg = v.reshape(bsz, rows, GRID_W, h, d)
    idx_r = key_rows[:, None, :, None]
    idx_c = key_cols[None, :, None, :]
    kb = kg[:, idx_r, idx_c]
    vb = vg[:, idx_r, idx_c]
    s_loc = jnp.einsum('brjqhd,brjyxhd->bhrjqyx', qg, kb, preferred_element_type=jnp.float32) * scale
    qcol = j[:, None] * NA_QBLK_W + jnp.arange(NA_QBLK_W)[None]
    win_c0 = jnp.clip(qcol - NA_WIN_W // 2, 0, GRID_W - NA_WIN_W)[..., None]
    kc = key_cols[:, None, :]
    in_win = (kc >= win_c0) & (kc < win_c0 + NA_WIN_W)
    dyi = key_rows - r[:, None] + (NA_WIN_H - 1)
    dxi = jnp.clip(kc - qcol[..., None] + (NA_WIN_W - 1), 0, 2 * NA_WIN_W - 2)
    bias = rpb[:, dyi[:, None, None, :, None], dxi[None, :, :, None, :]]
    s_loc = jnp.where(in_win[None, None, None, :, :, None, :], s_loc + bias.astype(jnp.float32)[None], NEG_INF)
    s_ctx = jnp.einsum('bshd,bchd->bhsc', q, k_ctx, preferred_element_type=jnp.float32) * scale
    s_ctx = s_ctx.reshape(bsz, h, rows, ncb, NA_QBLK_W, n_ctx)
    n_loc = kh * NA_KBLK_W
    p = jax.nn.softmax(jnp.concatenate([s_loc.reshape(bsz, h, rows, ncb, NA_QBLK_W, n_loc), s_ctx], axis=-1), axis=-1)
    p = p.astype(v.dtype)
    p_loc = p[..., :n_loc].reshape(bsz, h, rows, ncb, NA_QBLK_W, kh, NA_KBLK_W)
    o = (jnp.einsum('bhrjqyx,brjyxhd->brjqhd', p_loc, vb)
         + jnp.einsum('bhrjqc,bchd->brjqhd', p[..., n_loc:], v_ctx))
    return o.reshape(bsz, n, h * d)


def window_attention(q, k, v, k_ctx, v_ctx, sink):
    bsz, n, h, d = q.shape
    hkv = k.shape[2]
    g = h // hkv
    nb = n // SW_BLOCK
    scale = d ** -0.5
    pad = ((0, 0), (SW_BLOCK, SW_BLOCK), (0, 0), (0, 0))
    kp, vp = jnp.pad(k, pad), jnp.pad(v, pad)
    blk_idx = jnp.arange(nb)[:, None] * SW_BLOCK + jnp.arange(3 * SW_BLOCK)[None]
    kb, vb = kp[:, blk_idx], vp[:, blk_idx]
    qb = q.reshape(bsz, nb, SW_BLOCK, hkv, g, d)
    s_loc = jnp.einsum('bnqkgd,bnckd->bkgnqc', qb, kb, preferred_element_type=jnp.float32) * scale
    qpos = jnp.arange(n).reshape(nb, SW_BLOCK)
    kpos = blk_idx - SW_BLOCK
    valid = (jnp.abs(qpos[:, :, None] - kpos[:, None, :]) <= SW_WINDOW) & (kpos >= 0)[:, None, :] & (kpos < n)[:, None, :]
    s_loc = jnp.where(valid[None, None, None], s_loc, NEG_INF)
    s_ctx = jnp.einsum('bnqkgd,bmkd->bkgnqm', qb, k_ctx, preferred_element_type=jnp.float32) * scale
    s_sink = jnp.broadcast_to(sink.astype(jnp.float32).reshape(1, hkv, g, 1, 1, 1), s_loc.shape[:-1] + (1,))
    p = jax.nn.softmax(jnp.concatenate([s_loc, s_ctx, s_sink], axis=-1), axis=-1).astype(v.dtype)
    n_loc = 3 * SW_BLOCK
    n_ctx = k_ctx.shape[1]
    o = (jnp.einsum('bkgnqc,bnckd->bnqkgd', p[..., :n_loc], vb)
         + jnp.einsum('bkgnqm,bmkd->bnqkgd', p[..., n_loc:n_loc + n_ctx], v_ctx))
    return o.reshape(bsz, n, h * d)


def context_attention(q, k, v, sink):
    bsz, n_ctx, h, d = q.shape
    hkv = k.shape[2]
    g = h // hkv
    qg = q.reshape(bsz, n_ctx, hkv, g, d)
    s = jnp.einsum('bqkgd,bckd->bkgqc', qg, k, preferred_element_type=jnp.float32) * d ** -0.5
    if sink is None:
        p = jax.nn.softmax(s, axis=-1)
    else:
        s_sink = jnp.broadcast_to(sink.astype(jnp.float32).reshape(1, hkv, g, 1, 1), s.shape[:-1] + (1,))
        p = jax.nn.softmax(jnp.concatenate([s, s_sink], axis=-1), axis=-1)[..., :n_ctx]
    o = jnp.einsum('bkgqc,bckd->bqkgd', p.astype(v.dtype), v)
    return o.reshape(bsz, n_ctx, h * d)


def conformer_conv(u, dw_w, dw_b, ln_g, ln_b):
    a, gt = jnp.split(u, 2, axis=-1)
    h = a * jax.nn.sigmoid(gt)
    h = lax.conv_general_dilated(h, dw_w.astype(h.dtype)[:, None, :], window_strides=(1,),
                                 padding=((CONV_WIDTH // 2, CONV_WIDTH // 2),),
                                 dimension_numbers=('NWC', 'WIO', 'NWC'),
                                 feature_group_count=CONV_DIM) + dw_b
    return jax.nn.silu(layer_norm(h, ln_g, ln_b))


def merge_branches(ya, yb, yc, gate_logits, b_gate, w_oa, w_ob, w_oc, w_o):
    ga, gb, gc = jnp.split(jax.nn.sigmoid(gate_logits + b_gate), N_BRANCH, axis=-1)
    return (ga * (ya @ w_oa) + gb * (yb @ w_ob) + gc * (yc @ w_oc)) @ w_o


def setup_inputs(seed: int = 0) -> dict:
    key = jax.random.key(seed)
    ks = jax.random.split(key, 32)
    f32 = jnp.float32
    D = D_MODEL
    nrm = lambda k, shape, fan_in: jax.random.normal(k, shape, f32) * fan_in ** -0.5
    small = lambda k, shape, s: jax.random.normal(k, shape, f32) * s
    return {
        'x': jax.random.normal(ks[0], (BATCH, SEQ, D), f32),
        'c': jax.random.normal(ks[1], (BATCH, D), f32),
        'ctx': jax.random.normal(ks[2], (BATCH, CTX_LEN, D), f32),
        'c_ctx': jax.random.normal(ks[3], (D,), f32),
        'w_ada': nrm(ks[4], (DEPTH, D, N_MOD * D), D),
        'b_ada': small(ks[5], (DEPTH, N_MOD * D), 0.02),
        'norm_g': 1.0 + small(ks[6], (DEPTH, 3, D), 0.05),
        'w_ffn1_gu': nrm(ks[7], (DEPTH, D, 2 * D_FF), D),
        'w_ffn1_down': nrm(ks[8], (DEPTH, D_FF, D), D_FF),
        'w_ffn2_gu': nrm(ks[9], (DEPTH, D, 2 * D_FF), D),
        'w_ffn2_down': nrm(ks[10], (DEPTH, D_FF, D), D_FF),
        'w_in': nrm(ks[11], (DEPTH, D, IN_DIM), D),
        'b_gate': small(ks[12], (DEPTH, N_BRANCH * D), 0.02),
        'na_rpb': small(ks[13], (DEPTH, NA_HEADS, 2 * NA_WIN_H - 1, 2 * NA_WIN_W - 1), 0.1),
        'sw_sink': small(ks[14], (DEPTH, SW_HEADS), 0.5),
        'conv_dw_w': nrm(ks[15], (DEPTH, CONV_WIDTH, CONV_DIM), CONV_WIDTH),
        'conv_dw_b': small(ks[16], (DEPTH, CONV_DIM), 0.02),
        'conv_ln_g': 1.0 + small(ks[17], (DEPTH, CONV_DIM), 0.05),
        'conv_ln_b': small(ks[18], (DEPTH, CONV_DIM), 0.02),
        'w_out_a': nrm(ks[19], (DEPTH, A_DIM, D), A_DIM),
        'w_out_b': nrm(ks[20], (DEPTH, B_Q_DIM, D), B_Q_DIM),
        'w_out_c': nrm(ks[21], (DEPTH, CONV_DIM, D), CONV_DIM),
        'w_out': nrm(ks[22], (DEPTH, D, D), D),
        'final_g': 1.0 + small(ks[23], (D,), 0.05),
    }


def reference(x, c, ctx, c_ctx, w_ada, b_ada, norm_g, w_ffn1_gu, w_ffn1_down, w_ffn2_gu, w_ffn2_down,
              w_in, b_gate, na_rpb, sw_sink, conv_dw_w, conv_dw_b, conv_ln_g, conv_ln_b,
              w_out_a, w_out_b, w_out_c, w_out, final_g):
    n_tok = x.shape[1]
    rows = n_tok // GRID_W
    cos, sin = axial_rope_tables(n_tok)
    h_ctx = ctx
    silu_c = jax.nn.silu(c)
    silu_cc = jax.nn.silu(c_ctx)
    for l in range(DEPTH):
        last = l == DEPTH - 1
        mx = jnp.split((silu_c @ w_ada[l] + b_ada[l])[:, None, :], N_MOD, axis=-1)
        mc = jnp.split((silu_cc @ w_ada[l] + b_ada[l])[None, None, :], N_MOD, axis=-1)
        x = x + 0.5 * mx[2] * swiglu(modulate(rms_norm(x, norm_g[l, 0]), mx[0], mx[1]), w_ffn1_gu[l], w_ffn1_down[l])
        h_ctx = h_ctx + 0.5 * mc[2] * swiglu(modulate(rms_norm(h_ctx, norm_g[l, 0]), mc[0], mc[1]), w_ffn1_gu[l], w_ffn1_down[l])
        zx = modulate(rms_norm(x, norm_g[l, 1]), mx[3], mx[4]) @ w_in[l]
        zc = modulate(rms_norm(h_ctx, norm_g[l, 1]), mc[3], mc[4]) @ w_in[l]
        qa, ka, va, qb, kb, vb, ux, gx = split_combined(zx)
        qa_c, ka_c, va_c, qb_c, kb_c, vb_c, uc, gc = split_combined(zc)
        qb = apply_axial_rope(qb, cos, sin)
        kb = apply_axial_rope(kb, cos, sin)
        ya = neighborhood_attention(qa, ka, va, ka_c, va_c, na_rpb[l], rows)
        yb = window_attention(qb, kb, vb, kb_c, vb_c, sw_sink[l])
        yc = conformer_conv(ux, conv_dw_w[l], conv_dw_b[l], conv_ln_g[l], conv_ln_b[l])
        x = x + mx[5] * merge_branches(ya, yb, yc, gx, b_gate[l], w_out_a[l], w_out_b[l], w_out_c[l], w_out[l])
        if not last:
            ya_c = context_attention(qa_c, ka_c, va_c, None)
            yb_c = context_attention(qb_c, kb_c, vb_c, sw_sink[l])
            yc_c = conformer_conv(uc, conv_dw_w[l], conv_dw_b[l], conv_ln_g[l], conv_ln_b[l])
            h_ctx = h_ctx + mc[5] * merge_branches(ya_c, yb_c, yc_c, gc, b_gate[l], w_out_a[l], w_out_b[l], w_out_c[l], w_out[l])
        x = x + 0.5 * mx[8] * swiglu(modulate(rms_norm(x, norm_g[l, 2]), mx[6], mx[7]), w_ffn2_gu[l], w_ffn2_down[l])
        if not last:
            h_ctx = h_ctx + 0.5 * mc[8] * swiglu(modulate(rms_norm(h_ctx, norm_g[l, 2]), mc[6], mc[7]), w_ffn2_gu[l], w_ffn2_down[l])
    return rms_norm(x, final_g)
```

```python
import contextlib
import numpy as np
import concourse.bass as bass
import concourse.mybir as mybir
from concourse.bass_utils import run_bass_kernel_spmd

F32 = mybir.dt.float32
BF16 = mybir.dt.bfloat16
AF = mybir.ActivationFunctionType
ALU = mybir.AluOpType

D = 1024
NCH = 8
SEQ = 4096
CTX = 256
NT = SEQ + CTX
DFF = 2816
NFF = 22
DEPTH = 2
EPS = 1e-6
NEG = -30000.0
GRID_W = 64
N_FM = 42
BLOCKS = [(SEQ, CTX, 1)] + [(i * 1024, 1024, 0) for i in range(4)]


class Sched:
    def __init__(self, nc, same_engine_sync=True, ndma=32, nconv=6):
        self.nc = nc
        self.engs = ["pe", "act", "dve", "pool", "sp"]
        self.q = {e: [] for e in self.engs}
        self.sem = {e: nc.alloc_semaphore(f"sem_{e}") for e in ["pe", "act", "dve", "pool"]}
        self.cnt = {e: 0 for e in self.sem}
        self.pending = {e: False for e in self.sem}
        self.seen = {e: {} for e in self.engs}
        self.lastw = {}
        self.readers = {}
        self.same = same_engine_sync
        self.ndma = ndma
        self.dsem = [nc.alloc_semaphore(f"dsem{i}") for i in range(ndma + nconv)]
        self.dval = [0] * (ndma + nconv)
        self.dnext = 0
        self.pnext = 0
        self.cnext = 0
        self.nconv = nconv

    def _semof(self, k):
        return self.sem[k[1]] if k[0] == "e" else self.dsem[k[1]]

    def _wait(self, e, tickets):
        need = {}
        for (k, v) in tickets:
            if k[0] == "e" and k[1] == e and (e == "pe" or not self.same):
                continue
            if v > need.get(k, 0):
                need[k] = v
        for k, v in need.items():
            if self.seen[e].get(k, 0) >= v:
                continue
            self.seen[e][k] = v
            sem = self._semof(k)
            self.q[e].append(lambda eng, sem=sem, v=v: eng.wait_ge(sem, v))

    def _deps(self, reads, writes):
        t = []
        for k in reads:
            if k in self.lastw:
                t.append(self.lastw[k])
        for k in writes:
            if k in self.lastw:
                t.append(self.lastw[k])
            t.extend(self.readers.get(k, {}).items())
        return t

    def _commit(self, tk, reads, writes):
        for k in reads:
            r = self.readers.setdefault(k, {})
            if tk[1] > r.get(tk[0], 0):
                r[tk[0]] = tk[1]
        for k in writes:
            self.lastw[k] = tk
            self.readers[k] = {}

    def op(self, e, fn, reads=(), writes=(), signal=True):
        self._wait(e, self._deps(reads, writes))
        if signal:
            self.cnt[e] += 1
            tk = (("e", e), self.cnt[e])
            sem = self.sem[e]
            self.q[e].append(lambda eng, fn=fn, sem=sem: fn(eng).then_inc(sem, 1))
            self.pending[e] = False
        else:
            tk = (("e", e), self.cnt[e] + 1)
            self.pending[e] = True
            self.q[e].append(lambda eng, fn=fn: fn(eng))
        self._commit(tk, reads, writes)

    def dma(self, e, out, in_, reads=(), writes=(), conv=False, **kw):
        if conv:
            i = self.ndma + self.cnext
            self.cnext = (self.cnext + 1) % self.nconv
        elif e == "pool":
            half = self.ndma // 2
            i = half + self.pnext
            self.pnext = (self.pnext + 1) % (self.ndma - half)
        else:
            i = self.dnext
            self.dnext = (self.dnext + 1) % (self.ndma // 2)
        deps = self._deps(reads, writes)
        if self.dval[i] > 0:
            deps.append((("d", i), self.dval[i]))
        self._wait(e, deps)
        self.dval[i] += 16
        tk = (("d", i), self.dval[i])
        sem = self.dsem[i]
        self.q[e].append(lambda eng, out=out, in_=in_, sem=sem, kw=kw:
                         eng.dma_start(out=out, in_=in_, **kw).then_inc(sem, 16))
        self._commit(tk, reads, writes)
        return tk

    def barrier(self, engines=None):
        for e in self.sem:
            assert not self.pending[e], f"unsignalled op pending on {e} at barrier"
        tks = [(("e", e), self.cnt[e]) for e in self.sem if self.cnt[e] > 0]
        tks += [(("d", i), self.dval[i]) for i in range(self.ndma) if self.dval[i] > 0]
        for e in (engines or self.engs):
            self.same, old = True, self.same
            self._wait(e, [t for t in tks if not (t[0][0] == "e" and t[0][1] == e)])
            self.same = old

    def mm(self, out, lhsT, rhs, start, stop, reads, writes, signal):
        self.op("pe", lambda eng: eng.matmul(out, lhsT=lhsT, rhs=rhs, start=start, stop=stop),
                reads, writes, signal)

    def act(self, out, in_, func, reads, writes, bias=None, scale=None):
        kw = {}
        if bias is not None:
            kw["bias"] = bias
        if scale is not None:
            kw["scale"] = scale
        self.op("act", lambda eng: eng.activation(out=out, in_=in_, func=func, **kw), reads, writes)

    def tt(self, e, out, in0, in1, op, reads, writes):
        self.op(e, lambda eng: eng.tensor_tensor(out=out, in0=in0, in1=in1, op=op), reads, writes)

    def ts(self, e, out, in0, s1, op0, reads, writes, s2=None, op1=None):
        if op1 is None:
            self.op(e, lambda eng: eng.tensor_scalar(out=out, in0=in0, scalar1=s1, scalar2=None, op0=op0),
                    reads, writes)
        else:
            self.op(e, lambda eng: eng.tensor_scalar(out=out, in0=in0, scalar1=s1, scalar2=s2, op0=op0, op1=op1),
                    reads, writes)

    def stt(self, out, in0, scalar, in1, op0, op1, reads, writes):
        self.op("dve", lambda eng: eng.scalar_tensor_tensor(out=out, in0=in0, scalar=scalar, in1=in1,
                                                            op0=op0, op1=op1), reads, writes)

    def copy(self, e, out, in_, reads, writes):
        self.op(e, lambda eng: eng.tensor_copy(out=out, in_=in_), reads, writes)

    def recip(self, out, in_, reads, writes):
        self.op("dve", lambda eng: eng.reciprocal(out=out, in_=in_), reads, writes)

    def memset(self, e, ap, val, writes):
        self.op(e, lambda eng: eng.memset(ap, val), (), writes)

    def emit(self):
        nc = self.nc
        q = self.q
        with nc.Block() as block:
            @block.sync
            def _(eng):
                for f in q["sp"]:
                    f(eng)

            @block.tensor
            def _(eng):
                for f in q["pe"]:
                    f(eng)

            @block.scalar
            def _(eng):
                for f in q["act"]:
                    f(eng)

            @block.vector
            def _(eng):
                for f in q["dve"]:
                    f(eng)

            @block.gpsimd
            def _(eng):
                for f in q["pool"]:
                    f(eng)


def _kr0(r):
    return min(max(r - 4, 0), 56)


def na_chunks(n):
    lo = min(_kr0(2 * n), _kr0(2 * n + 1))
    hi = max(_kr0(2 * n), _kr0(2 * n + 1)) + 7
    return list(range(lo // 2, hi // 2 + 1))


def _na_tile_index(n, m):
    key = np.arange(128)
    yl, kc = key // 64, key % 64
    qq = np.arange(128)
    rl, qc = qq // 64, qq % 64
    y = 2 * m + yl[:, None]
    r = 2 * n + rl[None, :]
    kr = np.clip(r - 4, 0, 56)
    vrow = (y >= kr) & (y <= kr + 7)
    wc0 = np.clip(qc - 8, 0, 48)[None, :]
    vcol = (kc[:, None] >= wc0) & (kc[:, None] < wc0 + 16)
    dy = np.clip(y - r + 7, 0, 14)
    dx = np.clip(kc[:, None] - qc[None, :] + 15, 0, 30)
    valid = vrow & vcol
    return dy, dx, valid


def na_tile_table():
    table, uniq, sig = {}, [], {}
    for n in range(32):
        for m in na_chunks(n):
            dy, dx, valid = _na_tile_index(n, m)
            s = (np.where(valid, dy * 31 + dx, -1)).astype(np.int16).tobytes()
            if s not in sig:
                sig[s] = len(uniq)
                uniq.append((dy, dx, valid))
            table[(n, m)] = sig[s]
    return table, uniq


NA_TABLE, NA_UNIQ = na_tile_table()
N_TILES = len(NA_UNIQ)
NA_STRIP = 2560


def na_block_plan(tb):
    ns = [4 * tb + i for i in range(4)]
    mlo = min(min(na_chunks(n)) for n in ns)
    mhi = max(max(na_chunks(n)) for n in ns)
    plan, off = [], 0
    for m in range(mlo, mhi + 1):
        nm = [n for n in ns if m in na_chunks(n)]
        assert nm == list(range(nm[0], nm[-1] + 1))
        plan.append((m, nm[0], nm[-1], off))
        off += 128 * len(nm)
    assert off <= NA_STRIP
    return plan


def na_cfg(tb):
    return 0 if tb == 0 else (2 if tb == 7 else 1)


def _swap_cols(w, nheads):
    k = w.shape[0]
    w4 = w.reshape(k, nheads, 2, 32)
    return w4[:, :, ::-1, :].reshape(k, nheads * 64)


def _chunks_lhsT(w):
    k, c = w.shape
    return np.ascontiguousarray(w.reshape(k // 128, 128, c // 128, 128).transpose(2, 1, 0, 3))


def _rows_lhsT(w):
    k, m = w.shape
    return np.ascontiguousarray(w.reshape(k // 128, 128, m).transpose(1, 0, 2))


def _vec_cols(v):
    return np.ascontiguousarray(v.reshape(-1, 128).T)


def rope_tables():
    t = np.arange(SEQ)
    row = (t // GRID_W).astype(np.float32)
    col = (t % GRID_W).astype(np.float32)
    n_freq = 16
    inv_freq = (np.float32(10000.0) ** (-np.arange(n_freq, dtype=np.float32) / np.float32(n_freq))).astype(np.float32)
    ang = np.concatenate([row[:, None] * inv_freq, col[:, None] * inv_freq], axis=-1).astype(np.float32)
    cos, sin = np.cos(ang).astype(np.float32), np.sin(ang).astype(np.float32)
    cosT = np.concatenate([cos.T, cos.T, cos.T, cos.T], axis=0)
    sinT = np.concatenate([-sin.T, sin.T, -sin.T, sin.T], axis=0)
    return np.ascontiguousarray(cosT), np.ascontiguousarray(sinT)


def prep_shared(inp):
    f = lambda a: np.ascontiguousarray(np.asarray(a, dtype=np.float32))
    sh = {}
    wa = f(inp["w_ada"])
    sh["wada"] = np.ascontiguousarray(wa.reshape(DEPTH, 8, 128, 36, 256).transpose(0, 3, 2, 1, 4)).reshape(DEPTH, 36, 128, 2048)
    sh["bada"] = np.stack([_vec_cols(f(inp["b_ada"])[l]) for l in range(DEPTH)])
    sh["normg"] = np.stack([np.stack([_vec_cols(f(inp["norm_g"])[l, s]) for s in range(3)]) for l in range(DEPTH)])
    sh["finalg"] = _vec_cols(f(inp["final_g"]))
    for nm, src in (("wgu1", "w_ffn1_gu"), ("wgu2", "w_ffn2_gu")):
        w = f(inp[src])
        outl = []
        for l in range(DEPTH):
            a = _chunks_lhsT(w[l][:, :DFF])
            b = _chunks_lhsT(w[l][:, DFF:])
            outl.append(np.stack([a, b], axis=2).reshape(NFF, 128, 2 * 8 * 128))
        sh[nm] = np.ascontiguousarray(np.stack(outl))
    for nm, src in (("wd1", "w_ffn1_down"), ("wd2", "w_ffn2_down")):
        w = f(inp[src])
        outl = []
        for l in range(DEPTH):
            r = _rows_lhsT(w[l])
            outl.append(np.stack([r[:, :, dc * 128:(dc + 1) * 128].reshape(128, NFF * 128) for dc in range(8)]))
        sh[nm] = np.ascontiguousarray(np.stack(outl))
    w_in = f(inp["w_in"])
    fm, vv = [], []
    for l in range(DEPTH):
        w = w_in[l]
        qa, ka, va = w[:, 0:384], w[:, 384:768], w[:, 768:1152]
        qb, kb, vb = w[:, 1152:1536], w[:, 1536:1664], w[:, 1664:1792]
        u, g = w[:, 1792:2304], w[:, 2304:5376]
        qbs, kbs = _swap_cols(qb, 6), _swap_cols(kb, 2)
        cols = [qa, ka]
        for c in range(3):
            cols += [qb[:, c * 128:(c + 1) * 128], qbs[:, c * 128:(c + 1) * 128]]
        cols += [kb, kbs]
        for c in range(2):
            cols += [u[:, c * 128:(c + 1) * 128], u[:, 256 + c * 128:256 + (c + 1) * 128]]
        cols += [g]
        allc = np.concatenate(cols, axis=1)
        assert allc.shape[1] == N_FM * 128
        fm.append(_chunks_lhsT(allc).reshape(N_FM, 128, 1024))
        vv.append(_rows_lhsT(np.concatenate([va, vb], axis=1)).reshape(128, 8 * 512))
    sh["winfm"] = np.ascontiguousarray(np.stack(fm))
    sh["winv"] = np.ascontiguousarray(np.stack(vv))
    sh["bgate"] = np.stack([_vec_cols(f(inp["b_gate"])[l]) for l in range(DEPTH)])
    rpb = f(inp["na_rpb"])
    strips = np.zeros((DEPTH, 3, 128, 6, NA_STRIP), np.float32)
    for l in range(DEPTH):
        for cfg, tb in ((0, 0), (1, 1), (2, 7)):
            for (m, n0_, n1_, off) in na_block_plan(tb):
                for n in range(n0_, n1_ + 1):
                    dy, dx, valid = NA_UNIQ[NA_TABLE[(n, m)]]
                    g = rpb[l][:, dy, dx]
                    g = np.where(valid[None], g, np.float32(NEG))
                    o = off + (n - n0_) * 128
                    strips[l, cfg, :, :, o:o + 128] = g.transpose(1, 0, 2)
    sh["natile"] = strips.reshape(DEPTH, 3, 128, 6 * NA_STRIP)
    j = np.arange(128)[:, None]
    i = np.arange(128)[None, :]
    swm = np.concatenate([np.where(j <= i, 0.0, NEG), np.zeros((128, 128)), np.where(i <= j, 0.0, NEG)], axis=1).astype(np.float32)
    sh["swmask"] = np.ascontiguousarray(swm)
    sink = f(inp["sw_sink"])
    sk = np.empty((DEPTH, 128, 3), np.float32)
    for l in range(DEPTH):
        for c in range(3):
            sk[l, :64, c] = sink[l, 2 * c]
            sk[l, 64:, c] = sink[l, 2 * c + 1]
    sh["sinkp"] = sk
    cw = f(inp["conv_dw_w"])
    sh["convw"] = np.ascontiguousarray(np.stack([cw[l].T.reshape(2, 128, 31).transpose(1, 0, 2) for l in range(DEPTH)]))
    sh["convb"] = np.stack([_vec_cols(f(inp["conv_dw_b"])[l]) for l in range(DEPTH)])
    sh["lng"] = np.stack([_vec_cols(f(inp["conv_ln_g"])[l]) for l in range(DEPTH)])
    sh["lnb"] = np.stack([_vec_cols(f(inp["conv_ln_b"])[l]) for l in range(DEPTH)])
    sh["woa"] = np.stack([_rows_lhsT(f(inp["w_out_a"])[l]).reshape(128, 3 * 1024) for l in range(DEPTH)])
    sh["wob"] = np.stack([_rows_lhsT(f(inp["w_out_b"])[l]).reshape(128, 3 * 1024) for l in range(DEPTH)])
    sh["woc"] = np.stack([_rows_lhsT(f(inp["w_out_c"])[l]).reshape(128, 2 * 1024) for l in range(DEPTH)])
    sh["wo"] = np.stack([_rows_lhsT(f(inp["w_out"])[l]).reshape(128, 8 * 1024) for l in range(DEPTH)])
    cosT, sinT = rope_tables()
    sh["costab"], sh["sintab"] = cosT, sinT
    sh["ident"] = np.eye(128, dtype=np.float32)
    return {k: np.ascontiguousarray(v, dtype=np.float32) for k, v in sh.items()}


def prep_core(inp, b):
    x = np.asarray(inp["x"][b], dtype=np.float32)
    ctx = np.asarray(inp["ctx"][b], dtype=np.float32)
    c = np.asarray(inp["c"][b], dtype=np.float32)
    cc = np.asarray(inp["c_ctx"], dtype=np.float32)
    return {
        "xT": np.ascontiguousarray(x.T),
        "ctxT": np.ascontiguousarray(ctx.T),
        "cc": np.ascontiguousarray(np.stack([_vec_cols(c), _vec_cols(cc)], axis=2)),
    }


def build(stop_after=None, dumps=(), same_engine_sync=True):
    nc = bass.Bass("TRN2", target_bir_lowering=False)
    S = Sched(nc, same_engine_sync=same_engine_sync)

    def din(name, shape):
        return nc.dram_tensor(name, list(shape), F32, kind="ExternalInput").ap()

    def dscr(name, shape, dt):
        kind = "ExternalOutput" if name in dumps else "Internal"
        return nc.dram_tensor(name, list(shape), dt, kind=kind).ap()

    xT = din("xT", [D, SEQ]); ctxT = din("ctxT", [D, CTX]); cc = din("cc", [128, 8, 2])
    wada = din("wada", [DEPTH, 36, 128, 2048]); bada = din("bada", [DEPTH, 128, 72])
    normg = din("normg", [DEPTH, 3, 128, 8]); finalg = din("finalg", [128, 8])
    wgu_in = [din("wgu1", [DEPTH, NFF, 128, 2048]), din("wgu2", [DEPTH, NFF, 128, 2048])]
    wd_in = [din("wd1", [DEPTH, 8, 128, NFF * 128]), din("wd2", [DEPTH, 8, 128, NFF * 128])]
    winfm_in = din("winfm", [DEPTH, N_FM, 128, 1024]); winv_in = din("winv", [DEPTH, 128, 4096])
    bgate = din("bgate", [DEPTH, 128, 24])
    natile = din("natile", [DEPTH, 3, 128, 6 * NA_STRIP]); swmask = din("swmask", [128, 384])
    sinkp = din("sinkp", [DEPTH, 128, 3])
    convw = din("convw", [DEPTH, 128, 2, 31]); convb = din("convb", [DEPTH, 128, 2])
    lng = din("lng", [DEPTH, 128, 2]); lnb = din("lnb", [DEPTH, 128, 2])
    woa_in = din("woa", [DEPTH, 128, 3072]); wob_in = din("wob", [DEPTH, 128, 3072])
    woc_in = din("woc", [DEPTH, 128, 2048]); wo_in = din("wo", [DEPTH, 128, 8192])
    costab = din("costab", [128, SEQ]); sintab = din("sintab", [128, SEQ]); ident_in = din("ident", [128, 128])
    outT = nc.dram_tensor("outT", [D, SEQ], F32, kind="ExternalOutput").ap()

    X = dscr("X", [D, NT], F32)
    WGU = [[dscr(f"WGU{l}{f}", [NFF, 128, 2048], BF16) for f in range(2)] for l in range(DEPTH)]
    WD = [[dscr(f"WD{l}{f}", [8, 128, NFF * 128], BF16) for f in range(2)] for l in range(DEPTH)]
    WINFM = [dscr(f"WINFM{l}", [N_FM, 128, 1024], BF16) for l in range(DEPTH)]
    WINV = [dscr(f"WINV{l}", [128, 4096], BF16) for l in range(DEPTH)]
    NATB = [dscr(f"NATB{l}", [3, 128, 6 * NA_STRIP], BF16) for l in range(DEPTH)]
    QA = dscr("QA", [384, NT], BF16); KA = dscr("KA", [384, NT], BF16); VA = dscr("VA", [NT, 384], BF16)
    QB = dscr("QB", [384, NT], BF16); KB = dscr("KB", [128, NT], BF16); VB = dscr("VB", [NT, 128], BF16)
    HC = dscr("HC", [256, NT], BF16); G = dscr("G", [3072, NT], BF16)

    Xv = X.rearrange("(c p) t -> p c t", p=128)
    outv = outT.rearrange("(c p) t -> p c t", p=128)

    stack0 = contextlib.ExitStack()

    name_ctr = {"n": 0}

    def sb(stack, name, shape, dt):
        name_ctr["n"] += 1
        return stack.enter_context(nc.sbuf_tensor(f"{name}_{name_ctr['n']}", list(shape), dt))

    ident_f = sb(stack0, "ident_f", [128, 128], F32)
    ident_b = sb(stack0, "ident_b", [128, 128], BF16)
    ones_b = sb(stack0, "ones_b", [128, 128], BF16)
    ones_f = sb(stack0, "ones_f", [128, 128], F32)
    mods = [sb(stack0, f"mods{l}", [128, 72, 2], F32) for l in range(DEPTH)]
    Aeff = [sb(stack0, f"Aeff{l}", [128, 3, 8, 2], F32) for l in range(DEPTH)]
    Gate = [sb(stack0, f"Gate{l}", [128, 3, 8, 2], F32) for l in range(DEPTH)]
    normg_s = sb(stack0, "normg_s", [128, DEPTH * 3, 8], F32)
    finalg_s = sb(stack0, "finalg_s", [128, 8], F32)
    bgate_s = sb(stack0, "bgate_s", [128, DEPTH, 24], F32)
    ps = [stack0.enter_context(nc.psum_tensor(f"ps{i}", [128, 512], F32)) for i in range(8)]
    PS = lambda i: ("ps", i)

    done = {"flag": False}

    def finish_phase(name):
        S.barrier()
        if stop_after == name:
            done["flag"] = True
        return done["flag"]

    conv_list = []

    def conv_dma(dst, src, key):
        conv_list.append((dst, src, key))

    def pump_conv(n):
        for _ in range(n):
            if not conv_list:
                return
            dst, src, key = conv_list.pop(0)
            S.dma("pool", dst, src, (), (key,), conv=True, max_dma_last_dim=4096)

    def pump_until(key):
        while conv_list and key not in S.lastw:
            pump_conv(1)
        assert key in S.lastw

    def issue_conversion(l):
        for j in range(NFF):
            conv_dma(WGU[l][0][j], wgu_in[0][l, j], ("WGU", l, 0, j))
        for dc in range(8):
            conv_dma(WD[l][0][dc], wd_in[0][l, dc], ("WD", l, 0, dc))
        for ci in range(N_FM):
            conv_dma(WINFM[l][ci], winfm_in[l, ci], ("WINFM", l, ci))
        conv_dma(WINV[l], winv_in[l], ("WINV", l))
        for cfg in range(3):
            for h in range(6):
                conv_dma(NATB[l][cfg, :, h * NA_STRIP:(h + 1) * NA_STRIP], natile[l, cfg, :, h * NA_STRIP:(h + 1) * NA_STRIP],
                         ("NATB", l, cfg, h))
        for j in range(NFF):
            conv_dma(WGU[l][1][j], wgu_in[1][l, j], ("WGU", l, 1, j))
        for dc in range(8):
            conv_dma(WD[l][1][dc], wd_in[1][l, dc], ("WD", l, 1, dc))

    eps_s = sb(stack0, "eps_s", [128, 1], F32)
    S.memset("dve", eps_s[:], EPS, ("eps_s",))

    ada_stack = contextlib.ExitStack()
    cc_s = sb(ada_stack, "cc_s", [128, 8, 2], F32)
    sc_b = sb(ada_stack, "sc_b", [128, 8, 2], BF16)
    bada_s = sb(ada_stack, "bada_s", [128, DEPTH, 72], F32)
    wab = [sb(ada_stack, f"wab{i}", [128, 8, 256], BF16) for i in range(2)]
    ada_state = {"k": 0}

    def ada_finalize(l, s_):
        pv = ps[7][:, l * 144:(l + 1) * 144].rearrange("p (j t) -> p j t", t=2)
        j0, j1 = 24 * s_, 24 * (s_ + 1)
        for col in range(2):
            S.tt("dve", mods[l][:, j0:j1, col], pv[:, j0:j1, col], bada_s[:, l, j0:j1], ALU.add,
                 (PS(7), "bada"), (("mods", l, s_),))
        for col in range(2):
            S.stt(Aeff[l][:, s_, :, col], mods[l][:, (3 * s_ + 1) * 8:(3 * s_ + 2) * 8, col], 1.0,
                  normg_s[:, l * 3 + s_, :], ALU.add, ALU.mult, (("mods", l, s_), "normg"), (("Aeff", l, s_),))
            S.ts("dve", Gate[l][:, s_, :, col], mods[l][:, (3 * s_ + 2) * 8:(3 * s_ + 3) * 8, col],
                 0.5 if s_ != 1 else 1.0, ALU.mult, (("mods", l, s_),), (("Gate", l, s_),))

    def ada_group(burst=None):
        k = ada_state["k"]
        if k >= 72:
            return False
        ada_state["k"] += 1
        l, g = k // 36, k % 36
        if burst is not None:
            wb, wbk = burst[k], ("wab_burst", k)
        else:
            wb, wbk = wab[k % 2], ("wab", k % 2)
        S.dma("pool", wb[:], wada[l, g].rearrange("p (kc m) -> p kc m", kc=8), (), (wbk,), max_dma_last_dim=4096)
        for oc in range(2):
            j = g * 2 + oc
            c0 = l * 144 + 2 * j
            for kc in range(8):
                S.mm(ps[7][:, c0:c0 + 2], wb[:, kc, oc * 128:(oc + 1) * 128], sc_b[:, kc, :],
                     kc == 0, kc == 7, (wbk, "sc_b"), (PS(7),), kc == 7)
        if g % 12 == 11:
            ada_finalize(l, g // 12)
        return True

    with contextlib.ExitStack() as st:
        S.dma("sp", ident_f[:], ident_in[:, :], (), ("ident_f",))
        S.copy("dve", ident_b[:], ident_f[:], ("ident_f",), ("ident_b",))
        S.memset("dve", ones_b[:], 1.0, ("ones_b",))
        S.memset("dve", ones_f[:], 1.0, ("ones_f",))
        S.dma("sp", normg_s[:], normg.rearrange("l s p c -> p (l s) c"), (), ("normg",))
        S.dma("sp", finalg_s[:], finalg[:, :], (), ("finalg",))
        S.dma("sp", bgate_s[:], bgate.rearrange("l p g -> p l g"), (), ("bgate",))
        for l in range(DEPTH):
            issue_conversion(l)
        S.dma("sp", cc_s[:], cc[:, :, :], (), ("cc_s",))
        S.dma("sp", bada_s[:], bada.rearrange("l p j -> p l j"), (), ("bada",))
        S.act(sc_b[:], cc_s[:], AF.Silu, ("cc_s",), ("sc_b",))
        burst = [sb(st, f"wabb{i}", [128, 8, 256], BF16) for i in range(12)]
        for _ in range(12):
            ada_group(burst)
        pump_conv(30)
        if finish_phase("ada"):
            pass

    def halves(T):
        return [(h0, min(512, T - h0)) for h0 in range(0, T, 512)]

    def norm_steps(l, s, col, xb, xbk, T, hT, hTk, sq, rs, tmp):
        xall = tuple((xbk[0], xbk[1], kc) for kc in range(8))
        for (h0, hl) in halves(T):
            S.act(sq[:, :, :hl], xb[:, :, h0:h0 + hl], AF.Square, xall, ("sq",))
            yield
            for kc in range(8):
                S.mm(ps[6][:, :hl], ones_b[:], sq[:, kc, :hl], kc == 0, kc == 7, ("ones_b", "sq"), (PS(6),), kc == 7)
            S.act(rs[:, h0:h0 + hl], ps[6][:, :hl], AF.Sqrt, (PS(6), "eps_s"), ("rs",), bias=eps_s[:, 0:1], scale=1.0 / D)
            yield
        S.recip(rs[:, :T], rs[:, :T], ("rs",), ("rs",))
        yield
        for kc in range(8):
            tb_ = tmp[kc % 2]
            S.tt("dve", tb_[:, :T], xb[:, kc, :T], rs[:, :T], ALU.mult, ((xbk[0], xbk[1], kc), "rs"), (("tmp", kc % 2),))
            S.act(hT[:, kc, :T], tb_[:, :T], AF.Identity, (("tmp", kc % 2), ("Aeff", l, s), ("mods", l, s)), (hTk,),
                  bias=mods[l][:, 3 * s * 8 + kc, col:col + 1], scale=Aeff[l][:, s, kc, col:col + 1])
            yield

    def run_all(gen):
        for _ in gen:
            pass


    def ffn_phase(l, f, blocks, fuse_final=False):
        s = 0 if f == 0 else 2
        pump_until(("WD", l, f, 7))
        with contextlib.ExitStack() as st:
            xbs = [sb(st, f"xb{i}", [128, 8, 1024], F32) for i in range(2)]
            hTs = [sb(st, f"hT{i}", [128, 8, 1024], BF16) for i in range(2)]
            sq = sb(st, "sq", [128, 8, 512], BF16)
            rs = sb(st, "rs", [128, 1024], F32)
            tmp = [sb(st, f"tmp{i}", [128, 1024], F32) for i in range(2)]
            gT = sb(st, "gT", [128, NFF, 1024], BF16)
            wgb = [sb(st, f"wgb{i}", [128, 2, 8, 128], BF16) for i in range(3)]
            wdb = [sb(st, f"wdb{i}", [128, NFF, 128], BF16) for i in range(2)]
            sa = [sb(st, f"sa{i}", [128, 512], F32) for i in range(2)]

            xTv = xT.rearrange("(c p) t -> p c t", p=128)
            ctxTv = ctxT.rearrange("(c p) t -> p c t", p=128)

            def load_x(bi, kcs=range(8)):
                t0, T, col = blocks[bi]
                for kc in kcs:
                    if (l, f) == (0, 0):
                        src = ctxTv[:, kc, :] if col == 1 else xTv[:, kc, t0:t0 + T]
                    else:
                        src = Xv[:, kc, t0:t0 + T]
                    S.dma("sp", xbs[bi % 2][:, kc, :T], src, (), (("xb", bi % 2, kc),))

            def norm_gen(bi):
                t0, T, col = blocks[bi]
                return norm_steps(l, s, col, xbs[bi % 2], ("xb", bi % 2), T, hTs[bi % 2], ("hT", bi % 2), sq, rs, tmp)

            load_x(0)
            run_all(norm_gen(0))
            u = 0
            itn = {"n": 0}
            pending_final = []
            for bi, (t0, T, col) in enumerate(blocks):
                xb, xbk = xbs[bi % 2], ("xb", bi % 2)
                hT, hTk = hTs[bi % 2], ("hT", bi % 2)
                nxt = None
                for j in range(NFF):
                    if 2 <= j < 10 and bi + 1 < len(blocks):
                        load_x(bi + 1, [j - 2])
                        if j == 9:
                            nxt = norm_gen(bi + 1)
                    wb, wbk = wgb[j % 3], ("wgb", j % 3)
                    S.dma("sp", wb[:], WGU[l][f][j].rearrange("p (a k m) -> p a k m", a=2, k=8),
                          (("WGU", l, f, j),), (wbk,))
                    for (h0, hl) in halves(T):
                        ia, ib = (u % 2) * 2, (u % 2) * 2 + 1
                        u += 1
                        for kc in range(8):
                            S.mm(ps[ia][:, :hl], wb[:, 0, kc, :], hT[:, kc, h0:h0 + hl], kc == 0, kc == 7, (wbk, hTk),
                                 (PS(ia),), False)
                        for kc in range(8):
                            S.mm(ps[ib][:, :hl], wb[:, 1, kc, :], hT[:, kc, h0:h0 + hl], kc == 0, kc == 7, (wbk, hTk),
                                 (PS(ib),), kc == 7)
                        S.act(sa[u % 2][:, :hl], ps[ia][:, :hl], AF.Silu, (PS(ia),), (("sa", u % 2),))
                        S.tt("dve", gT[:, j, h0:h0 + hl], sa[u % 2][:, :hl], ps[ib][:, :hl], ALU.mult,
                             (("sa", u % 2), PS(ib)), (("gT", j),))
                    if nxt is not None and j >= 10:
                        next(nxt, None)
                    if j == 0 and pending_final:
                        pending_final.pop(0)()
                    itn["n"] += 1
                    if (l, f) == (0, 0):
                        if itn["n"] > 2 and ada_state["k"] < 72:
                            ada_group()
                        elif ada_state["k"] >= 72:
                            pump_conv(1)
                    elif itn["n"] % 2 == 0:
                        pump_conv(1)
                if nxt is not None:
                    run_all(nxt)
                v = 0
                for dc in range(8):
                    wb, wbk = wdb[dc % 2], ("wdb", dc % 2)
                    S.dma("sp", wb[:], WD[l][f][dc].rearrange("p (j m) -> p j m", j=NFF),
                          (("WD", l, f, dc),), (wbk,))
                    for (h0, hl) in halves(T):
                        io = 4 + v % 2
                        v += 1
                        for j in range(NFF):
                            S.mm(ps[io][:, :hl], wb[:, j, :], gT[:, j, h0:h0 + hl], j == 0, j == NFF - 1, (wbk, ("gT", j)),
                                 (PS(io),), j == NFF - 1)
                        S.stt(xb[:, dc, h0:h0 + hl], ps[io][:, :hl], Gate[l][:, s, dc, col:col + 1], xb[:, dc, h0:h0 + hl],
                              ALU.mult, ALU.add, (PS(io), ("Gate", l, s), (xbk[0], xbk[1], dc)), ((xbk[0], xbk[1], dc),))
                    if not fuse_final:
                        S.dma("pool", Xv[:, dc, t0:t0 + T], xb[:, dc, :T], ((xbk[0], xbk[1], dc),), ())
                if fuse_final:
                    def final_norm(xb=xb, xbk=xbk, t0=t0, T=T):
                        xall = tuple((xbk[0], xbk[1], kc) for kc in range(8))
                        for (h0, hl) in halves(T):
                            S.act(sq[:, :, :hl], xb[:, :, h0:h0 + hl], AF.Square, xall, ("sq",))
                            for kc in range(8):
                                S.mm(ps[6][:, :hl], ones_b[:], sq[:, kc, :hl], kc == 0, kc == 7, ("ones_b", "sq"), (PS(6),), kc == 7)
                            S.act(rs[:, h0:h0 + hl], ps[6][:, :hl], AF.Sqrt, (PS(6), "eps_s"), ("rs",), bias=eps_s[:, 0:1], scale=1.0 / D)
                        S.recip(rs[:, :T], rs[:, :T], ("rs",), ("rs",))
                        for kc in range(8):
                            S.stt(xb[:, kc, :T], xb[:, kc, :T], finalg_s[:, kc:kc + 1], rs[:, :T], ALU.mult, ALU.mult,
                                  ((xbk[0], xbk[1], kc), "finalg", "rs"), ((xbk[0], xbk[1], kc),))
                            S.dma("pool", outv[:, kc, t0:t0 + T], xb[:, kc, :T], ((xbk[0], xbk[1], kc),), ())
                    if bi + 1 < len(blocks):
                        pending_final.append(final_norm)
                    else:
                        final_norm()
            if (l, f) == (0, 0):
                while ada_group():
                    pass
        return finish_phase(f"ffn{l}{f}")

    res = {}

    def load_out_weights(l, stack):
        res["woa"] = sb(stack, "woa", [128, 3, 1024], BF16); res["wob"] = sb(stack, "wob", [128, 3, 1024], BF16)
        res["woc"] = sb(stack, "woc", [128, 2, 1024], BF16); res["wo"] = sb(stack, "wo", [128, 8, 1024], BF16)

    def out_weight_loads(l):
        woa, wob, woc, wo = res["woa"], res["wob"], res["woc"], res["wo"]
        todo = []
        for k in range(3):
            todo.append((woa[:, k, :], woa_in[l][:, k * 1024:(k + 1) * 1024], "woa"))
            todo.append((wob[:, k, :], wob_in[l][:, k * 1024:(k + 1) * 1024], "wob"))
        for k in range(2):
            todo.append((woc[:, k, :], woc_in[l][:, k * 1024:(k + 1) * 1024], "woc"))
        for k in range(8):
            todo.append((wo[:, k, :], wo_in[l][:, k * 1024:(k + 1) * 1024], "wo"))
        return [lambda d=d, s_=s_, k_=k_: S.dma("pool", d, s_, (), (k_,), max_dma_last_dim=4096) for (d, s_, k_) in todo]

    def proj_phase(l):
        pump_until(("WINV", l))
        with contextlib.ExitStack() as st:
            xbs = [sb(st, f"xb{i}", [128, 8, 1024], F32) for i in range(2)]
            hTs = [sb(st, f"hT{i}", [128, 8, 1024], BF16) for i in range(2)]
            sq = sb(st, "sq", [128, 8, 512], BF16)
            rs = sb(st, "rs", [128, 1024], F32)
            tmp = [sb(st, f"tmp{i}", [128, 1024], F32) for i in range(2)]
            wv = sb(st, "wv", [128, 8, 512], BF16)
            wfb = [sb(st, f"wfb{i}", [128, 8, 128], BF16) for i in range(6)]
            ob = [sb(st, f"ob{i}", [128, 1024], BF16) for i in range(4)]
            cosb = sb(st, "cosb", [128, 1024], F32)
            sinb = sb(st, "sinb", [128, 1024], F32)
            t1 = [sb(st, f"t1{i}", [128, 512], F32) for i in range(2)]
            t2 = [sb(st, f"t2{i}", [128, 512], F32) for i in range(2)]
            vst = [sb(st, f"vst{i}", [128, 512], BF16) for i in range(2)]
            cnt = {"w": 0, "o": 0, "p": 0, "t": 0}
            blocks = BLOCKS

            def load_x(bi, kcs=range(8)):
                t0, T, col = blocks[bi]
                for kc in kcs:
                    S.dma("sp", xbs[bi % 2][:, kc, :T], Xv[:, kc, t0:t0 + T], (), (("xb", bi % 2, kc),))

            def norm_gen(bi):
                t0, T, col = blocks[bi]
                return norm_steps(l, 1, col, xbs[bi % 2], ("xb", bi % 2), T, hTs[bi % 2], ("hT", bi % 2), sq, rs, tmp)

            def load_w(ci):
                wi = cnt["w"] % 6
                cnt["w"] += 1
                wb, wbk = wfb[wi], ("wfb", wi)
                S.dma("sp", wb[:], WINFM[l][ci].rearrange("p (k m) -> p k m", k=8), (("WINFM", l, ci),), (wbk,))
                return wb, wbk

            def new_ob():
                oi = cnt["o"] % 4
                cnt["o"] += 1
                return ob[oi], ("ob", oi)

            load_x(0)
            run_all(norm_gen(0))
            owl = []
            for bi, (t0, T, col) in enumerate(blocks):
                hT, hTk = hTs[bi % 2], ("hT", bi % 2)
                nxt = None
                latent = col == 0
                if bi == 1:
                    owl = out_weight_loads(l)
                if latent:
                    S.dma("sp", cosb[:, :T], costab[:, t0:t0 + T], (), ("cosb",))
                    S.dma("sp", sinb[:, :T], sintab[:, t0:t0 + T], (), ("sinb",))

                def fm(w, h0, hl):
                    wb, wbk = w
                    pi = cnt["p"] % 4
                    cnt["p"] += 1
                    for kc in range(8):
                        S.mm(ps[pi][:, :hl], wb[:, kc, :], hT[:, kc, h0:h0 + hl], kc == 0, kc == 7, (wbk, hTk), (PS(pi),), kc == 7)
                    return pi


                ci = 0
                for c in range(3):
                    w = load_w(ci); ci += 1
                    o, ok = new_ob()
                    for (h0, hl) in halves(T):
                        pi = fm(w, h0, hl)
                        S.act(o[:, h0:h0 + hl], ps[pi][:, :hl], AF.Copy, (PS(pi),), (ok,), scale=0.125)
                    S.dma("pool", QA[c * 128:(c + 1) * 128, t0:t0 + T], o[:, :T], (ok,), ())
                for c in range(3):
                    w = load_w(ci); ci += 1
                    o, ok = new_ob()
                    for (h0, hl) in halves(T):
                        pi = fm(w, h0, hl)
                        S.copy("dve", o[:, h0:h0 + hl], ps[pi][:, :hl], (PS(pi),), (ok,))
                    S.dma("pool", KA[c * 128:(c + 1) * 128, t0:t0 + T], o[:, :T], (ok,), ())
                for c in range(4):
                    dst = QB[c * 128:(c + 1) * 128, t0:t0 + T] if c < 3 else KB[:, t0:t0 + T]
                    w = load_w(ci); ci += 1
                    wsw = load_w(ci) if latent else None
                    ci += 1
                    o, ok = new_ob()
                    for (h0, hl) in halves(T):
                        pq = fm(w, h0, hl)
                        if latent:
                            psw = fm(wsw, h0, hl)
                            ti = cnt["t"] % 2
                            cnt["t"] += 1
                            S.tt("dve", t1[ti][:, :hl], ps[pq][:, :hl], cosb[:, h0:h0 + hl], ALU.mult, (PS(pq), "cosb"), (("t1", ti),))
                            S.tt("dve", t2[ti][:, :hl], ps[psw][:, :hl], sinb[:, h0:h0 + hl], ALU.mult, (PS(psw), "sinb"), (("t2", ti),))
                            S.tt("pool", o[:, h0:h0 + hl], t1[ti][:, :hl], t2[ti][:, :hl], ALU.add, (("t1", ti), ("t2", ti)), (ok,))
                        else:
                            S.copy("dve", o[:, h0:h0 + hl], ps[pq][:, :hl], (PS(pq),), (ok,))
                    S.dma("pool", dst, o[:, :T], (ok,), ())
                for c in range(2):
                    wa_ = load_w(ci); ci += 1
                    wg_ = load_w(ci); ci += 1
                    o, ok = new_ob()
                    for (h0, hl) in halves(T):
                        pa_ = fm(wa_, h0, hl)
                        pg_ = fm(wg_, h0, hl)
                        ti = cnt["t"] % 2
                        cnt["t"] += 1
                        S.act(t1[ti][:, :hl], ps[pg_][:, :hl], AF.Sigmoid, (PS(pg_),), (("t1", ti),))
                        S.tt("dve", o[:, h0:h0 + hl], t1[ti][:, :hl], ps[pa_][:, :hl], ALU.mult, (("t1", ti), PS(pa_)), (ok,))
                    S.dma("pool", HC[c * 128:(c + 1) * 128, t0:t0 + T], o[:, :T], (ok,), ())
                for gi in range(24):
                    if gi == 0 and bi == 0:
                        S.dma("sp", wv[:], WINV[l].rearrange("p (k m) -> p k m", k=8), (("WINV", l),), ("wv",))
                    if gi < 8 and bi + 1 < len(blocks):
                        load_x(bi + 1, [gi])
                        if gi == 7:
                            nxt = norm_gen(bi + 1)
                    if gi % 4 == 0:
                        pump_conv(1)
                    if gi % 4 == 2 and owl:
                        owl.pop(0)()
                    w = load_w(ci); ci += 1
                    o, ok = new_ob()
                    for (h0, hl) in halves(T):
                        pi = fm(w, h0, hl)
                        S.act(o[:, h0:h0 + hl], ps[pi][:, :hl], AF.Sigmoid, (PS(pi), "bgate"), (ok,), bias=bgate_s[:, l, gi:gi + 1])
                    S.dma("pool", G[gi * 128:(gi + 1) * 128, t0:t0 + T], o[:, :T], (ok,), ())
                    if gi >= 8 and nxt is not None:
                        next(nxt, None)
                assert ci == N_FM
                for tt_ in range(T // 128):
                    pi = 4 + tt_ % 2
                    for kc in range(8):
                        S.mm(ps[pi][:, :], hT[:, kc, tt_ * 128:(tt_ + 1) * 128], wv[:, kc, :], kc == 0, kc == 7,
                             (hTk, "wv"), (PS(pi),), kc == 7)
                    v, vk = vst[tt_ % 2], ("vst", tt_ % 2)
                    S.copy("dve", v[:], ps[pi][:, :], (PS(pi),), (vk,))
                    r0 = t0 + tt_ * 128
                    S.dma("pool", VA[r0:r0 + 128, :], v[:, 0:384], (vk,), ())
                    S.dma("pool", VB[r0:r0 + 128, :], v[:, 384:512], (vk,), ())
                    if nxt is not None:
                        next(nxt, None)
                if nxt is not None:
                    run_all(nxt)
            while owl:
                owl.pop(0)()
        return finish_phase(f"proj{l}")

    def mix_phase(l, last):
        pump_until(("NATB", l, 2, 5))
        with contextlib.ExitStack() as st:
            woa, wob, woc, wo = res["woa"], res["wob"], res["woc"], res["wo"]
            nat = sb(st, "nat", [128, 6, NA_STRIP], BF16)
            swm = sb(st, "swm", [128, 384], BF16)
            esk = sb(st, "esk", [128, 3], F32)
            cw = sb(st, "cw", [128, 2, 31], F32); cb = sb(st, "cb", [128, 2], F32)
            lg = sb(st, "lg", [128, 2], F32); lb = sb(st, "lb", [128, 2], F32)
            diag = sb(st, "diag", [128, 2, 31, 128], BF16)
            kac = sb(st, "kac", [64, 6, 256], BF16); vac = sb(st, "vac", [128, 2, 384], BF16)
            kbc = sb(st, "kbc", [64, 2, 256], BF16); vbc = sb(st, "vbc", [128, 2, 128], BF16)
            kat = sb(st, "kat", [64, 6, 1024], BF16); qat = sb(st, "qat", [64, 6, 512], BF16)
            vat = sb(st, "vat", [128, 8, 384], BF16)
            kbt = sb(st, "kbt", [64, 2, 768], BF16); qbt = sb(st, "qbt", [64, 6, 512], BF16)
            vbt = sb(st, "vbt", [128, 6, 128], BF16)
            pT = [sb(st, f"pT{i}", [128, 512], BF16) for i in range(8)]
            rD = [sb(st, f"rD{i}", [128, 512], F32) for i in range(2)]
            yaT = sb(st, "yaT", [128, 3, 512], BF16); ybT = sb(st, "ybT", [128, 3, 512], BF16)
            ycT = sb(st, "ycT", [128, 2, 512], BF16)
            hcb = sb(st, "hcb", [128, 2, 544], BF16)
            hv = sb(st, "hv", [128, 2, 512], F32); hq = sb(st, "hq", [128, 2, 512], F32)
            mu = sb(st, "mu", [128, 512], F32); var = sb(st, "var", [128, 512], F32)
            gt = [sb(st, f"gt{i}", [128, 3, 512], BF16) for i in range(2)]
            m1 = [sb(st, f"m1{i}", [128, 512], F32) for i in range(2)]
            m2 = [sb(st, f"m2{i}", [128, 512], F32) for i in range(2)]
            m3 = [sb(st, f"m3{i}", [128, 512], F32) for i in range(2)]
            mT = sb(st, "mT", [128, 8, 512], BF16)
            xb = sb(st, "xbm", [128, 8, 512], F32)

            S.dma("pool", swm[:], swmask[:, :], (), ("swm",), max_dma_last_dim=4096)
            S.dma("sp", esk[:], sinkp[l], (), ("esk",))
            S.act(esk[:], esk[:], AF.Exp, ("esk",), ("esk",))
            S.dma("sp", cw[:], convw[l], (), ("cw",)); S.dma("sp", cb[:], convb[l], (), ("cb",))
            S.dma("sp", lg[:], lng[l], (), ("lg",)); S.dma("sp", lb[:], lnb[l], (), ("lb",))
            for c in range(2):
                for w in range(31):
                    S.ts("dve", diag[:, c, w, :], ident_f[:], cw[:, c, w:w + 1], ALU.mult, ("ident_f", "cw"), ("diag",))
            S.dma("sp", kac[:], KA[:, SEQ:NT].rearrange("(h d) t -> d h t", d=64), (), ("kac",))
            S.dma("sp", kbc[:], KB[:, SEQ:NT].rearrange("(h d) t -> d h t", d=64), (), ("kbc",))
            S.dma("sp", vac[:], VA[SEQ:NT, :].rearrange("(c p) f -> p c f", p=128), (), ("vac",))
            S.dma("sp", vbc[:], VB[SEQ:NT, :].rearrange("(c p) f -> p c f", p=128), (), ("vbc",))

            att = {"i": 0, "queue": [], "pend": [], "pair": 0, "cfg": None, "bg": None}
            GRP = 3
            SBANKS = (0, 1, 2, 7)

            def emit_pv_group(items):
                for (p, pk, v_ap, vk, qa, qb, hp, first, last, obank, dbank, fin) in items:
                    nq = qb - qa
                    lo = hp * 64
                    S.mm(ps[obank][lo:lo + 64, qa:qb], v_ap, p[:, :nq], first, last, (vk, pk), (PS(obank),), False)
                    S.mm(ps[dbank][lo:lo + 64, qa:qb], ones_b[:, 0:64], p[:, :nq], first, last, ("ones_b", pk), (PS(dbank),), True)
                    if fin is not None:
                        fin()

            def flush_group():
                grp = att["pend"]
                att["pend"] = []
                if not grp:
                    return
                steps = []
                for (q_ap, qk, chunk, hp, exp_scale, first, last, obank, dbank, fin) in grp:
                    i = att["i"]
                    att["i"] += 1
                    steps.append((SBANKS[i % 4], pT[i % 8], ("pT", i % 8)))
                for (sbk, p, pk), (q_ap, qk, chunk, hp, exp_scale, first, last, obank, dbank, fin) in zip(steps, grp):
                    (k_ap, kk, v_ap, vk, qa, qb, b_ap, bk) = chunk
                    nq = qb - qa
                    S.mm(ps[sbk][:, :nq], k_ap, q_ap[:, qa:qb], True, b_ap is None, (kk, qk), (PS(sbk),), b_ap is None)
                for (sbk, p, pk), (q_ap, qk, chunk, hp, exp_scale, first, last, obank, dbank, fin) in zip(steps, grp):
                    (k_ap, kk, v_ap, vk, qa, qb, b_ap, bk) = chunk
                    nq = qb - qa
                    if b_ap is not None:
                        S.mm(ps[sbk][:, :nq], ident_b[:], b_ap, False, True, ("ident_b", bk), (PS(sbk),), True)
                items = []
                for (sbk, p, pk), (q_ap, qk, chunk, hp, exp_scale, first, last, obank, dbank, fin) in zip(steps, grp):
                    (k_ap, kk, v_ap, vk, qa, qb, b_ap, bk) = chunk
                    nq = qb - qa
                    S.act(p[:, :nq], ps[sbk][:, :nq], AF.Exp, (PS(sbk),), (pk,), scale=exp_scale)
                    items.append((p, pk, v_ap, vk, qa, qb, hp, first, last, obank, dbank, fin))
                if att["queue"]:
                    emit_pv_group(att["queue"])
                att["queue"] = items
                if att["i"] % 24 == 0:
                    pump_conv(1)
                if att["bg"] is not None:
                    if next(att["bg"], "done") == "done":
                        att["bg"] = None

            def att_step(q_ap, qk, chunk, hp, exp_scale, first, last, obank, dbank, fin):
                att["pend"].append((q_ap, qk, chunk, hp, exp_scale, first, last, obank, dbank, fin))
                if len(att["pend"]) >= GRP:
                    flush_group()

            def att_drain():
                flush_group()
                if att["queue"]:
                    emit_pv_group(att["queue"])
                    att["queue"] = []

            def att_pair(T, q_tile, qk, c, chunks_of, exp_scale, dst, dstk, sink_ap):
                pr = att["pair"] % 2
                att["pair"] += 1
                obank, dbank = (3, 4) if pr == 0 else (5, 6)
                r, rk = rD[pr], ("rD", pr)

                def fin():
                    if sink_ap is not None:
                        S.act(r[:, :T], ps[dbank][:, :T], AF.Ln, (PS(dbank), "esk"), (rk,), bias=sink_ap)
                    else:
                        S.act(r[:, :T], ps[dbank][:, :T], AF.Ln, (PS(dbank),), (rk,))
                    S.act(r[:, :T], r[:, :T], AF.Exp, (rk,), (rk,), scale=-1.0)
                    S.tt("dve", dst[:, c, :T], ps[obank][:, :T], r[:, :T], ALU.mult, (PS(obank), rk), (dstk,))

                for hp in range(2):
                    h = 2 * c + hp
                    chunks = chunks_of(h)
                    assert chunks[0][4] == 0 and chunks[0][5] == T
                    for ci, ch in enumerate(chunks):
                        lastc = ci == len(chunks) - 1
                        att_step(q_tile[:, h, :], qk, ch, hp, exp_scale, ci == 0, lastc, obank, dbank,
                                 fin if (lastc and hp == 1) else None)

            def ctx_chunks_a(h, T):
                return [(kac[:, h, cc_ * 128:(cc_ + 1) * 128], "kac", vac[:, cc_, h * 64:(h + 1) * 64], "vac", 0, T, None, None)
                        for cc_ in range(2)]

            def ctx_chunks_b(kv, T):
                return [(kbc[:, kv, cc_ * 128:(cc_ + 1) * 128], "kbc", vbc[:, cc_, kv * 64:(kv + 1) * 64], "vbc", 0, T, None, None)
                        for cc_ in range(2)]

            def load_hcb(T, t0, lim_lo, lim_hi):
                a0 = max(t0 - 15, lim_lo)
                a1 = min(t0 + T + 15, lim_hi)
                if a0 > t0 - 15:
                    S.memset("pool", hcb[:, :, 0:15], 0.0, ("hcb",))
                if a1 < t0 + T + 15:
                    S.memset("pool", hcb[:, :, T + 15:T + 30], 0.0, ("hcb",))
                o0 = a0 - (t0 - 15)
                S.dma("sp", hcb[:, :, o0:o0 + (a1 - a0)], HC[:, a0:a1].rearrange("(c p) t -> p c t", p=128), (), ("hcb",))

            def conv_ln(T):
                for c in range(2):
                    bank = (2, 7)[c]
                    for w in range(31):
                        S.mm(ps[bank][:, :T], diag[:, c, w, :], hcb[:, c, w:w + T], w == 0, w == 30, ("diag", "hcb"),
                             (PS(bank),), w == 30)
                    S.act(hv[:, c, :T], ps[bank][:, :T], AF.Identity, (PS(bank), "cb"), ("hv",), bias=cb[:, c:c + 1])
                S.act(hq[:, :, :T], hv[:, :, :T], AF.Square, ("hv",), ("hq",))
                for c in range(2):
                    S.mm(ps[2][:, :T], ones_f[:], hv[:, c, :T], c == 0, c == 1, ("ones_f", "hv"), (PS(2),), c == 1)
                for c in range(2):
                    S.mm(ps[7][:, :T], ones_f[:], hq[:, c, :T], c == 0, c == 1, ("ones_f", "hq"), (PS(7),), c == 1)
                S.act(mu[:, :T], ps[2][:, :T], AF.Copy, (PS(2),), ("mu",), scale=1.0 / 256)
                S.act(hq[:, 0, :T], ps[7][:, :T], AF.Copy, (PS(7), "hq"), ("hq",), scale=1.0 / 256)
                yield
                S.tt("dve", var[:, :T], mu[:, :T], mu[:, :T], ALU.mult, ("mu",), ("var",))
                S.tt("dve", var[:, :T], hq[:, 0, :T], var[:, :T], ALU.subtract, ("hq", "var"), ("var",))
                S.ts("dve", var[:, :T], var[:, :T], 0.0, ALU.max, ("var",), ("var",))
                yield
                S.act(var[:, :T], var[:, :T], AF.Ln, ("var", "eps_s"), ("var",), bias=eps_s[:, 0:1], scale=1.0)
                yield
                S.act(var[:, :T], var[:, :T], AF.Exp, ("var",), ("var",), scale=-0.5)
                yield
                for c in range(2):
                    S.tt("dve", hv[:, c, :T], hv[:, c, :T], mu[:, :T], ALU.subtract, ("hv", "mu"), ("hv",))
                    S.tt("dve", hv[:, c, :T], hv[:, c, :T], var[:, :T], ALU.mult, ("hv", "var"), ("hv",))
                    yield

            def conv_silu(T):
                for c in range(2):
                    S.act(ycT[:, c, :T], hv[:, c, :T], AF.Silu, ("hv", "lg", "lb"), ("ycT",),
                          bias=lb[:, c:c + 1], scale=lg[:, c:c + 1])

            def merge(T, t0, col):
                Gv = G.rearrange("(b c p) t -> p b c t", b=3, p=128)
                for dc in range(8):
                    g_, gk = gt[dc % 2], ("gt", dc % 2)
                    S.dma("sp", g_[:, :, :T], Gv[:, :, dc, t0:t0 + T], (), (gk,))
                    for k in range(3):
                        S.mm(ps[0][:, :T], woa[:, k, dc * 128:(dc + 1) * 128], yaT[:, k, :T], k == 0, k == 2,
                             ("woa", "yaT"), (PS(0),), k == 2)
                    for k in range(3):
                        S.mm(ps[1][:, :T], wob[:, k, dc * 128:(dc + 1) * 128], ybT[:, k, :T], k == 0, k == 2,
                             ("wob", "ybT"), (PS(1),), k == 2)
                    for k in range(2):
                        S.mm(ps[2][:, :T], woc[:, k, dc * 128:(dc + 1) * 128], ycT[:, k, :T], k == 0, k == 1,
                             ("woc", "ycT"), (PS(2),), k == 1)
                    i = dc % 2
                    S.tt("dve", m1[i][:, :T], ps[0][:, :T], g_[:, 0, :T], ALU.mult, (PS(0), gk), (("m1", i),))
                    S.tt("dve", m2[i][:, :T], ps[1][:, :T], g_[:, 1, :T], ALU.mult, (PS(1), gk), (("m2", i),))
                    S.tt("dve", m3[i][:, :T], ps[2][:, :T], g_[:, 2, :T], ALU.mult, (PS(2), gk), (("m3", i),))
                    S.tt("pool", m1[i][:, :T], m1[i][:, :T], m2[i][:, :T], ALU.add, (("m1", i), ("m2", i)), (("m1", i),))
                    S.tt("pool", mT[:, dc, :T], m1[i][:, :T], m3[i][:, :T], ALU.add, (("m1", i), ("m3", i)), ("mT",))
                for dc in range(8):
                    bank = (7, 3)[dc % 2]
                    for k in range(8):
                        S.mm(ps[bank][:, :T], wo[:, k, dc * 128:(dc + 1) * 128], mT[:, k, :T], k == 0, k == 7,
                             ("wo", "mT"), (PS(bank),), k == 7)
                    S.stt(xb[:, dc, :T], ps[bank][:, :T], Gate[l][:, 1, dc, col:col + 1], xb[:, dc, :T],
                          ALU.mult, ALU.add, (PS(bank), ("Gate", l, 1), "xbm"), ("xbm",))
                S.dma("pool", Xv[:, :, t0:t0 + T], xb[:, :, :T], ("xbm",), ())

            def load_na(tb):
                plan = na_block_plan(tb)
                mlo, mhi = plan[0][0], plan[-1][0]
                nk = (mhi - mlo + 1) * 128
                assert nk <= 1024
                t0 = tb * 512
                if att["cfg"] != na_cfg(tb):
                    att["cfg"] = na_cfg(tb)
                    natv = NATB[l][att["cfg"]].rearrange("p (h q) -> p h q", h=6)
                    S.dma("sp", nat[:], natv, tuple(("NATB", l, att["cfg"], h) for h in range(6)), ("nat",))
                S.dma("sp", kat[:, :, 0:nk], KA[:, mlo * 128:mlo * 128 + nk].rearrange("(h d) t -> d h t", d=64), (), ("kat",))
                S.dma("sp", qat[:], QA[:, t0:t0 + 512].rearrange("(h d) t -> d h t", d=64), (), ("qat",))
                S.dma("sp", vat[:, 0:nk // 128, :], VA[mlo * 128:mlo * 128 + nk, :].rearrange("(c p) f -> p c f", p=128),
                      (), ("vat",))

            def load_sw(tb):
                t0 = tb * 512
                k0 = max(t0 - 128, 0)
                k1 = min(t0 + 640, SEQ)
                S.dma("sp", kbt[:, :, 0:k1 - k0], KB[:, k0:k1].rearrange("(h d) t -> d h t", d=64), (), ("kbt",))
                S.dma("sp", qbt[:], QB[:, t0:t0 + 512].rearrange("(h d) t -> d h t", d=64), (), ("qbt",))
                S.dma("sp", vbt[:, 0:(k1 - k0) // 128, :], VB[k0:k1, :].rearrange("(c p) f -> p c f", p=128), (), ("vbt",))

            for tb in range(8):
                t0 = tb * 512
                plan = na_block_plan(tb)
                mlo, mhi = plan[0][0], plan[-1][0]
                k0 = max(t0 - 128, 0)
                k1 = min(t0 + 640, SEQ)
                if tb == 0:
                    load_na(0)
                    load_sw(0)
                load_hcb(512, t0, 0, SEQ)
                S.dma("sp", xb[:, :, :512], Xv[:, :, t0:t0 + 512], (), ("xbm",))

                def na_chunks_of(h):
                    chunks = ctx_chunks_a(h, 512)
                    for (m, nf, nl, off) in plan:
                        qa_, qb_ = (nf - 4 * tb) * 128, (nl - 4 * tb + 1) * 128
                        chunks.append((kat[:, h, (m - mlo) * 128:(m - mlo + 1) * 128], "kat",
                                       vat[:, m - mlo, h * 64:(h + 1) * 64], "vat", qa_, qb_,
                                       nat[:, h, off:off + (qb_ - qa_)], "nat"))
                    return chunks

                def sw_chunks_of(h):
                    kv = h // 3
                    chunks = ctx_chunks_b(kv, 512)
                    for kb_ in range(max(4 * tb - 1, 0), min(4 * tb + 4, 31) + 1):
                        nlo, nhi = max(kb_ - 1, 4 * tb), min(kb_ + 1, 4 * tb + 3)
                        qa_, qb_ = (nlo - 4 * tb) * 128, (nhi - 4 * tb + 1) * 128
                        off = kb_ * 128 - k0
                        mo = (nlo - (kb_ - 1)) * 128
                        chunks.append((kbt[:, kv, off:off + 128], "kbt", vbt[:, off // 128, kv * 64:(kv + 1) * 64], "vbt",
                                       qa_, qb_, swm[:, mo:mo + (qb_ - qa_)], "swm"))
                    return chunks

                for c in range(3):
                    att_pair(512, qat, "qat", c, na_chunks_of, 1.0, yaT, "yaT", None)
                att_drain()
                if tb + 1 < 8:
                    load_na(tb + 1)
                elif not last:
                    S.dma("sp", qat[:, :, 0:256], QA[:, SEQ:NT].rearrange("(h d) t -> d h t", d=64), (), ("qat",))
                att["bg"] = conv_ln(512)
                next(att["bg"])
                for c in range(3):
                    att_pair(512, qbt, "qbt", c, sw_chunks_of, 0.125, ybT, "ybT", esk[:, c:c + 1])
                att_drain()
                if att["bg"] is not None:
                    run_all(att["bg"])
                    att["bg"] = None
                conv_silu(512)
                if tb + 1 < 8:
                    load_sw(tb + 1)
                elif not last:
                    S.dma("sp", qbt[:, :, 0:256], QB[:, SEQ:NT].rearrange("(h d) t -> d h t", d=64), (), ("qbt",))
                merge(512, t0, 0)

            if not last:
                load_hcb(256, SEQ, SEQ, NT)
                S.dma("sp", xb[:, :, :256], Xv[:, :, SEQ:NT], (), ("xbm",))
                for c in range(3):
                    att_pair(256, qat, "qat", c, lambda h: ctx_chunks_a(h, 256), 1.0, yaT, "yaT", None)
                att_drain()
                att["bg"] = conv_ln(256)
                next(att["bg"])
                for c in range(3):
                    att_pair(256, qbt, "qbt", c, lambda h: ctx_chunks_b(h // 3, 256), 0.125, ybT, "ybT", esk[:, c:c + 1])
                att_drain()
                if att["bg"] is not None:
                    run_all(att["bg"])
                    att["bg"] = None
                conv_silu(256)
                merge(256, SEQ, 1)
        return finish_phase(f"mix{l}")

    def final_phase():
        with contextlib.ExitStack() as st:
            xbs = [sb(st, f"xb{i}", [128, 8, 512], F32) for i in range(2)]
            sq = sb(st, "sq", [128, 8, 512], BF16)
            rs = sb(st, "rs", [128, 512], F32)
            obf = [sb(st, f"obf{i}", [128, 8, 512], F32) for i in range(2)]
            S.dma("sp", xbs[0][:], Xv[:, :, 0:512], (), (("xb", 0),))
            for bi in range(8):
                t0 = bi * 512
                xb, xbk = xbs[bi % 2], ("xb", bi % 2)
                if bi + 1 < 8:
                    S.dma("sp", xbs[(bi + 1) % 2][:], Xv[:, :, t0 + 512:t0 + 1024], (), (("xb", (bi + 1) % 2),))
                S.act(sq[:], xb[:], AF.Square, (xbk,), ("sq",))
                for kc in range(8):
                    S.mm(ps[6][:, :], ones_b[:], sq[:, kc, :], kc == 0, kc == 7, ("ones_b", "sq"), (PS(6),), kc == 7)
                S.act(rs[:], ps[6][:, :], AF.Sqrt, (PS(6), "eps_s"), ("rs",), bias=eps_s[:, 0:1], scale=1.0 / D)
                S.recip(rs[:], rs[:], ("rs",), ("rs",))
                o, ok = obf[bi % 2], ("obf", bi % 2)
                for kc in range(8):
                    S.stt(o[:, kc, :], xb[:, kc, :], finalg_s[:, kc:kc + 1], rs[:], ALU.mult, ALU.mult,
                          (xbk, "finalg", "rs"), (ok,))
                S.dma("pool", outv[:, :, t0:t0 + 512], o[:], (ok,), ())
        S.barrier()

    def program():
        if done["flag"]:
            return
        for l in range(DEPTH):
            last = l == DEPTH - 1
            if ffn_phase(l, 0, (BLOCKS[1:] + BLOCKS[:1]) if l == 0 else BLOCKS):
                return
            if l == 0:
                ada_stack.close()
                ada_state["closed"] = True
            with contextlib.ExitStack() as lst:
                load_out_weights(l, lst)
                if proj_phase(l):
                    return
                if mix_phase(l, last):
                    return
            if ffn_phase(l, 1, ([(0, 256, 0), (256, 768, 0)] + BLOCKS[2:]) if last else BLOCKS, fuse_final=last):
                return
        pass

    program()
    S.barrier()
    S.emit()
    if not ada_state.get("closed"):
        ada_stack.close()
    stack0.close()
    return nc


_CACHE = {}


def kernel(**inputs):
    shared = prep_shared(inputs)
    if "nc" not in _CACHE:
        _CACHE["nc"] = build()
    nc = _CACHE["nc"]
    in_maps = []
    for b in range(8):
        m = dict(shared)
        m.update(prep_core(inputs, b))
        in_maps.append(m)
    res = run_bass_kernel_spmd(nc, in_maps, core_ids=list(range(8)))
    out = np.stack([np.ascontiguousarray(np.asarray(r["outT"], dtype=np.float32).T) for r in res.results])
    return out
```

```python
import contextlib
import numpy as np
import concourse.bass as bass
import concourse.mybir as mybir
from concourse.bass_utils import run_bass_kernel_spmd

F32 = mybir.dt.float32
BF16 = mybir.dt.bfloat16
AF = mybir.ActivationFunctionType
ALU = mybir.AluOpType

D = 1024
NCH = 8
SEQ = 4096
CTX = 256
NT = SEQ + CTX
DFF = 2816
NFF = 22
DEPTH = 2
EPS = 1e-6
NEG = -30000.0
GRID_W = 64
N_FM = 42
BLOCKS = [(SEQ, CTX, 1)] + [(i * 1024, 1024, 0) for i in range(4)]


class Sched:
    def __init__(self, nc, same_engine_sync=True, ndma=32, nconv=6):
        self.nc = nc
        self.engs = ["pe", "act", "dve", "pool", "sp"]
        self.q = {e: [] for e in self.engs}
        self.sem = {e: nc.alloc_semaphore(f"sem_{e}") for e in ["pe", "act", "dve", "pool"]}
        self.cnt = {e: 0 for e in self.sem}
        self.pending = {e: False for e in self.sem}
        self.seen = {e: {} for e in self.engs}
        self.lastw = {}
        self.readers = {}
        self.same = same_engine_sync
        self.ndma = ndma
        self.dsem = [nc.alloc_semaphore(f"dsem{i}") for i in range(ndma + nconv)]
        self.dval = [0] * (ndma + nconv)
        self.dnext = 0
        self.pnext = 0
        self.cnext = 0
        self.nconv = nconv

    def _semof(self, k):
        return self.sem[k[1]] if k[0] == "e" else self.dsem[k[1]]

    def _wait(self, e, tickets):
        need = {}
        for (k, v) in tickets:
            if k[0] == "e" and k[1] == e and (e == "pe" or not self.same):
                continue
            if v > need.get(k, 0):
                need[k] = v
        for k, v in need.items():
            if self.seen[e].get(k, 0) >= v:
                continue
            self.seen[e][k] = v
            sem = self._semof(k)
            self.q[e].append(lambda eng, sem=sem, v=v: eng.wait_ge(sem, v))

    def _deps(self, reads, writes):
        t = []
        for k in reads:
            if k in self.lastw:
                t.append(self.lastw[k])
        for k in writes:
            if k in self.lastw:
                t.append(self.lastw[k])
            t.extend(self.readers.get(k, {}).items())
        return t

    def _commit(self, tk, reads, writes):
        for k in reads:
            r = self.readers.setdefault(k, {})
            if tk[1] > r.get(tk[0], 0):
                r[tk[0]] = tk[1]
        for k in writes:
            self.lastw[k] = tk
            self.readers[k] = {}

    def op(self, e, fn, reads=(), writes=(), signal=True):
        self._wait(e, self._deps(reads, writes))
        if signal:
            self.cnt[e] += 1
            tk = (("e", e), self.cnt[e])
            sem = self.sem[e]
            self.q[e].append(lambda eng, fn=fn, sem=sem: fn(eng).then_inc(sem, 1))
            self.pending[e] = False
        else:
            tk = (("e", e), self.cnt[e] + 1)
            self.pending[e] = True
            self.q[e].append(lambda eng, fn=fn: fn(eng))
        self._commit(tk, reads, writes)

    def dma(self, e, out, in_, reads=(), writes=(), conv=False, **kw):
        if conv:
            i = self.ndma + self.cnext
            self.cnext = (self.cnext + 1) % self.nconv
        elif e == "pool":
            half = self.ndma // 2
            i = half + self.pnext
            self.pnext = (self.pnext + 1) % (self.ndma - half)
        else:
            i = self.dnext
            self.dnext = (self.dnext + 1) % (self.ndma // 2)
        for k in reads:
            if isinstance(k, tuple) and k and k[0] in ("WGU", "WD", "WINFM", "WINV", "NATB"):
                assert k in self.lastw, f"converted weight {k} read before its conversion DMA was issued"
        deps = self._deps(reads, writes)
        if self.dval[i] > 0:
            deps.append((("d", i), self.dval[i]))
        self._wait(e, deps)
        self.dval[i] += 16
        tk = (("d", i), self.dval[i])
        sem = self.dsem[i]
        self.q[e].append(lambda eng, out=out, in_=in_, sem=sem, kw=kw:
                         eng.dma_start(out=out, in_=in_, **kw).then_inc(sem, 16))
        self._commit(tk, reads, writes)
        return tk

    def barrier(self, engines=None):
        for e in self.sem:
            assert not self.pending[e], f"unsignalled op pending on {e} at barrier"
        tks = [(("e", e), self.cnt[e]) for e in self.sem if self.cnt[e] > 0]
        tks += [(("d", i), self.dval[i]) for i in range(self.ndma) if self.dval[i] > 0]
        for e in (engines or self.engs):
            self.same, old = True, self.same
            self._wait(e, [t for t in tks if not (t[0][0] == "e" and t[0][1] == e)])
            self.same = old

    def mm(self, out, lhsT, rhs, start, stop, reads, writes, signal):
        self.op("pe", lambda eng: eng.matmul(out, lhsT=lhsT, rhs=rhs, start=start, stop=stop),
                reads, writes, signal)

    def act(self, out, in_, func, reads, writes, bias=None, scale=None):
        kw = {}
        if bias is not None:
            kw["bias"] = bias
        if scale is not None:
            kw["scale"] = scale
        self.op("act", lambda eng: eng.activation(out=out, in_=in_, func=func, **kw), reads, writes)

    def tt(self, e, out, in0, in1, op, reads, writes):
        self.op(e, lambda eng: eng.tensor_tensor(out=out, in0=in0, in1=in1, op=op), reads, writes)

    def ts(self, e, out, in0, s1, op0, reads, writes, s2=None, op1=None):
        if op1 is None:
            self.op(e, lambda eng: eng.tensor_scalar(out=out, in0=in0, scalar1=s1, scalar2=None, op0=op0),
                    reads, writes)
        else:
            self.op(e, lambda eng: eng.tensor_scalar(out=out, in0=in0, scalar1=s1, scalar2=s2, op0=op0, op1=op1),
                    reads, writes)

    def stt(self, out, in0, scalar, in1, op0, op1, reads, writes):
        self.op("dve", lambda eng: eng.scalar_tensor_tensor(out=out, in0=in0, scalar=scalar, in1=in1,
                                                            op0=op0, op1=op1), reads, writes)

    def copy(self, e, out, in_, reads, writes):
        self.op(e, lambda eng: eng.tensor_copy(out=out, in_=in_), reads, writes)

    def recip(self, out, in_, reads, writes):
        self.op("dve", lambda eng: eng.reciprocal(out=out, in_=in_), reads, writes)

    def memset(self, e, ap, val, writes):
        self.op(e, lambda eng: eng.memset(ap, val), (), writes)

    def emit(self):
        nc = self.nc
        q = self.q
        with nc.Block() as block:
            @block.sync
            def _(eng):
                for f in q["sp"]:
                    f(eng)

            @block.tensor
            def _(eng):
                for f in q["pe"]:
                    f(eng)

            @block.scalar
            def _(eng):
                for f in q["act"]:
                    f(eng)

            @block.vector
            def _(eng):
                for f in q["dve"]:
                    f(eng)

            @block.gpsimd
            def _(eng):
                for f in q["pool"]:
                    f(eng)


def _kr0(r):
    return min(max(r - 4, 0), 56)


def na_chunks(n):
    lo = min(_kr0(2 * n), _kr0(2 * n + 1))
    hi = max(_kr0(2 * n), _kr0(2 * n + 1)) + 7
    return list(range(lo // 2, hi // 2 + 1))


def _na_tile_index(n, m):
    key = np.arange(128)
    yl, kc = key // 64, key % 64
    qq = np.arange(128)
    rl, qc = qq // 64, qq % 64
    y = 2 * m + yl[:, None]
    r = 2 * n + rl[None, :]
    kr = np.clip(r - 4, 0, 56)
    vrow = (y >= kr) & (y <= kr + 7)
    wc0 = np.clip(qc - 8, 0, 48)[None, :]
    vcol = (kc[:, None] >= wc0) & (kc[:, None] < wc0 + 16)
    dy = np.clip(y - r + 7, 0, 14)
    dx = np.clip(kc[:, None] - qc[None, :] + 15, 0, 30)
    valid = vrow & vcol
    return dy, dx, valid


def na_tile_table():
    table, uniq, sig = {}, [], {}
    for n in range(32):
        for m in na_chunks(n):
            dy, dx, valid = _na_tile_index(n, m)
            s = (np.where(valid, dy * 31 + dx, -1)).astype(np.int16).tobytes()
            if s not in sig:
                sig[s] = len(uniq)
                uniq.append((dy, dx, valid))
            table[(n, m)] = sig[s]
    return table, uniq


NA_TABLE, NA_UNIQ = na_tile_table()
N_TILES = len(NA_UNIQ)
NA_STRIP = 2560


def na_block_plan(tb):
    ns = [4 * tb + i for i in range(4)]
    mlo = min(min(na_chunks(n)) for n in ns)
    mhi = max(max(na_chunks(n)) for n in ns)
    plan, off = [], 0
    for m in range(mlo, mhi + 1):
        nm = [n for n in ns if m in na_chunks(n)]
        assert nm == list(range(nm[0], nm[-1] + 1))
        plan.append((m, nm[0], nm[-1], off))
        off += 128 * len(nm)
    assert off <= NA_STRIP
    return plan


def na_cfg(tb):
    return 0 if tb == 0 else (2 if tb == 7 else 1)


def _swap_cols(w, nheads):
    k = w.shape[0]
    w4 = w.reshape(k, nheads, 2, 32)
    return w4[:, :, ::-1, :].reshape(k, nheads * 64)


def _chunks_lhsT(w):
    k, c = w.shape
    return np.ascontiguousarray(w.reshape(k // 128, 128, c // 128, 128).transpose(2, 1, 0, 3))


def _rows_lhsT(w):
    k, m = w.shape
    return np.ascontiguousarray(w.reshape(k // 128, 128, m).transpose(1, 0, 2))


def _vec_cols(v):
    return np.ascontiguousarray(v.reshape(-1, 128).T)


def rope_tables():
    t = np.arange(SEQ)
    row = (t // GRID_W).astype(np.float32)
    col = (t % GRID_W).astype(np.float32)
    n_freq = 16
    inv_freq = (np.float32(10000.0) ** (-np.arange(n_freq, dtype=np.float32) / np.float32(n_freq))).astype(np.float32)
    ang = np.concatenate([row[:, None] * inv_freq, col[:, None] * inv_freq], axis=-1).astype(np.float32)
    cos, sin = np.cos(ang).astype(np.float32), np.sin(ang).astype(np.float32)
    cosT = np.concatenate([cos.T, cos.T, cos.T, cos.T], axis=0)
    sinT = np.concatenate([-sin.T, sin.T, -sin.T, sin.T], axis=0)
    return np.ascontiguousarray(cosT), np.ascontiguousarray(sinT)


def prep_shared(inp):
    f = lambda a: np.ascontiguousarray(np.asarray(a, dtype=np.float32))
    sh = {}
    wa = f(inp["w_ada"])
    sh["wada"] = np.ascontiguousarray(wa.reshape(DEPTH, 8, 128, 36, 256).transpose(0, 3, 2, 1, 4)).reshape(DEPTH, 36, 128, 2048)
    sh["bada"] = np.stack([_vec_cols(f(inp["b_ada"])[l]) for l in range(DEPTH)])
    sh["normg"] = np.stack([np.stack([_vec_cols(f(inp["norm_g"])[l, s]) for s in range(3)]) for l in range(DEPTH)])
    sh["finalg"] = _vec_cols(f(inp["final_g"]))
    for nm, src in (("wgu1", "w_ffn1_gu"), ("wgu2", "w_ffn2_gu")):
        w = f(inp[src])
        outl = []
        for l in range(DEPTH):
            a = _chunks_lhsT(w[l][:, :DFF])
            b = _chunks_lhsT(w[l][:, DFF:])
            outl.append(np.stack([a, b], axis=2).reshape(NFF, 128, 2 * 8 * 128))
        sh[nm] = np.ascontiguousarray(np.stack(outl))
    for nm, src in (("wd1", "w_ffn1_down"), ("wd2", "w_ffn2_down")):
        w = f(inp[src])
        outl = []
        for l in range(DEPTH):
            r = _rows_lhsT(w[l])
            outl.append(np.stack([r[:, :, dc * 128:(dc + 1) * 128].reshape(128, NFF * 128) for dc in range(8)]))
        sh[nm] = np.ascontiguousarray(np.stack(outl))
    w_in = f(inp["w_in"])
    fm, vv = [], []
    for l in range(DEPTH):
        w = w_in[l]
        qa, ka, va = w[:, 0:384], w[:, 384:768], w[:, 768:1152]
        qb, kb, vb = w[:, 1152:1536], w[:, 1536:1664], w[:, 1664:1792]
        u, g = w[:, 1792:2304], w[:, 2304:5376]
        qbs, kbs = _swap_cols(qb, 6), _swap_cols(kb, 2)
        cols = [qa, ka]
        for c in range(3):
            cols += [qb[:, c * 128:(c + 1) * 128], qbs[:, c * 128:(c + 1) * 128]]
        cols += [kb, kbs]
        for c in range(2):
            cols += [u[:, c * 128:(c + 1) * 128], u[:, 256 + c * 128:256 + (c + 1) * 128]]
        cols += [g]
        allc = np.concatenate(cols, axis=1)
        assert allc.shape[1] == N_FM * 128
        fm.append(_chunks_lhsT(allc).reshape(N_FM, 128, 1024))
        vv.append(_rows_lhsT(np.concatenate([va, vb], axis=1)).reshape(128, 8 * 512))
    sh["winfm"] = np.ascontiguousarray(np.stack(fm))
    sh["winv"] = np.ascontiguousarray(np.stack(vv))
    sh["bgate"] = np.stack([_vec_cols(f(inp["b_gate"])[l]) for l in range(DEPTH)])
    rpb = f(inp["na_rpb"])
    strips = np.zeros((DEPTH, 3, 128, 6, NA_STRIP), np.float32)
    for l in range(DEPTH):
        for cfg, tb in ((0, 0), (1, 1), (2, 7)):
            for (m, n0_, n1_, off) in na_block_plan(tb):
                for n in range(n0_, n1_ + 1):
                    dy, dx, valid = NA_UNIQ[NA_TABLE[(n, m)]]
                    g = rpb[l][:, dy, dx]
                    g = np.where(valid[None], g, np.float32(NEG))
                    o = off + (n - n0_) * 128
                    strips[l, cfg, :, :, o:o + 128] = g.transpose(1, 0, 2)
    sh["natile"] = strips.reshape(DEPTH, 3, 128, 6 * NA_STRIP)
    j = np.arange(128)[:, None]
    i = np.arange(128)[None, :]
    swm = np.concatenate([np.where(j <= i, 0.0, NEG), np.zeros((128, 128)), np.where(i <= j, 0.0, NEG)], axis=1).astype(np.float32)
    sh["swmask"] = np.ascontiguousarray(swm)
    sink = f(inp["sw_sink"])
    sk = np.empty((DEPTH, 128, 3), np.float32)
    for l in range(DEPTH):
        for c in range(3):
            sk[l, :64, c] = sink[l, 2 * c]
            sk[l, 64:, c] = sink[l, 2 * c + 1]
    sh["sinkp"] = sk
    cw = f(inp["conv_dw_w"])
    sh["convw"] = np.ascontiguousarray(np.stack([cw[l].T.reshape(2, 128, 31).transpose(1, 0, 2) for l in range(DEPTH)]))
    sh["convb"] = np.stack([_vec_cols(f(inp["conv_dw_b"])[l]) for l in range(DEPTH)])
    sh["lng"] = np.stack([_vec_cols(f(inp["conv_ln_g"])[l]) for l in range(DEPTH)])
    sh["lnb"] = np.stack([_vec_cols(f(inp["conv_ln_b"])[l]) for l in range(DEPTH)])
    sh["woa"] = np.stack([_rows_lhsT(f(inp["w_out_a"])[l]).reshape(128, 3 * 1024) for l in range(DEPTH)])
    sh["wob"] = np.stack([_rows_lhsT(f(inp["w_out_b"])[l]).reshape(128, 3 * 1024) for l in range(DEPTH)])
    sh["woc"] = np.stack([_rows_lhsT(f(inp["w_out_c"])[l]).reshape(128, 2 * 1024) for l in range(DEPTH)])
    sh["wo"] = np.stack([_rows_lhsT(f(inp["w_out"])[l]).reshape(128, 8 * 1024) for l in range(DEPTH)])
    cosT, sinT = rope_tables()
    sh["costab"], sh["sintab"] = cosT, sinT
    sh["ident"] = np.eye(128, dtype=np.float32)
    return {k: np.ascontiguousarray(v, dtype=np.float32) for k, v in sh.items()}


def prep_core(inp, b):
    x = np.asarray(inp["x"][b], dtype=np.float32)
    ctx = np.asarray(inp["ctx"][b], dtype=np.float32)
    c = np.asarray(inp["c"][b], dtype=np.float32)
    cc = np.asarray(inp["c_ctx"], dtype=np.float32)
    return {
        "xT": np.ascontiguousarray(x.T),
        "ctxT": np.ascontiguousarray(ctx.T),
        "cc": np.ascontiguousarray(np.stack([_vec_cols(c), _vec_cols(cc)], axis=2)),
    }


def build(stop_after=None, dumps=(), same_engine_sync=True):
    nc = bass.Bass("TRN2", target_bir_lowering=False)
    S = Sched(nc, same_engine_sync=same_engine_sync)

    def din(name, shape):
        return nc.dram_tensor(name, list(shape), F32, kind="ExternalInput").ap()

    def dscr(name, shape, dt):
        kind = "ExternalOutput" if name in dumps else "Internal"
        return nc.dram_tensor(name, list(shape), dt, kind=kind).ap()

    xT = din("xT", [D, SEQ]); ctxT = din("ctxT", [D, CTX]); cc = din("cc", [128, 8, 2])
    wada = din("wada", [DEPTH, 36, 128, 2048]); bada = din("bada", [DEPTH, 128, 72])
    normg = din("normg", [DEPTH, 3, 128, 8]); finalg = din("finalg", [128, 8])
    wgu_in = [din("wgu1", [DEPTH, NFF, 128, 2048]), din("wgu2", [DEPTH, NFF, 128, 2048])]
    wd_in = [din("wd1", [DEPTH, 8, 128, NFF * 128]), din("wd2", [DEPTH, 8, 128, NFF * 128])]
    winfm_in = din("winfm", [DEPTH, N_FM, 128, 1024]); winv_in = din("winv", [DEPTH, 128, 4096])
    bgate = din("bgate", [DEPTH, 128, 24])
    natile = din("natile", [DEPTH, 3, 128, 6 * NA_STRIP]); swmask = din("swmask", [128, 384])
    sinkp = din("sinkp", [DEPTH, 128, 3])
    convw = din("convw", [DEPTH, 128, 2, 31]); convb = din("convb", [DEPTH, 128, 2])
    lng = din("lng", [DEPTH, 128, 2]); lnb = din("lnb", [DEPTH, 128, 2])
    woa_in = din("woa", [DEPTH, 128, 3072]); wob_in = din("wob", [DEPTH, 128, 3072])
    woc_in = din("woc", [DEPTH, 128, 2048]); wo_in = din("wo", [DEPTH, 128, 8192])
    costab = din("costab", [128, SEQ]); sintab = din("sintab", [128, SEQ]); ident_in = din("ident", [128, 128])
    outT = nc.dram_tensor("outT", [D, SEQ], F32, kind="ExternalOutput").ap()

    X = dscr("X", [D, NT], F32)
    WGU = [[dscr(f"WGU{l}{f}", [NFF, 128, 2048], BF16) for f in range(2)] for l in range(DEPTH)]
    WD = [[dscr(f"WD{l}{f}", [8, 128, NFF * 128], BF16) for f in range(2)] for l in range(DEPTH)]
    WINFM = [dscr(f"WINFM{l}", [N_FM, 128, 1024], BF16) for l in range(DEPTH)]
    WINV = [dscr(f"WINV{l}", [128, 4096], BF16) for l in range(DEPTH)]
    NATB = [dscr(f"NATB{l}", [3, 128, 6 * NA_STRIP], BF16) for l in range(DEPTH)]
    QA = dscr("QA", [384, NT], BF16); KA = dscr("KA", [384, NT], BF16); VA = dscr("VA", [NT, 384], BF16)
    QB = dscr("QB", [384, NT], BF16); KB = dscr("KB", [128, NT], BF16); VB = dscr("VB", [NT, 128], BF16)
    HC = dscr("HC", [256, NT], BF16); G = dscr("G", [3072, NT], BF16)

    Xv = X.rearrange("(c p) t -> p c t", p=128)
    outv = outT.rearrange("(c p) t -> p c t", p=128)

    stack0 = contextlib.ExitStack()

    name_ctr = {"n": 0}

    def sb(stack, name, shape, dt):
        name_ctr["n"] += 1
        return stack.enter_context(nc.sbuf_tensor(f"{name}_{name_ctr['n']}", list(shape), dt))

    ident_f = sb(stack0, "ident_f", [128, 128], F32)
    ident_b = sb(stack0, "ident_b", [128, 128], BF16)
    ones_b = sb(stack0, "ones_b", [128, 128], BF16)
    ones_f = sb(stack0, "ones_f", [128, 128], F32)
    mods = [sb(stack0, f"mods{l}", [128, 72, 2], F32) for l in range(DEPTH)]
    Aeff = [sb(stack0, f"Aeff{l}", [128, 3, 8, 2], F32) for l in range(DEPTH)]
    Gate = [sb(stack0, f"Gate{l}", [128, 3, 8, 2], F32) for l in range(DEPTH)]
    normg_s = sb(stack0, "normg_s", [128, DEPTH * 3, 8], F32)
    finalg_s = sb(stack0, "finalg_s", [128, 8], F32)
    bgate_s = sb(stack0, "bgate_s", [128, DEPTH, 24], F32)
    ps = [stack0.enter_context(nc.psum_tensor(f"ps{i}", [128, 512], F32)) for i in range(8)]
    PS = lambda i: ("ps", i)

    done = {"flag": False}

    def finish_phase(name):
        S.barrier()
        if stop_after == name:
            done["flag"] = True
        return done["flag"]

    conv_list = []

    def conv_dma(dst, src, key):
        conv_list.append((dst, src, key))

    def pump_conv(n):
        for _ in range(n):
            if not conv_list:
                return
            dst, src, key = conv_list.pop(0)
            S.dma("pool", dst, src, (), (key,), conv=True, max_dma_last_dim=4096)

    def pump_until(key):
        while conv_list and key not in S.lastw:
            pump_conv(1)
        assert key in S.lastw

    def issue_conversion(l):
        for j in range(NFF):
            conv_dma(WGU[l][0][j], wgu_in[0][l, j], ("WGU", l, 0, j))
        for dc in range(8):
            conv_dma(WD[l][0][dc], wd_in[0][l, dc], ("WD", l, 0, dc))
        for ci in range(N_FM):
            conv_dma(WINFM[l][ci], winfm_in[l, ci], ("WINFM", l, ci))
        conv_dma(WINV[l], winv_in[l], ("WINV", l))
        for cfg in range(3):
            for h in range(6):
                conv_dma(NATB[l][cfg, :, h * NA_STRIP:(h + 1) * NA_STRIP], natile[l, cfg, :, h * NA_STRIP:(h + 1) * NA_STRIP],
                         ("NATB", l, cfg, h))
        for j in range(NFF):
            conv_dma(WGU[l][1][j], wgu_in[1][l, j], ("WGU", l, 1, j))
        for dc in range(8):
            conv_dma(WD[l][1][dc], wd_in[1][l, dc], ("WD", l, 1, dc))

    eps_s = sb(stack0, "eps_s", [128, 1], F32)
    S.memset("dve", eps_s[:], EPS, ("eps_s",))

    ada_stack = contextlib.ExitStack()
    cc_s = sb(ada_stack, "cc_s", [128, 8, 2], F32)
    sc_b = sb(ada_stack, "sc_b", [128, 8, 2], BF16)
    bada_s = sb(ada_stack, "bada_s", [128, DEPTH, 72], F32)
    wab = [sb(ada_stack, f"wab{i}", [128, 8, 256], BF16) for i in range(2)]
    ada_state = {"k": 0}

    def ada_finalize(l, s_):
        pv = ps[7][:, l * 144:(l + 1) * 144].rearrange("p (j t) -> p j t", t=2)
        j0, j1 = 24 * s_, 24 * (s_ + 1)
        for col in range(2):
            S.tt("dve", mods[l][:, j0:j1, col], pv[:, j0:j1, col], bada_s[:, l, j0:j1], ALU.add,
                 (PS(7), "bada"), (("mods", l, s_),))
        for col in range(2):
            S.stt(Aeff[l][:, s_, :, col], mods[l][:, (3 * s_ + 1) * 8:(3 * s_ + 2) * 8, col], 1.0,
                  normg_s[:, l * 3 + s_, :], ALU.add, ALU.mult, (("mods", l, s_), "normg"), (("Aeff", l, s_),))
            S.ts("dve", Gate[l][:, s_, :, col], mods[l][:, (3 * s_ + 2) * 8:(3 * s_ + 3) * 8, col],
                 0.5 if s_ != 1 else 1.0, ALU.mult, (("mods", l, s_),), (("Gate", l, s_),))

    def ada_group(burst=None):
        k = ada_state["k"]
        if k >= 72:
            return False
        ada_state["k"] += 1
        l, g = k // 36, k % 36
        if burst is not None:
            wb, wbk = burst[k], ("wab_burst", k)
        else:
            wb, wbk = wab[k % 2], ("wab", k % 2)
        S.dma("pool", wb[:], wada[l, g].rearrange("p (kc m) -> p kc m", kc=8), (), (wbk,), max_dma_last_dim=4096)
        for oc in range(2):
            j = g * 2 + oc
            c0 = l * 144 + 2 * j
            for kc in range(8):
                S.mm(ps[7][:, c0:c0 + 2], wb[:, kc, oc * 128:(oc + 1) * 128], sc_b[:, kc, :],
                     kc == 0, kc == 7, (wbk, "sc_b"), (PS(7),), kc == 7)
        if g % 12 == 11:
            ada_finalize(l, g // 12)
        return True

    with contextlib.ExitStack() as st:
        S.dma("sp", ident_f[:], ident_in[:, :], (), ("ident_f",))
        S.copy("dve", ident_b[:], ident_f[:], ("ident_f",), ("ident_b",))
        S.memset("dve", ones_b[:], 1.0, ("ones_b",))
        S.memset("dve", ones_f[:], 1.0, ("ones_f",))
        S.dma("sp", normg_s[:], normg.rearrange("l s p c -> p (l s) c"), (), ("normg",))
        S.dma("sp", finalg_s[:], finalg[:, :], (), ("finalg",))
        S.dma("sp", bgate_s[:], bgate.rearrange("l p g -> p l g"), (), ("bgate",))
        for l in range(DEPTH):
            issue_conversion(l)
        S.dma("sp", cc_s[:], cc[:, :, :], (), ("cc_s",))
        S.dma("sp", bada_s[:], bada.rearrange("l p j -> p l j"), (), ("bada",))
        S.act(sc_b[:], cc_s[:], AF.Silu, ("cc_s",), ("sc_b",))
        burst = [sb(st, f"wabb{i}", [128, 8, 256], BF16) for i in range(12)]
        for _ in range(12):
            ada_group(burst)
        pump_conv(22)
        if finish_phase("ada"):
            pass

    def halves(T):
        return [(h0, min(512, T - h0)) for h0 in range(0, T, 512)]

    def norm_steps(l, s, col, xb, xbk, T, hT, hTk, sq, rs, tmp):
        xall = tuple((xbk[0], xbk[1], kc) for kc in range(8))
        for (h0, hl) in halves(T):
            S.act(sq[:, :, :hl], xb[:, :, h0:h0 + hl], AF.Square, xall, ("sq",))
            yield
            for kc in range(8):
                S.mm(ps[6][:, :hl], ones_b[:], sq[:, kc, :hl], kc == 0, kc == 7, ("ones_b", "sq"), (PS(6),), kc == 7)
            S.act(rs[:, h0:h0 + hl], ps[6][:, :hl], AF.Sqrt, (PS(6), "eps_s"), ("rs",), bias=eps_s[:, 0:1], scale=1.0 / D)
            yield
        S.recip(rs[:, :T], rs[:, :T], ("rs",), ("rs",))
        yield
        for kc in range(8):
            tb_ = tmp[kc % 2]
            S.tt("dve", tb_[:, :T], xb[:, kc, :T], rs[:, :T], ALU.mult, ((xbk[0], xbk[1], kc), "rs"), (("tmp", kc % 2),))
            S.act(hT[:, kc, :T], tb_[:, :T], AF.Identity, (("tmp", kc % 2), ("Aeff", l, s), ("mods", l, s)), (hTk,),
                  bias=mods[l][:, 3 * s * 8 + kc, col:col + 1], scale=Aeff[l][:, s, kc, col:col + 1])
            yield

    def run_all(gen):
        for _ in gen:
            pass


    def ffn_phase(l, f, blocks, fuse_final=False):
        s = 0 if f == 0 else 2
        pump_until(("WD", l, f, 7) if (l, f) != (0, 0) else ("WGU", 0, 0, NFF - 1))
        with contextlib.ExitStack() as st:
            xbs = [sb(st, f"xb{i}", [128, 8, 1024], F32) for i in range(2)]
            hTs = [sb(st, f"hT{i}", [128, 8, 1024], BF16) for i in range(2)]
            sq = sb(st, "sq", [128, 8, 512], BF16)
            rs = sb(st, "rs", [128, 1024], F32)
            tmp = [sb(st, f"tmp{i}", [128, 1024], F32) for i in range(2)]
            gT = sb(st, "gT", [128, NFF, 1024], BF16)
            wgb = [sb(st, f"wgb{i}", [128, 2, 8, 128], BF16) for i in range(3)]
            wdb = [sb(st, f"wdb{i}", [128, NFF, 128], BF16) for i in range(2)]
            sa = [sb(st, f"sa{i}", [128, 512], F32) for i in range(2)]

            xTv = xT.rearrange("(c p) t -> p c t", p=128)
            ctxTv = ctxT.rearrange("(c p) t -> p c t", p=128)

            def load_x(bi, kcs=range(8)):
                t0, T, col = blocks[bi]
                for kc in kcs:
                    if (l, f) == (0, 0):
                        src = ctxTv[:, kc, :] if col == 1 else xTv[:, kc, t0:t0 + T]
                    else:
                        src = Xv[:, kc, t0:t0 + T]
                    S.dma("sp", xbs[bi % 2][:, kc, :T], src, (), (("xb", bi % 2, kc),))

            def norm_gen(bi):
                t0, T, col = blocks[bi]
                return norm_steps(l, s, col, xbs[bi % 2], ("xb", bi % 2), T, hTs[bi % 2], ("hT", bi % 2), sq, rs, tmp)

            load_x(0)
            run_all(norm_gen(0))
            u = 0
            itn = {"n": 0}
            pending_final = []
            for bi, (t0, T, col) in enumerate(blocks):
                xb, xbk = xbs[bi % 2], ("xb", bi % 2)
                hT, hTk = hTs[bi % 2], ("hT", bi % 2)
                nxt = None
                for j in range(NFF):
                    if 2 <= j < 10 and bi + 1 < len(blocks):
                        load_x(bi + 1, [j - 2])
                        if j == 9:
                            nxt = norm_gen(bi + 1)
                    wb, wbk = wgb[j % 3], ("wgb", j % 3)
                    S.dma("sp", wb[:], WGU[l][f][j].rearrange("p (a k m) -> p a k m", a=2, k=8),
                          (("WGU", l, f, j),), (wbk,))
                    for (h0, hl) in halves(T):
                        ia, ib = (u % 2) * 2, (u % 2) * 2 + 1
                        u += 1
                        for kc in range(8):
                            S.mm(ps[ia][:, :hl], wb[:, 0, kc, :], hT[:, kc, h0:h0 + hl], kc == 0, kc == 7, (wbk, hTk),
                                 (PS(ia),), False)
                        for kc in range(8):
                            S.mm(ps[ib][:, :hl], wb[:, 1, kc, :], hT[:, kc, h0:h0 + hl], kc == 0, kc == 7, (wbk, hTk),
                                 (PS(ib),), kc == 7)
                        S.act(sa[u % 2][:, :hl], ps[ia][:, :hl], AF.Silu, (PS(ia),), (("sa", u % 2),))
                        S.tt("dve", gT[:, j, h0:h0 + hl], sa[u % 2][:, :hl], ps[ib][:, :hl], ALU.mult,
                             (("sa", u % 2), PS(ib)), (("gT", j),))
                    if nxt is not None and j >= 10:
                        next(nxt, None)
                    if j == 0 and pending_final:
                        pending_final.pop(0)()
                    itn["n"] += 1
                    if (l, f) == (0, 0):
                        if itn["n"] > 5 and ada_state["k"] < 72:
                            ada_group()
                            if 10 <= itn["n"] < 18:
                                pump_conv(1)
                        elif ada_state["k"] >= 72:
                            pump_conv(1)
                    elif itn["n"] % 2 == 0:
                        pump_conv(1)
                if nxt is not None:
                    run_all(nxt)
                v = 0
                for dc in range(8):
                    wb, wbk = wdb[dc % 2], ("wdb", dc % 2)
                    S.dma("sp", wb[:], WD[l][f][dc].rearrange("p (j m) -> p j m", j=NFF),
                          (("WD", l, f, dc),), (wbk,))
                    for (h0, hl) in halves(T):
                        io = 4 + v % 2
                        v += 1
                        for j in range(NFF):
                            S.mm(ps[io][:, :hl], wb[:, j, :], gT[:, j, h0:h0 + hl], j == 0, j == NFF - 1, (wbk, ("gT", j)),
                                 (PS(io),), j == NFF - 1)
                        S.stt(xb[:, dc, h0:h0 + hl], ps[io][:, :hl], Gate[l][:, s, dc, col:col + 1], xb[:, dc, h0:h0 + hl],
                              ALU.mult, ALU.add, (PS(io), ("Gate", l, s), (xbk[0], xbk[1], dc)), ((xbk[0], xbk[1], dc),))
                    if not fuse_final:
                        S.dma("pool", Xv[:, dc, t0:t0 + T], xb[:, dc, :T], ((xbk[0], xbk[1], dc),), ())
                if fuse_final:
                    def final_norm(xb=xb, xbk=xbk, t0=t0, T=T):
                        xall = tuple((xbk[0], xbk[1], kc) for kc in range(8))
                        for (h0, hl) in halves(T):
                            S.act(sq[:, :, :hl], xb[:, :, h0:h0 + hl], AF.Square, xall, ("sq",))
                            for kc in range(8):
                                S.mm(ps[6][:, :hl], ones_b[:], sq[:, kc, :hl], kc == 0, kc == 7, ("ones_b", "sq"), (PS(6),), kc == 7)
                            S.act(rs[:, h0:h0 + hl], ps[6][:, :hl], AF.Sqrt, (PS(6), "eps_s"), ("rs",), bias=eps_s[:, 0:1], scale=1.0 / D)
                        S.recip(rs[:, :T], rs[:, :T], ("rs",), ("rs",))
                        for kc in range(8):
                            S.stt(xb[:, kc, :T], xb[:, kc, :T], finalg_s[:, kc:kc + 1], rs[:, :T], ALU.mult, ALU.mult,
                                  ((xbk[0], xbk[1], kc), "finalg", "rs"), ((xbk[0], xbk[1], kc),))
                            S.dma("pool", outv[:, kc, t0:t0 + T], xb[:, kc, :T], ((xbk[0], xbk[1], kc),), ())
                    if bi + 1 < len(blocks):
                        pending_final.append(final_norm)
                    else:
                        final_norm()
            if (l, f) == (0, 0):
                while ada_group():
                    pass
        return finish_phase(f"ffn{l}{f}")

    res = {}

    def load_out_weights(l, stack):
        res["woa"] = sb(stack, "woa", [128, 3, 1024], BF16); res["wob"] = sb(stack, "wob", [128, 3, 1024], BF16)
        res["woc"] = sb(stack, "woc", [128, 2, 1024], BF16); res["wo"] = sb(stack, "wo", [128, 8, 1024], BF16)

    def out_weight_loads(l):
        woa, wob, woc, wo = res["woa"], res["wob"], res["woc"], res["wo"]
        todo = []
        for k in range(3):
            todo.append((woa[:, k, :], woa_in[l][:, k * 1024:(k + 1) * 1024], "woa"))
            todo.append((wob[:, k, :], wob_in[l][:, k * 1024:(k + 1) * 1024], "wob"))
        for k in range(2):
            todo.append((woc[:, k, :], woc_in[l][:, k * 1024:(k + 1) * 1024], "woc"))
        for k in range(8):
            todo.append((wo[:, k, :], wo_in[l][:, k * 1024:(k + 1) * 1024], "wo"))
        return [lambda d=d, s_=s_, k_=k_: S.dma("pool", d, s_, (), (k_,), max_dma_last_dim=4096) for (d, s_, k_) in todo]

    def proj_phase(l):
        pump_until(("WINV", l))
        with contextlib.ExitStack() as st:
            xbs = [sb(st, f"xb{i}", [128, 8, 1024], F32) for i in range(2)]
            hTs = [sb(st, f"hT{i}", [128, 8, 1024], BF16) for i in range(2)]
            sq = sb(st, "sq", [128, 8, 512], BF16)
            rs = sb(st, "rs", [128, 1024], F32)
            tmp = [sb(st, f"tmp{i}", [128, 1024], F32) for i in range(2)]
            wv = sb(st, "wv", [128, 8, 512], BF16)
            wfb = [sb(st, f"wfb{i}", [128, 8, 128], BF16) for i in range(6)]
            ob = [sb(st, f"ob{i}", [128, 1024], BF16) for i in range(4)]
            cosb = sb(st, "cosb", [128, 1024], F32)
            sinb = sb(st, "sinb", [128, 1024], F32)
            t1 = [sb(st, f"t1{i}", [128, 512], F32) for i in range(2)]
            t2 = [sb(st, f"t2{i}", [128, 512], F32) for i in range(2)]
            vst = [sb(st, f"vst{i}", [128, 512], BF16) for i in range(2)]
            cnt = {"w": 0, "o": 0, "p": 0, "t": 0}
            blocks = BLOCKS

            def load_x(bi, kcs=range(8)):
                t0, T, col = blocks[bi]
                for kc in kcs:
                    S.dma("sp", xbs[bi % 2][:, kc, :T], Xv[:, kc, t0:t0 + T], (), (("xb", bi % 2, kc),))

            def norm_gen(bi):
                t0, T, col = blocks[bi]
                return norm_steps(l, 1, col, xbs[bi % 2], ("xb", bi % 2), T, hTs[bi % 2], ("hT", bi % 2), sq, rs, tmp)

            def load_w(ci):
                wi = cnt["w"] % 6
                cnt["w"] += 1
                wb, wbk = wfb[wi], ("wfb", wi)
                S.dma("sp", wb[:], WINFM[l][ci].rearrange("p (k m) -> p k m", k=8), (("WINFM", l, ci),), (wbk,))
                return wb, wbk

            def new_ob():
                oi = cnt["o"] % 4
                cnt["o"] += 1
                return ob[oi], ("ob", oi)

            load_x(0)
            run_all(norm_gen(0))
            owl = []
            for bi, (t0, T, col) in enumerate(blocks):
                hT, hTk = hTs[bi % 2], ("hT", bi % 2)
                nxt = None
                latent = col == 0
                if bi == 1:
                    owl = out_weight_loads(l)
                if latent:
                    S.dma("sp", cosb[:, :T], costab[:, t0:t0 + T], (), ("cosb",))
                    S.dma("sp", sinb[:, :T], sintab[:, t0:t0 + T], (), ("sinb",))

                def fm(w, h0, hl):
                    wb, wbk = w
                    pi = cnt["p"] % 4
                    cnt["p"] += 1
                    for kc in range(8):
                        S.mm(ps[pi][:, :hl], wb[:, kc, :], hT[:, kc, h0:h0 + hl], kc == 0, kc == 7, (wbk, hTk), (PS(pi),), kc == 7)
                    return pi


                ci = 0
                for c in range(3):
                    w = load_w(ci); ci += 1
                    o, ok = new_ob()
                    for (h0, hl) in halves(T):
                        pi = fm(w, h0, hl)
                        S.act(o[:, h0:h0 + hl], ps[pi][:, :hl], AF.Copy, (PS(pi),), (ok,), scale=0.125)
                    S.dma("pool", QA[c * 128:(c + 1) * 128, t0:t0 + T], o[:, :T], (ok,), ())
                for c in range(3):
                    w = load_w(ci); ci += 1
                    o, ok = new_ob()
                    for (h0, hl) in halves(T):
                        pi = fm(w, h0, hl)
                        S.copy("dve", o[:, h0:h0 + hl], ps[pi][:, :hl], (PS(pi),), (ok,))
                    S.dma("pool", KA[c * 128:(c + 1) * 128, t0:t0 + T], o[:, :T], (ok,), ())
                for c in range(4):
                    dst = QB[c * 128:(c + 1) * 128, t0:t0 + T] if c < 3 else KB[:, t0:t0 + T]
                    w = load_w(ci); ci += 1
                    wsw = load_w(ci) if latent else None
                    ci += 1
                    o, ok = new_ob()
                    for (h0, hl) in halves(T):
                        pq = fm(w, h0, hl)
                        if latent:
                            psw = fm(wsw, h0, hl)
                            ti = cnt["t"] % 2
                            cnt["t"] += 1
                            S.tt("dve", t1[ti][:, :hl], ps[pq][:, :hl], cosb[:, h0:h0 + hl], ALU.mult, (PS(pq), "cosb"), (("t1", ti),))
                            S.tt("dve", t2[ti][:, :hl], ps[psw][:, :hl], sinb[:, h0:h0 + hl], ALU.mult, (PS(psw), "sinb"), (("t2", ti),))
                            S.tt("pool", o[:, h0:h0 + hl], t1[ti][:, :hl], t2[ti][:, :hl], ALU.add, (("t1", ti), ("t2", ti)), (ok,))
                        else:
                            S.copy("dve", o[:, h0:h0 + hl], ps[pq][:, :hl], (PS(pq),), (ok,))
                    S.dma("pool", dst, o[:, :T], (ok,), ())
                for c in range(2):
                    wa_ = load_w(ci); ci += 1
                    wg_ = load_w(ci); ci += 1
                    o, ok = new_ob()
                    for (h0, hl) in halves(T):
                        pa_ = fm(wa_, h0, hl)
                        pg_ = fm(wg_, h0, hl)
                        ti = cnt["t"] % 2
                        cnt["t"] += 1
                        S.act(t1[ti][:, :hl], ps[pg_][:, :hl], AF.Sigmoid, (PS(pg_),), (("t1", ti),))
                        S.tt("dve", o[:, h0:h0 + hl], t1[ti][:, :hl], ps[pa_][:, :hl], ALU.mult, (("t1", ti), PS(pa_)), (ok,))
                    S.dma("pool", HC[c * 128:(c + 1) * 128, t0:t0 + T], o[:, :T], (ok,), ())
                for gi in range(24):
                    if gi == 0 and bi == 0:
                        S.dma("sp", wv[:], WINV[l].rearrange("p (k m) -> p k m", k=8), (("WINV", l),), ("wv",))
                    if gi < 8 and bi + 1 < len(blocks):
                        load_x(bi + 1, [gi])
                        if gi == 7:
                            nxt = norm_gen(bi + 1)
                    if gi % 4 == 0:
                        pump_conv(1)
                    if gi % 4 == 2 and owl:
                        owl.pop(0)()
                    w = load_w(ci); ci += 1
                    o, ok = new_ob()
                    for (h0, hl) in halves(T):
                        pi = fm(w, h0, hl)
                        S.act(o[:, h0:h0 + hl], ps[pi][:, :hl], AF.Sigmoid, (PS(pi), "bgate"), (ok,), bias=bgate_s[:, l, gi:gi + 1])
                    S.dma("pool", G[gi * 128:(gi + 1) * 128, t0:t0 + T], o[:, :T], (ok,), ())
                    if gi >= 8 and nxt is not None:
                        next(nxt, None)
                assert ci == N_FM
                for tt_ in range(T // 128):
                    pi = 4 + tt_ % 2
                    for kc in range(8):
                        S.mm(ps[pi][:, :], hT[:, kc, tt_ * 128:(tt_ + 1) * 128], wv[:, kc, :], kc == 0, kc == 7,
                             (hTk, "wv"), (PS(pi),), kc == 7)
                    v, vk = vst[tt_ % 2], ("vst", tt_ % 2)
                    S.copy("dve", v[:], ps[pi][:, :], (PS(pi),), (vk,))
                    r0 = t0 + tt_ * 128
                    S.dma("pool", VA[r0:r0 + 128, :], v[:, 0:384], (vk,), ())
                    S.dma("pool", VB[r0:r0 + 128, :], v[:, 384:512], (vk,), ())
                    if nxt is not None:
                        next(nxt, None)
                if nxt is not None:
                    run_all(nxt)
            while owl:
                owl.pop(0)()
        return finish_phase(f"proj{l}")

    def mix_phase(l, last):
        pump_until(("NATB", l, 2, 5))
        with contextlib.ExitStack() as st:
            woa, wob, woc, wo = res["woa"], res["wob"], res["woc"], res["wo"]
            nat = sb(st, "nat", [128, 6, NA_STRIP], BF16)
            swm = sb(st, "swm", [128, 384], BF16)
            esk = sb(st, "esk", [128, 3], F32)
            cw = sb(st, "cw", [128, 2, 31], F32); cb = sb(st, "cb", [128, 2], F32)
            lg = sb(st, "lg", [128, 2], F32); lb = sb(st, "lb", [128, 2], F32)
            diag = sb(st, "diag", [128, 2, 31, 128], BF16)
            kac = sb(st, "kac", [64, 6, 256], BF16); vac = sb(st, "vac", [128, 2, 384], BF16)
            kbc = sb(st, "kbc", [64, 2, 256], BF16); vbc = sb(st, "vbc", [128, 2, 128], BF16)
            kat = sb(st, "kat", [64, 6, 1024], BF16); qat = sb(st, "qat", [64, 6, 512], BF16)
            vat = sb(st, "vat", [128, 8, 384], BF16)
            kbt = sb(st, "kbt", [64, 2, 768], BF16); qbt = sb(st, "qbt", [64, 6, 512], BF16)
            vbt = sb(st, "vbt", [128, 6, 128], BF16)
            pT = [sb(st, f"pT{i}", [128, 512], BF16) for i in range(8)]
            rD = [sb(st, f"rD{i}", [128, 512], F32) for i in range(2)]
            yaT = sb(st, "yaT", [128, 3, 512], BF16); ybT = sb(st, "ybT", [128, 3, 512], BF16)
            ycT = sb(st, "ycT", [128, 2, 512], BF16)
            hcb = sb(st, "hcb", [128, 2, 544], BF16)
            hv = sb(st, "hv", [128, 2, 512], F32); hq = sb(st, "hq", [128, 2, 512], F32)
            mu = sb(st, "mu", [128, 512], F32); var = sb(st, "var", [128, 512], F32)
            gt = [sb(st, f"gt{i}", [128, 3, 512], BF16) for i in range(2)]
            m1 = [sb(st, f"m1{i}", [128, 512], F32) for i in range(2)]
            m2 = [sb(st, f"m2{i}", [128, 512], F32) for i in range(2)]
            m3 = [sb(st, f"m3{i}", [128, 512], F32) for i in range(2)]
            mT = sb(st, "mT", [128, 8, 512], BF16)
            xb = sb(st, "xbm", [128, 8, 512], F32)

            S.dma("pool", swm[:], swmask[:, :], (), ("swm",), max_dma_last_dim=4096)
            S.dma("sp", esk[:], sinkp[l], (), ("esk",))
            S.act(esk[:], esk[:], AF.Exp, ("esk",), ("esk",))
            S.dma("sp", cw[:], convw[l], (), ("cw",)); S.dma("sp", cb[:], convb[l], (), ("cb",))
            S.dma("sp", lg[:], lng[l], (), ("lg",)); S.dma("sp", lb[:], lnb[l], (), ("lb",))
            for c in range(2):
                for w in range(31):
                    S.ts("dve", diag[:, c, w, :], ident_f[:], cw[:, c, w:w + 1], ALU.mult, ("ident_f", "cw"), ("diag",))
            S.dma("sp", kac[:], KA[:, SEQ:NT].rearrange("(h d) t -> d h t", d=64), (), ("kac",))
            S.dma("sp", kbc[:], KB[:, SEQ:NT].rearrange("(h d) t -> d h t", d=64), (), ("kbc",))
            S.dma("sp", vac[:], VA[SEQ:NT, :].rearrange("(c p) f -> p c f", p=128), (), ("vac",))
            S.dma("sp", vbc[:], VB[SEQ:NT, :].rearrange("(c p) f -> p c f", p=128), (), ("vbc",))

            att = {"i": 0, "queue": [], "pend": [], "pair": 0, "cfg": None, "bg": None}
            GRP = 3
            SBANKS = (0, 1, 2, 7)

            def emit_pv_group(items):
                for (p, pk, v_ap, vk, qa, qb, hp, first, last, obank, dbank, fin) in items:
                    nq = qb - qa
                    lo = hp * 64
                    S.mm(ps[obank][lo:lo + 64, qa:qb], v_ap, p[:, :nq], first, last, (vk, pk), (PS(obank),), False)
                    S.mm(ps[dbank][lo:lo + 64, qa:qb], ones_b[:, 0:64], p[:, :nq], first, last, ("ones_b", pk), (PS(dbank),), True)
                    if fin is not None:
                        fin()

            def flush_group():
                grp = att["pend"]
                att["pend"] = []
                if not grp:
                    return
                steps = []
                for (q_ap, qk, chunk, hp, exp_scale, first, last, obank, dbank, fin) in grp:
                    i = att["i"]
                    att["i"] += 1
                    steps.append((SBANKS[i % 4], pT[i % 8], ("pT", i % 8)))
                for (sbk, p, pk), (q_ap, qk, chunk, hp, exp_scale, first, last, obank, dbank, fin) in zip(steps, grp):
                    (k_ap, kk, v_ap, vk, qa, qb, b_ap, bk) = chunk
                    nq = qb - qa
                    S.mm(ps[sbk][:, :nq], k_ap, q_ap[:, qa:qb], True, b_ap is None, (kk, qk), (PS(sbk),), b_ap is None)
                for (sbk, p, pk), (q_ap, qk, chunk, hp, exp_scale, first, last, obank, dbank, fin) in zip(steps, grp):
                    (k_ap, kk, v_ap, vk, qa, qb, b_ap, bk) = chunk
                    nq = qb - qa
                    if b_ap is not None:
                        S.mm(ps[sbk][:, :nq], ident_b[:], b_ap, False, True, ("ident_b", bk), (PS(sbk),), True)
                items = []
                for (sbk, p, pk), (q_ap, qk, chunk, hp, exp_scale, first, last, obank, dbank, fin) in zip(steps, grp):
                    (k_ap, kk, v_ap, vk, qa, qb, b_ap, bk) = chunk
                    nq = qb - qa
                    S.act(p[:, :nq], ps[sbk][:, :nq], AF.Exp, (PS(sbk),), (pk,), scale=exp_scale)
                    items.append((p, pk, v_ap, vk, qa, qb, hp, first, last, obank, dbank, fin))
                if att["queue"]:
                    emit_pv_group(att["queue"])
                att["queue"] = items
                if att["i"] % 24 == 0:
                    pump_conv(1)
                if att["bg"] is not None:
                    if next(att["bg"], "done") == "done":
                        att["bg"] = None

            def att_step(q_ap, qk, chunk, hp, exp_scale, first, last, obank, dbank, fin):
                att["pend"].append((q_ap, qk, chunk, hp, exp_scale, first, last, obank, dbank, fin))
                if len(att["pend"]) >= GRP:
                    flush_group()

            def att_drain():
                flush_group()
                if att["queue"]:
                    emit_pv_group(att["queue"])
                    att["queue"] = []

            def att_pair(T, q_tile, qk, c, chunks_of, exp_scale, dst, dstk, sink_ap):
                pr = att["pair"] % 2
                att["pair"] += 1
                obank, dbank = (3, 4) if pr == 0 else (5, 6)
                r, rk = rD[pr], ("rD", pr)

                def fin():
                    if sink_ap is not None:
                        S.act(r[:, :T], ps[dbank][:, :T], AF.Ln, (PS(dbank), "esk"), (rk,), bias=sink_ap)
                    else:
                        S.act(r[:, :T], ps[dbank][:, :T], AF.Ln, (PS(dbank),), (rk,))
                    S.act(r[:, :T], r[:, :T], AF.Exp, (rk,), (rk,), scale=-1.0)
                    S.tt("dve", dst[:, c, :T], ps[obank][:, :T], r[:, :T], ALU.mult, (PS(obank), rk), (dstk,))

                for hp in range(2):
                    h = 2 * c + hp
                    chunks = chunks_of(h)
                    assert chunks[0][4] == 0 and chunks[0][5] == T
                    for ci, ch in enumerate(chunks):
                        lastc = ci == len(chunks) - 1
                        att_step(q_tile[:, h, :], qk, ch, hp, exp_scale, ci == 0, lastc, obank, dbank,
                                 fin if (lastc and hp == 1) else None)

            def ctx_chunks_a(h, T):
                return [(kac[:, h, cc_ * 128:(cc_ + 1) * 128], "kac", vac[:, cc_, h * 64:(h + 1) * 64], "vac", 0, T, None, None)
                        for cc_ in range(2)]

            def ctx_chunks_b(kv, T):
                return [(kbc[:, kv, cc_ * 128:(cc_ + 1) * 128], "kbc", vbc[:, cc_, kv * 64:(kv + 1) * 64], "vbc", 0, T, None, None)
                        for cc_ in range(2)]

            def load_hcb(T, t0, lim_lo, lim_hi):
                a0 = max(t0 - 15, lim_lo)
                a1 = min(t0 + T + 15, lim_hi)
                if a0 > t0 - 15:
                    S.memset("pool", hcb[:, :, 0:15], 0.0, ("hcb",))
                if a1 < t0 + T + 15:
                    S.memset("pool", hcb[:, :, T + 15:T + 30], 0.0, ("hcb",))
                o0 = a0 - (t0 - 15)
                S.dma("sp", hcb[:, :, o0:o0 + (a1 - a0)], HC[:, a0:a1].rearrange("(c p) t -> p c t", p=128), (), ("hcb",))

            def conv_ln(T):
                for c in range(2):
                    bank = (2, 7)[c]
                    for w in range(31):
                        S.mm(ps[bank][:, :T], diag[:, c, w, :], hcb[:, c, w:w + T], w == 0, w == 30, ("diag", "hcb"),
                             (PS(bank),), w == 30)
                    S.act(hv[:, c, :T], ps[bank][:, :T], AF.Identity, (PS(bank), "cb"), ("hv",), bias=cb[:, c:c + 1])
                S.act(hq[:, :, :T], hv[:, :, :T], AF.Square, ("hv",), ("hq",))
                for c in range(2):
                    S.mm(ps[2][:, :T], ones_f[:], hv[:, c, :T], c == 0, c == 1, ("ones_f", "hv"), (PS(2),), c == 1)
                for c in range(2):
                    S.mm(ps[7][:, :T], ones_f[:], hq[:, c, :T], c == 0, c == 1, ("ones_f", "hq"), (PS(7),), c == 1)
                S.act(mu[:, :T], ps[2][:, :T], AF.Copy, (PS(2),), ("mu",), scale=1.0 / 256)
                S.act(hq[:, 0, :T], ps[7][:, :T], AF.Copy, (PS(7), "hq"), ("hq",), scale=1.0 / 256)
                yield
                S.tt("dve", var[:, :T], mu[:, :T], mu[:, :T], ALU.mult, ("mu",), ("var",))
                S.tt("dve", var[:, :T], hq[:, 0, :T], var[:, :T], ALU.subtract, ("hq", "var"), ("var",))
                S.ts("dve", var[:, :T], var[:, :T], 0.0, ALU.max, ("var",), ("var",))
                yield
                S.act(var[:, :T], var[:, :T], AF.Ln, ("var", "eps_s"), ("var",), bias=eps_s[:, 0:1], scale=1.0)
                yield
                S.act(var[:, :T], var[:, :T], AF.Exp, ("var",), ("var",), scale=-0.5)
                yield
                for c in range(2):
                    S.tt("dve", hv[:, c, :T], hv[:, c, :T], mu[:, :T], ALU.subtract, ("hv", "mu"), ("hv",))
                    S.tt("dve", hv[:, c, :T], hv[:, c, :T], var[:, :T], ALU.mult, ("hv", "var"), ("hv",))
                    yield

            def conv_silu(T):
                for c in range(2):
                    S.act(ycT[:, c, :T], hv[:, c, :T], AF.Silu, ("hv", "lg", "lb"), ("ycT",),
                          bias=lb[:, c:c + 1], scale=lg[:, c:c + 1])

            def merge(T, t0, col):
                Gv = G.rearrange("(b c p) t -> p b c t", b=3, p=128)
                for dc in range(8):
                    g_, gk = gt[dc % 2], ("gt", dc % 2)
                    S.dma("sp", g_[:, :, :T], Gv[:, :, dc, t0:t0 + T], (), (gk,))
                    for k in range(3):
                        S.mm(ps[0][:, :T], woa[:, k, dc * 128:(dc + 1) * 128], yaT[:, k, :T], k == 0, k == 2,
                             ("woa", "yaT"), (PS(0),), k == 2)
                    for k in range(3):
                        S.mm(ps[1][:, :T], wob[:, k, dc * 128:(dc + 1) * 128], ybT[:, k, :T], k == 0, k == 2,
                             ("wob", "ybT"), (PS(1),), k == 2)
                    for k in range(2):
                        S.mm(ps[2][:, :T], woc[:, k, dc * 128:(dc + 1) * 128], ycT[:, k, :T], k == 0, k == 1,
                             ("woc", "ycT"), (PS(2),), k == 1)
                    i = dc % 2
                    S.tt("dve", m1[i][:, :T], ps[0][:, :T], g_[:, 0, :T], ALU.mult, (PS(0), gk), (("m1", i),))
                    S.tt("dve", m2[i][:, :T], ps[1][:, :T], g_[:, 1, :T], ALU.mult, (PS(1), gk), (("m2", i),))
                    S.tt("dve", m3[i][:, :T], ps[2][:, :T], g_[:, 2, :T], ALU.mult, (PS(2), gk), (("m3", i),))
                    S.tt("pool", m1[i][:, :T], m1[i][:, :T], m2[i][:, :T], ALU.add, (("m1", i), ("m2", i)), (("m1", i),))
                    S.tt("pool", mT[:, dc, :T], m1[i][:, :T], m3[i][:, :T], ALU.add, (("m1", i), ("m3", i)), ("mT",))
                for dc in range(8):
                    bank = (7, 3)[dc % 2]
                    for k in range(8):
                        S.mm(ps[bank][:, :T], wo[:, k, dc * 128:(dc + 1) * 128], mT[:, k, :T], k == 0, k == 7,
                             ("wo", "mT"), (PS(bank),), k == 7)
                    S.stt(xb[:, dc, :T], ps[bank][:, :T], Gate[l][:, 1, dc, col:col + 1], xb[:, dc, :T],
                          ALU.mult, ALU.add, (PS(bank), ("Gate", l, 1), "xbm"), ("xbm",))
                S.dma("pool", Xv[:, :, t0:t0 + T], xb[:, :, :T], ("xbm",), ())

            def load_na(tb):
                plan = na_block_plan(tb)
                mlo, mhi = plan[0][0], plan[-1][0]
                nk = (mhi - mlo + 1) * 128
                assert nk <= 1024
                t0 = tb * 512
                if att["cfg"] != na_cfg(tb):
                    att["cfg"] = na_cfg(tb)
                    natv = NATB[l][att["cfg"]].rearrange("p (h q) -> p h q", h=6)
                    S.dma("sp", nat[:], natv, tuple(("NATB", l, att["cfg"], h) for h in range(6)), ("nat",))
                S.dma("sp", kat[:, :, 0:nk], KA[:, mlo * 128:mlo * 128 + nk].rearrange("(h d) t -> d h t", d=64), (), ("kat",))
                S.dma("sp", qat[:], QA[:, t0:t0 + 512].rearrange("(h d) t -> d h t", d=64), (), ("qat",))
                S.dma("sp", vat[:, 0:nk // 128, :], VA[mlo * 128:mlo * 128 + nk, :].rearrange("(c p) f -> p c f", p=128),
                      (), ("vat",))

            def load_sw(tb):
                t0 = tb * 512
                k0 = max(t0 - 128, 0)
                k1 = min(t0 + 640, SEQ)
                S.dma("sp", kbt[:, :, 0:k1 - k0], KB[:, k0:k1].rearrange("(h d) t -> d h t", d=64), (), ("kbt",))
                S.dma("sp", qbt[:], QB[:, t0:t0 + 512].rearrange("(h d) t -> d h t", d=64), (), ("qbt",))
                S.dma("sp", vbt[:, 0:(k1 - k0) // 128, :], VB[k0:k1, :].rearrange("(c p) f -> p c f", p=128), (), ("vbt",))

            for tb in range(8):
                t0 = tb * 512
                plan = na_block_plan(tb)
                mlo, mhi = plan[0][0], plan[-1][0]
                k0 = max(t0 - 128, 0)
                k1 = min(t0 + 640, SEQ)
                if tb == 0:
                    load_na(0)
                    load_sw(0)
                load_hcb(512, t0, 0, SEQ)
                S.dma("sp", xb[:, :, :512], Xv[:, :, t0:t0 + 512], (), ("xbm",))

                def na_chunks_of(h):
                    chunks = ctx_chunks_a(h, 512)
                    for (m, nf, nl, off) in plan:
                        qa_, qb_ = (nf - 4 * tb) * 128, (nl - 4 * tb + 1) * 128
                        chunks.append((kat[:, h, (m - mlo) * 128:(m - mlo + 1) * 128], "kat",
                                       vat[:, m - mlo, h * 64:(h + 1) * 64], "vat", qa_, qb_,
                                       nat[:, h, off:off + (qb_ - qa_)], "nat"))
                    return chunks

                def sw_chunks_of(h):
                    kv = h // 3
                    chunks = ctx_chunks_b(kv, 512)
                    for kb_ in range(max(4 * tb - 1, 0), min(4 * tb + 4, 31) + 1):
                        nlo, nhi = max(kb_ - 1, 4 * tb), min(kb_ + 1, 4 * tb + 3)
                        qa_, qb_ = (nlo - 4 * tb) * 128, (nhi - 4 * tb + 1) * 128
                        off = kb_ * 128 - k0
                        mo = (nlo - (kb_ - 1)) * 128
                        chunks.append((kbt[:, kv, off:off + 128], "kbt", vbt[:, off // 128, kv * 64:(kv + 1) * 64], "vbt",
                                       qa_, qb_, swm[:, mo:mo + (qb_ - qa_)], "swm"))
                    return chunks

                for c in range(3):
                    att_pair(512, qat, "qat", c, na_chunks_of, 1.0, yaT, "yaT", None)
                att_drain()
                if tb + 1 < 8:
                    load_na(tb + 1)
                elif not last:
                    S.dma("sp", qat[:, :, 0:256], QA[:, SEQ:NT].rearrange("(h d) t -> d h t", d=64), (), ("qat",))
                att["bg"] = conv_ln(512)
                next(att["bg"])
                for c in range(3):
                    att_pair(512, qbt, "qbt", c, sw_chunks_of, 0.125, ybT, "ybT", esk[:, c:c + 1])
                att_drain()
                if att["bg"] is not None:
                    run_all(att["bg"])
                    att["bg"] = None
                conv_silu(512)
                if tb + 1 < 8:
                    load_sw(tb + 1)
                elif not last:
                    S.dma("sp", qbt[:, :, 0:256], QB[:, SEQ:NT].rearrange("(h d) t -> d h t", d=64), (), ("qbt",))
                merge(512, t0, 0)

            if not last:
                load_hcb(256, SEQ, SEQ, NT)
                S.dma("sp", xb[:, :, :256], Xv[:, :, SEQ:NT], (), ("xbm",))
                for c in range(3):
                    att_pair(256, qat, "qat", c, lambda h: ctx_chunks_a(h, 256), 1.0, yaT, "yaT", None)
                att_drain()
                att["bg"] = conv_ln(256)
                next(att["bg"])
                for c in range(3):
                    att_pair(256, qbt, "qbt", c, lambda h: ctx_chunks_b(h // 3, 256), 0.125, ybT, "ybT", esk[:, c:c + 1])
                att_drain()
                if att["bg"] is not None:
                    run_all(att["bg"])
                    att["bg"] = None
                conv_silu(256)
                merge(256, SEQ, 1)
        return finish_phase(f"mix{l}")

    def final_phase():
        with contextlib.ExitStack() as st:
            xbs = [sb(st, f"xb{i}", [128, 8, 512], F32) for i in range(2)]
            sq = sb(st, "sq", [128, 8, 512], BF16)
            rs = sb(st, "rs", [128, 512], F32)
            obf = [sb(st, f"obf{i}", [128, 8, 512], F32) for i in range(2)]
            S.dma("sp", xbs[0][:], Xv[:, :, 0:512], (), (("xb", 0),))
            for bi in range(8):
                t0 = bi * 512
                xb, xbk = xbs[bi % 2], ("xb", bi % 2)
                if bi + 1 < 8:
                    S.dma("sp", xbs[(bi + 1) % 2][:], Xv[:, :, t0 + 512:t0 + 1024], (), (("xb", (bi + 1) % 2),))
                S.act(sq[:], xb[:], AF.Square, (xbk,), ("sq",))
                for kc in range(8):
                    S.mm(ps[6][:, :], ones_b[:], sq[:, kc, :], kc == 0, kc == 7, ("ones_b", "sq"), (PS(6),), kc == 7)
                S.act(rs[:], ps[6][:, :], AF.Sqrt, (PS(6), "eps_s"), ("rs",), bias=eps_s[:, 0:1], scale=1.0 / D)
                S.recip(rs[:], rs[:], ("rs",), ("rs",))
                o, ok = obf[bi % 2], ("obf", bi % 2)
                for kc in range(8):
                    S.stt(o[:, kc, :], xb[:, kc, :], finalg_s[:, kc:kc + 1], rs[:], ALU.mult, ALU.mult,
                          (xbk, "finalg", "rs"), (ok,))
                S.dma("pool", outv[:, :, t0:t0 + 512], o[:], (ok,), ())
        S.barrier()

    def program():
        if done["flag"]:
            return
        for l in range(DEPTH):
            last = l == DEPTH - 1
            if ffn_phase(l, 0, (BLOCKS[1:] + BLOCKS[:1]) if l == 0 else BLOCKS):
                return
            if l == 0:
                ada_stack.close()
                ada_state["closed"] = True
            with contextlib.ExitStack() as lst:
                load_out_weights(l, lst)
                if proj_phase(l):
                    return
                if mix_phase(l, last):
                    return
            if ffn_phase(l, 1, ([(0, 256, 0), (256, 768, 0)] + BLOCKS[2:]) if last else BLOCKS, fuse_final=last):
                return
        pass

    program()
    S.barrier()
    S.emit()
    if not ada_state.get("closed"):
        ada_stack.close()
    stack0.close()
    return nc


_CACHE = {}


def kernel(**inputs):
    shared = prep_shared(inputs)
    if "nc" not in _CACHE:
        _CACHE["nc"] = build()
    nc = _CACHE["nc"]
    in_maps = []
    for b in range(8):
        m = dict(shared)
        m.update(prep_core(inputs, b))
        in_maps.append(m)
    res = run_bass_kernel_spmd(nc, in_maps, core_ids=list(range(8)))
    out = np.stack([np.ascontiguousarray(np.asarray(r["outT"], dtype=np.float32).T) for r in res.results])
    return out
```

```python
import contextlib
import numpy as np
import concourse.bass as bass
import concourse.mybir as mybir
from concourse.bass_utils import run_bass_kernel_spmd

F32 = mybir.dt.float32
BF16 = mybir.dt.bfloat16
AF = mybir.ActivationFunctionType
ALU = mybir.AluOpType

D = 1024
NCH = 8
SEQ = 4096
CTX = 256
NT = SEQ + CTX
DFF = 2816
NFF = 22
DEPTH = 2
EPS = 1e-6
NEG = -30000.0
GRID_W = 64
N_FM = 42
BLOCKS = [(SEQ, CTX, 1)] + [(i * 1024, 1024, 0) for i in range(4)]


class Sched:
    def __init__(self, nc, same_engine_sync=True, ndma=32, nconv=6):
        self.nc = nc
        self.engs = ["pe", "act", "dve", "pool", "sp"]
        self.q = {e: [] for e in self.engs}
        self.sem = {e: nc.alloc_semaphore(f"sem_{e}") for e in ["pe", "act", "dve", "pool"]}
        self.cnt = {e: 0 for e in self.sem}
        self.pending = {e: False for e in self.sem}
        self.seen = {e: {} for e in self.engs}
        self.lastw = {}
        self.readers = {}
        self.same = same_engine_sync
        self.ndma = ndma
        self.dsem = [nc.alloc_semaphore(f"dsem{i}") for i in range(ndma + nconv)]
        self.dval = [0] * (ndma + nconv)
        self.dnext = 0
        self.pnext = 0
        self.cnext = 0
        self.nconv = nconv

    def _semof(self, k):
        return self.sem[k[1]] if k[0] == "e" else self.dsem[k[1]]

    def _wait(self, e, tickets):
        need = {}
        for (k, v) in tickets:
            if k[0] == "e" and k[1] == e and (e == "pe" or not self.same):
                continue
            if v > need.get(k, 0):
                need[k] = v
        for k, v in need.items():
            if self.seen[e].get(k, 0) >= v:
                continue
            self.seen[e][k] = v
            sem = self._semof(k)
            self.q[e].append(lambda eng, sem=sem, v=v: eng.wait_ge(sem, v))

    def _deps(self, reads, writes):
        t = []
        for k in reads:
            if k in self.lastw:
                t.append(self.lastw[k])
        for k in writes:
            if k in self.lastw:
                t.append(self.lastw[k])
            t.extend(self.readers.get(k, {}).items())
        return t

    def _commit(self, tk, reads, writes):
        for k in reads:
            r = self.readers.setdefault(k, {})
            if tk[1] > r.get(tk[0], 0):
                r[tk[0]] = tk[1]
        for k in writes:
            self.lastw[k] = tk
            self.readers[k] = {}

    def op(self, e, fn, reads=(), writes=(), signal=True):
        self._wait(e, self._deps(reads, writes))
        if signal:
            self.cnt[e] += 1
            tk = (("e", e), self.cnt[e])
            sem = self.sem[e]
            self.q[e].append(lambda eng, fn=fn, sem=sem: fn(eng).then_inc(sem, 1))
            self.pending[e] = False
        else:
            tk = (("e", e), self.cnt[e] + 1)
            self.pending[e] = True
            self.q[e].append(lambda eng, fn=fn: fn(eng))
        self._commit(tk, reads, writes)

    def dma(self, e, out, in_, reads=(), writes=(), conv=False, **kw):
        if conv:
            i = self.ndma + self.cnext
            self.cnext = (self.cnext + 1) % self.nconv
        elif e == "pool":
            half = self.ndma // 2
            i = half + self.pnext
            self.pnext = (self.pnext + 1) % (self.ndma - half)
        else:
            i = self.dnext
            self.dnext = (self.dnext + 1) % (self.ndma // 2)
        for k in reads:
            if isinstance(k, tuple) and k and k[0] in ("WGU", "WD", "WINFM", "WINV", "NATB"):
                assert k in self.lastw, f"converted weight {k} read before its conversion DMA was issued"
        deps = self._deps(reads, writes)
        if self.dval[i] > 0:
            deps.append((("d", i), self.dval[i]))
        self._wait(e, deps)
        self.dval[i] += 16
        tk = (("d", i), self.dval[i])
        sem = self.dsem[i]
        self.q[e].append(lambda eng, out=out, in_=in_, sem=sem, kw=kw:
                         eng.dma_start(out=out, in_=in_, **kw).then_inc(sem, 16))
        self._commit(tk, reads, writes)
        return tk

    def barrier(self, engines=None):
        for e in self.sem:
            assert not self.pending[e], f"unsignalled op pending on {e} at barrier"
        tks = [(("e", e), self.cnt[e]) for e in self.sem if self.cnt[e] > 0]
        tks += [(("d", i), self.dval[i]) for i in range(self.ndma) if self.dval[i] > 0]
        for e in (engines or self.engs):
            self.same, old = True, self.same
            self._wait(e, [t for t in tks if not (t[0][0] == "e" and t[0][1] == e)])
            self.same = old

    def mm(self, out, lhsT, rhs, start, stop, reads, writes, signal):
        self.op("pe", lambda eng: eng.matmul(out, lhsT=lhsT, rhs=rhs, start=start, stop=stop),
                reads, writes, signal)

    def act(self, out, in_, func, reads, writes, bias=None, scale=None):
        kw = {}
        if bias is not None:
            kw["bias"] = bias
        if scale is not None:
            kw["scale"] = scale
        self.op("act", lambda eng: eng.activation(out=out, in_=in_, func=func, **kw), reads, writes)

    def tt(self, e, out, in0, in1, op, reads, writes):
        self.op(e, lambda eng: eng.tensor_tensor(out=out, in0=in0, in1=in1, op=op), reads, writes)

    def ts(self, e, out, in0, s1, op0, reads, writes, s2=None, op1=None):
        if op1 is None:
            self.op(e, lambda eng: eng.tensor_scalar(out=out, in0=in0, scalar1=s1, scalar2=None, op0=op0),
                    reads, writes)
        else:
            self.op(e, lambda eng: eng.tensor_scalar(out=out, in0=in0, scalar1=s1, scalar2=s2, op0=op0, op1=op1),
                    reads, writes)

    def stt(self, out, in0, scalar, in1, op0, op1, reads, writes):
        self.op("dve", lambda eng: eng.scalar_tensor_tensor(out=out, in0=in0, scalar=scalar, in1=in1,
                                                            op0=op0, op1=op1), reads, writes)

    def copy(self, e, out, in_, reads, writes):
        self.op(e, lambda eng: eng.tensor_copy(out=out, in_=in_), reads, writes)

    def recip(self, out, in_, reads, writes):
        self.op("dve", lambda eng: eng.reciprocal(out=out, in_=in_), reads, writes)

    def memset(self, e, ap, val, writes):
        self.op(e, lambda eng: eng.memset(ap, val), (), writes)

    def emit(self):
        nc = self.nc
        q = self.q
        with nc.Block() as block:
            @block.sync
            def _(eng):
                for f in q["sp"]:
                    f(eng)

            @block.tensor
            def _(eng):
                for f in q["pe"]:
                    f(eng)

            @block.scalar
            def _(eng):
                for f in q["act"]:
                    f(eng)

            @block.vector
            def _(eng):
                for f in q["dve"]:
                    f(eng)

            @block.gpsimd
            def _(eng):
                for f in q["pool"]:
                    f(eng)


def _kr0(r):
    return min(max(r - 4, 0), 56)


def na_chunks(n):
    lo = min(_kr0(2 * n), _kr0(2 * n + 1))
    hi = max(_kr0(2 * n), _kr0(2 * n + 1)) + 7
    return list(range(lo // 2, hi // 2 + 1))


def _na_tile_index(n, m):
    key = np.arange(128)
    yl, kc = key // 64, key % 64
    qq = np.arange(128)
    rl, qc = qq // 64, qq % 64
    y = 2 * m + yl[:, None]
    r = 2 * n + rl[None, :]
    kr = np.clip(r - 4, 0, 56)
    vrow = (y >= kr) & (y <= kr + 7)
    wc0 = np.clip(qc - 8, 0, 48)[None, :]
    vcol = (kc[:, None] >= wc0) & (kc[:, None] < wc0 + 16)
    dy = np.clip(y - r + 7, 0, 14)
    dx = np.clip(kc[:, None] - qc[None, :] + 15, 0, 30)
    valid = vrow & vcol
    return dy, dx, valid


def na_tile_table():
    table, uniq, sig = {}, [], {}
    for n in range(32):
        for m in na_chunks(n):
            dy, dx, valid = _na_tile_index(n, m)
            s = (np.where(valid, dy * 31 + dx, -1)).astype(np.int16).tobytes()
            if s not in sig:
                sig[s] = len(uniq)
                uniq.append((dy, dx, valid))
            table[(n, m)] = sig[s]
    return table, uniq


NA_TABLE, NA_UNIQ = na_tile_table()
N_TILES = len(NA_UNIQ)
NA_STRIP = 2560


def na_block_plan(tb):
    ns = [4 * tb + i for i in range(4)]
    mlo = min(min(na_chunks(n)) for n in ns)
    mhi = max(max(na_chunks(n)) for n in ns)
    plan, off = [], 0
    for m in range(mlo, mhi + 1):
        nm = [n for n in ns if m in na_chunks(n)]
        assert nm == list(range(nm[0], nm[-1] + 1))
        plan.append((m, nm[0], nm[-1], off))
        off += 128 * len(nm)
    assert off <= NA_STRIP
    return plan


def na_cfg(tb):
    return 0 if tb == 0 else (2 if tb == 7 else 1)


def _swap_cols(w, nheads):
    k = w.shape[0]
    w4 = w.reshape(k, nheads, 2, 32)
    return w4[:, :, ::-1, :].reshape(k, nheads * 64)


def _chunks_lhsT(w):
    k, c = w.shape
    return np.ascontiguousarray(w.reshape(k // 128, 128, c // 128, 128).transpose(2, 1, 0, 3))


def _rows_lhsT(w):
    k, m = w.shape
    return np.ascontiguousarray(w.reshape(k // 128, 128, m).transpose(1, 0, 2))


def _vec_cols(v):
    return np.ascontiguousarray(v.reshape(-1, 128).T)


def rope_tables():
    t = np.arange(SEQ)
    row = (t // GRID_W).astype(np.float32)
    col = (t % GRID_W).astype(np.float32)
    n_freq = 16
    inv_freq = (np.float32(10000.0) ** (-np.arange(n_freq, dtype=np.float32) / np.float32(n_freq))).astype(np.float32)
    ang = np.concatenate([row[:, None] * inv_freq, col[:, None] * inv_freq], axis=-1).astype(np.float32)
    cos, sin = np.cos(ang).astype(np.float32), np.sin(ang).astype(np.float32)
    cosT = np.concatenate([cos.T, cos.T, cos.T, cos.T], axis=0)
    sinT = np.concatenate([-sin.T, sin.T, -sin.T, sin.T], axis=0)
    return np.ascontiguousarray(cosT), np.ascontiguousarray(sinT)


def prep_shared(inp):
    f = lambda a: np.ascontiguousarray(np.asarray(a, dtype=np.float32))
    sh = {}
    wa = f(inp["w_ada"])
    sh["wada"] = np.ascontiguousarray(wa.reshape(DEPTH, 8, 128, 36, 256).transpose(0, 3, 2, 1, 4)).reshape(DEPTH, 36, 128, 2048)
    sh["bada"] = np.stack([_vec_cols(f(inp["b_ada"])[l]) for l in range(DEPTH)])
    sh["normg"] = np.stack([np.stack([_vec_cols(f(inp["norm_g"])[l, s]) for s in range(3)]) for l in range(DEPTH)])
    sh["finalg"] = _vec_cols(f(inp["final_g"]))
    for nm, src in (("wgu1", "w_ffn1_gu"), ("wgu2", "w_ffn2_gu")):
        w = f(inp[src])
        outl = []
        for l in range(DEPTH):
            a = _chunks_lhsT(w[l][:, :DFF])
            b = _chunks_lhsT(w[l][:, DFF:])
            outl.append(np.stack([a, b], axis=2).reshape(NFF, 128, 2 * 8 * 128))
        sh[nm] = np.ascontiguousarray(np.stack(outl))
    for nm, src in (("wd1", "w_ffn1_down"), ("wd2", "w_ffn2_down")):
        w = f(inp[src])
        outl = []
        for l in range(DEPTH):
            r = _rows_lhsT(w[l])
            outl.append(np.stack([r[:, :, dc * 128:(dc + 1) * 128].reshape(128, NFF * 128) for dc in range(8)]))
        sh[nm] = np.ascontiguousarray(np.stack(outl))
    w_in = f(inp["w_in"])
    fm, vv = [], []
    for l in range(DEPTH):
        w = w_in[l]
        qa, ka, va = w[:, 0:384], w[:, 384:768], w[:, 768:1152]
        qb, kb, vb = w[:, 1152:1536], w[:, 1536:1664], w[:, 1664:1792]
        u, g = w[:, 1792:2304], w[:, 2304:5376]
        qbs, kbs = _swap_cols(qb, 6), _swap_cols(kb, 2)
        cols = [qa, ka]
        for c in range(3):
            cols += [qb[:, c * 128:(c + 1) * 128], qbs[:, c * 128:(c + 1) * 128]]
        cols += [kb, kbs]
        for c in range(2):
            cols += [u[:, c * 128:(c + 1) * 128], u[:, 256 + c * 128:256 + (c + 1) * 128]]
        cols += [g]
        allc = np.concatenate(cols, axis=1)
        assert allc.shape[1] == N_FM * 128
        fm.append(_chunks_lhsT(allc).reshape(N_FM, 128, 1024))
        vv.append(_rows_lhsT(np.concatenate([va, vb], axis=1)).reshape(128, 8 * 512))
    sh["winfm"] = np.ascontiguousarray(np.stack(fm))
    sh["winv"] = np.ascontiguousarray(np.stack(vv))
    sh["bgate"] = np.stack([_vec_cols(f(inp["b_gate"])[l]) for l in range(DEPTH)])
    rpb = f(inp["na_rpb"])
    strips = np.zeros((DEPTH, 3, 128, 6, NA_STRIP), np.float32)
    for l in range(DEPTH):
        for cfg, tb in ((0, 0), (1, 1), (2, 7)):
            for (m, n0_, n1_, off) in na_block_plan(tb):
                for n in range(n0_, n1_ + 1):
                    dy, dx, valid = NA_UNIQ[NA_TABLE[(n, m)]]
                    g = rpb[l][:, dy, dx]
                    g = np.where(valid[None], g, np.float32(NEG))
                    o = off + (n - n0_) * 128
                    strips[l, cfg, :, :, o:o + 128] = g.transpose(1, 0, 2)
    sh["natile"] = strips.reshape(DEPTH, 3, 128, 6 * NA_STRIP)
    j = np.arange(128)[:, None]
    i = np.arange(128)[None, :]
    swm = np.concatenate([np.where(j <= i, 0.0, NEG), np.zeros((128, 128)), np.where(i <= j, 0.0, NEG)], axis=1).astype(np.float32)
    sh["swmask"] = np.ascontiguousarray(swm)
    sink = f(inp["sw_sink"])
    sk = np.empty((DEPTH, 128, 3), np.float32)
    for l in range(DEPTH):
        for c in range(3):
            sk[l, :64, c] = sink[l, 2 * c]
            sk[l, 64:, c] = sink[l, 2 * c + 1]
    sh["sinkp"] = sk
    cw = f(inp["conv_dw_w"])
    sh["convw"] = np.ascontiguousarray(np.stack([cw[l].T.reshape(2, 128, 31).transpose(1, 0, 2) for l in range(DEPTH)]))
    sh["convb"] = np.stack([_vec_cols(f(inp["conv_dw_b"])[l]) for l in range(DEPTH)])
    sh["lng"] = np.stack([_vec_cols(f(inp["conv_ln_g"])[l]) for l in range(DEPTH)])
    sh["lnb"] = np.stack([_vec_cols(f(inp["conv_ln_b"])[l]) for l in range(DEPTH)])
    sh["woa"] = np.stack([_rows_lhsT(f(inp["w_out_a"])[l]).reshape(128, 3 * 1024) for l in range(DEPTH)])
    sh["wob"] = np.stack([_rows_lhsT(f(inp["w_out_b"])[l]).reshape(128, 3 * 1024) for l in range(DEPTH)])
    sh["woc"] = np.stack([_rows_lhsT(f(inp["w_out_c"])[l]).reshape(128, 2 * 1024) for l in range(DEPTH)])
    sh["wo"] = np.stack([_rows_lhsT(f(inp["w_out"])[l]).reshape(128, 8 * 1024) for l in range(DEPTH)])
    cosT, sinT = rope_tables()
    sh["costab"], sh["sintab"] = cosT, sinT
    sh["ident"] = np.eye(128, dtype=np.float32)
    return {k: np.ascontiguousarray(v, dtype=np.float32) for k, v in sh.items()}


def prep_core(inp, b):
    x = np.asarray(inp["x"][b], dtype=np.float32)
    ctx = np.asarray(inp["ctx"][b], dtype=np.float32)
    c = np.asarray(inp["c"][b], dtype=np.float32)
    cc = np.asarray(inp["c_ctx"], dtype=np.float32)
    return {
        "xT": np.ascontiguousarray(x.T),
        "ctxT": np.ascontiguousarray(ctx.T),
        "cc": np.ascontiguousarray(np.stack([_vec_cols(c), _vec_cols(cc)], axis=2)),
    }


def build(stop_after=None, dumps=(), same_engine_sync=True):
    nc = bass.Bass("TRN2", target_bir_lowering=False)
    S = Sched(nc, same_engine_sync=same_engine_sync)

    def din(name, shape):
        return nc.dram_tensor(name, list(shape), F32, kind="ExternalInput").ap()

    def dscr(name, shape, dt):
        kind = "ExternalOutput" if name in dumps else "Internal"
        return nc.dram_tensor(name, list(shape), dt, kind=kind).ap()

    xT = din("xT", [D, SEQ]); ctxT = din("ctxT", [D, CTX]); cc = din("cc", [128, 8, 2])
    wada = din("wada", [DEPTH, 36, 128, 2048]); bada = din("bada", [DEPTH, 128, 72])
    normg = din("normg", [DEPTH, 3, 128, 8]); finalg = din("finalg", [128, 8])
    wgu_in = [din("wgu1", [DEPTH, NFF, 128, 2048]), din("wgu2", [DEPTH, NFF, 128, 2048])]
    wd_in = [din("wd1", [DEPTH, 8, 128, NFF * 128]), din("wd2", [DEPTH, 8, 128, NFF * 128])]
    winfm_in = din("winfm", [DEPTH, N_FM, 128, 1024]); winv_in = din("winv", [DEPTH, 128, 4096])
    bgate = din("bgate", [DEPTH, 128, 24])
    natile = din("natile", [DEPTH, 3, 128, 6 * NA_STRIP]); swmask = din("swmask", [128, 384])
    sinkp = din("sinkp", [DEPTH, 128, 3])
    convw = din("convw", [DEPTH, 128, 2, 31]); convb = din("convb", [DEPTH, 128, 2])
    lng = din("lng", [DEPTH, 128, 2]); lnb = din("lnb", [DEPTH, 128, 2])
    woa_in = din("woa", [DEPTH, 128, 3072]); wob_in = din("wob", [DEPTH, 128, 3072])
    woc_in = din("woc", [DEPTH, 128, 2048]); wo_in = din("wo", [DEPTH, 128, 8192])
    costab = din("costab", [128, SEQ]); sintab = din("sintab", [128, SEQ]); ident_in = din("ident", [128, 128])
    outT = nc.dram_tensor("outT", [D, SEQ], F32, kind="ExternalOutput").ap()

    X = dscr("X", [D, NT], F32)
    WGU = [[dscr(f"WGU{l}{f}", [NFF, 128, 2048], BF16) for f in range(2)] for l in range(DEPTH)]
    WD = [[dscr(f"WD{l}{f}", [8, 128, NFF * 128], BF16) for f in range(2)] for l in range(DEPTH)]
    WINFM = [dscr(f"WINFM{l}", [N_FM, 128, 1024], BF16) for l in range(DEPTH)]
    WINV = [dscr(f"WINV{l}", [128, 4096], BF16) for l in range(DEPTH)]
    NATB = [dscr(f"NATB{l}", [3, 128, 6 * NA_STRIP], BF16) for l in range(DEPTH)]
    QA = dscr("QA", [384, NT], BF16); KA = dscr("KA", [384, NT], BF16); VA = dscr("VA", [NT, 384], BF16)
    QB = dscr("QB", [384, NT], BF16); KB = dscr("KB", [128, NT], BF16); VB = dscr("VB", [NT, 128], BF16)
    HC = dscr("HC", [256, NT], BF16); G = dscr("G", [3072, NT], BF16)

    Xv = X.rearrange("(c p) t -> p c t", p=128)
    outv = outT.rearrange("(c p) t -> p c t", p=128)

    stack0 = contextlib.ExitStack()

    name_ctr = {"n": 0}

    def sb(stack, name, shape, dt):
        name_ctr["n"] += 1
        return stack.enter_context(nc.sbuf_tensor(f"{name}_{name_ctr['n']}", list(shape), dt))

    ident_f = sb(stack0, "ident_f", [128, 128], F32)
    ident_b = sb(stack0, "ident_b", [128, 128], BF16)
    ones_b = sb(stack0, "ones_b", [128, 128], BF16)
    ones_f = sb(stack0, "ones_f", [128, 128], F32)
    mods = [sb(stack0, f"mods{l}", [128, 72, 2], F32) for l in range(DEPTH)]
    Aeff = [sb(stack0, f"Aeff{l}", [128, 3, 8, 2], F32) for l in range(DEPTH)]
    Gate = [sb(stack0, f"Gate{l}", [128, 3, 8, 2], F32) for l in range(DEPTH)]
    normg_s = sb(stack0, "normg_s", [128, DEPTH * 3, 8], F32)
    finalg_s = sb(stack0, "finalg_s", [128, 8], F32)
    bgate_s = sb(stack0, "bgate_s", [128, DEPTH, 24], F32)
    ps = [stack0.enter_context(nc.psum_tensor(f"ps{i}", [128, 512], F32)) for i in range(8)]
    PS = lambda i: ("ps", i)

    done = {"flag": False}

    def finish_phase(name):
        S.barrier()
        if stop_after == name:
            done["flag"] = True
        return done["flag"]

    conv_list = []

    def conv_dma(dst, src, key):
        conv_list.append((dst, src, key))

    def pump_conv(n):
        for _ in range(n):
            if not conv_list:
                return
            dst, src, key = conv_list.pop(0)
            S.dma("pool", dst, src, (), (key,), conv=True, max_dma_last_dim=4096)

    def pump_until(key):
        while conv_list and key not in S.lastw:
            pump_conv(1)
        assert key in S.lastw

    def issue_conversion(l):
        for j in range(NFF):
            conv_dma(WGU[l][0][j], wgu_in[0][l, j], ("WGU", l, 0, j))
        for dc in range(8):
            conv_dma(WD[l][0][dc], wd_in[0][l, dc], ("WD", l, 0, dc))
        for ci in range(N_FM):
            conv_dma(WINFM[l][ci], winfm_in[l, ci], ("WINFM", l, ci))
        conv_dma(WINV[l], winv_in[l], ("WINV", l))
        for cfg in range(3):
            for h in range(6):
                conv_dma(NATB[l][cfg, :, h * NA_STRIP:(h + 1) * NA_STRIP], natile[l, cfg, :, h * NA_STRIP:(h + 1) * NA_STRIP],
                         ("NATB", l, cfg, h))
        for j in range(NFF):
            conv_dma(WGU[l][1][j], wgu_in[1][l, j], ("WGU", l, 1, j))
        for dc in range(8):
            conv_dma(WD[l][1][dc], wd_in[1][l, dc], ("WD", l, 1, dc))

    eps_s = sb(stack0, "eps_s", [128, 1], F32)
    S.memset("dve", eps_s[:], EPS, ("eps_s",))

    ada_stack = contextlib.ExitStack()
    cc_s = sb(ada_stack, "cc_s", [128, 8, 2], F32)
    sc_b = sb(ada_stack, "sc_b", [128, 8, 2], BF16)
    bada_s = sb(ada_stack, "bada_s", [128, DEPTH, 72], F32)
    wab = [sb(ada_stack, f"wab{i}", [128, 8, 256], BF16) for i in range(2)]
    ada_state = {"k": 0}

    def ada_finalize(l, s_):
        pv = ps[7][:, l * 144:(l + 1) * 144].rearrange("p (j t) -> p j t", t=2)
        j0, j1 = 24 * s_, 24 * (s_ + 1)
        for col in range(2):
            S.tt("dve", mods[l][:, j0:j1, col], pv[:, j0:j1, col], bada_s[:, l, j0:j1], ALU.add,
                 (PS(7), "bada"), (("mods", l, s_),))
        for col in range(2):
            S.stt(Aeff[l][:, s_, :, col], mods[l][:, (3 * s_ + 1) * 8:(3 * s_ + 2) * 8, col], 1.0,
                  normg_s[:, l * 3 + s_, :], ALU.add, ALU.mult, (("mods", l, s_), "normg"), (("Aeff", l, s_),))
            S.ts("dve", Gate[l][:, s_, :, col], mods[l][:, (3 * s_ + 2) * 8:(3 * s_ + 3) * 8, col],
                 0.5 if s_ != 1 else 1.0, ALU.mult, (("mods", l, s_),), (("Gate", l, s_),))

    def ada_group(burst=None):
        k = ada_state["k"]
        if k >= 72:
            return False
        ada_state["k"] += 1
        l, g = k // 36, k % 36
        if burst is not None:
            wb, wbk = burst[k], ("wab_burst", k)
        else:
            wb, wbk = wab[k % 2], ("wab", k % 2)
        S.dma("pool", wb[:], wada[l, g].rearrange("p (kc m) -> p kc m", kc=8), (), (wbk,), max_dma_last_dim=4096)
        for oc in range(2):
            j = g * 2 + oc
            c0 = l * 144 + 2 * j
            for kc in range(8):
                S.mm(ps[7][:, c0:c0 + 2], wb[:, kc, oc * 128:(oc + 1) * 128], sc_b[:, kc, :],
                     kc == 0, kc == 7, (wbk, "sc_b"), (PS(7),), kc == 7)
        if g % 12 == 11:
            ada_finalize(l, g // 12)
        return True

    with contextlib.ExitStack() as st:
        S.dma("sp", ident_f[:], ident_in[:, :], (), ("ident_f",))
        S.copy("dve", ident_b[:], ident_f[:], ("ident_f",), ("ident_b",))
        S.memset("dve", ones_b[:], 1.0, ("ones_b",))
        S.memset("dve", ones_f[:], 1.0, ("ones_f",))
        S.dma("sp", normg_s[:], normg.rearrange("l s p c -> p (l s) c"), (), ("normg",))
        S.dma("sp", finalg_s[:], finalg[:, :], (), ("finalg",))
        S.dma("sp", bgate_s[:], bgate.rearrange("l p g -> p l g"), (), ("bgate",))
        for l in range(DEPTH):
            issue_conversion(l)
        S.dma("sp", cc_s[:], cc[:, :, :], (), ("cc_s",))
        S.dma("sp", bada_s[:], bada.rearrange("l p j -> p l j"), (), ("bada",))
        S.act(sc_b[:], cc_s[:], AF.Silu, ("cc_s",), ("sc_b",))
        burst = [sb(st, f"wabb{i}", [128, 8, 256], BF16) for i in range(12)]
        for _ in range(12):
            ada_group(burst)
        pump_conv(22)
        if finish_phase("ada"):
            pass

    def halves(T):
        return [(h0, min(512, T - h0)) for h0 in range(0, T, 512)]

    def norm_steps(l, s, col, xb, xbk, T, hT, hTk, sq, rs, tmp):
        xall = tuple((xbk[0], xbk[1], kc) for kc in range(8))
        for (h0, hl) in halves(T):
            S.act(sq[:, :, :hl], xb[:, :, h0:h0 + hl], AF.Square, xall, ("sq",))
            yield
            for kc in range(8):
                S.mm(ps[6][:, :hl], ones_b[:], sq[:, kc, :hl], kc == 0, kc == 7, ("ones_b", "sq"), (PS(6),), kc == 7)
            S.act(rs[:, h0:h0 + hl], ps[6][:, :hl], AF.Sqrt, (PS(6), "eps_s"), ("rs",), bias=eps_s[:, 0:1], scale=1.0 / D)
            yield
        S.recip(rs[:, :T], rs[:, :T], ("rs",), ("rs",))
        yield
        for kc in range(8):
            tb_ = tmp[kc % 2]
            S.tt("dve", tb_[:, :T], xb[:, kc, :T], rs[:, :T], ALU.mult, ((xbk[0], xbk[1], kc), "rs"), (("tmp", kc % 2),))
            S.act(hT[:, kc, :T], tb_[:, :T], AF.Identity, (("tmp", kc % 2), ("Aeff", l, s), ("mods", l, s)), (hTk,),
                  bias=mods[l][:, 3 * s * 8 + kc, col:col + 1], scale=Aeff[l][:, s, kc, col:col + 1])
            yield

    def run_all(gen):
        for _ in gen:
            pass


    def ffn_phase(l, f, blocks, fuse_final=False):
        s = 0 if f == 0 else 2
        pump_until(("WD", l, f, 7) if (l, f) != (0, 0) else ("WGU", 0, 0, NFF - 1))
        with contextlib.ExitStack() as st:
            xbs = [sb(st, f"xb{i}", [128, 8, 1024], F32) for i in range(2)]
            hTs = [sb(st, f"hT{i}", [128, 8, 1024], BF16) for i in range(2)]
            sq = sb(st, "sq", [128, 8, 512], BF16)
            rs = sb(st, "rs", [128, 1024], F32)
            tmp = [sb(st, f"tmp{i}", [128, 1024], F32) for i in range(2)]
            gT = sb(st, "gT", [128, NFF, 1024], BF16)
            NWG = 4
            NWD = 2 if (l, f) == (0, 0) else 3
            wgb = [sb(st, f"wgb{i}", [128, 2, 8, 128], BF16) for i in range(NWG)]
            wdb = [sb(st, f"wdb{i}", [128, NFF, 128], BF16) for i in range(NWD)]
            sa = [sb(st, f"sa{i}", [128, 512], F32) for i in range(2)]

            xTv = xT.rearrange("(c p) t -> p c t", p=128)
            ctxTv = ctxT.rearrange("(c p) t -> p c t", p=128)

            def load_x(bi, kcs=range(8)):
                t0, T, col = blocks[bi]
                for kc in kcs:
                    if (l, f) == (0, 0):
                        src = ctxTv[:, kc, :] if col == 1 else xTv[:, kc, t0:t0 + T]
                    else:
                        src = Xv[:, kc, t0:t0 + T]
                    S.dma("sp", xbs[bi % 2][:, kc, :T], src, (), (("xb", bi % 2, kc),))

            def norm_gen(bi):
                t0, T, col = blocks[bi]
                return norm_steps(l, s, col, xbs[bi % 2], ("xb", bi % 2), T, hTs[bi % 2], ("hT", bi % 2), sq, rs, tmp)

            load_x(0)
            run_all(norm_gen(0))
            u = 0
            itn = {"n": 0}
            pending_final = []
            for bi, (t0, T, col) in enumerate(blocks):
                xb, xbk = xbs[bi % 2], ("xb", bi % 2)
                hT, hTk = hTs[bi % 2], ("hT", bi % 2)
                nxt = None
                for j in range(NFF):
                    if 2 <= j < 10 and bi + 1 < len(blocks):
                        load_x(bi + 1, [j - 2])
                        if j == 9:
                            nxt = norm_gen(bi + 1)
                    wb, wbk = wgb[j % NWG], ("wgb", j % NWG)
                    S.dma("sp", wb[:], WGU[l][f][j].rearrange("p (a k m) -> p a k m", a=2, k=8),
                          (("WGU", l, f, j),), (wbk,))
                    for (h0, hl) in halves(T):
                        ia, ib = (u % 2) * 2, (u % 2) * 2 + 1
                        u += 1
                        for kc in range(8):
                            S.mm(ps[ia][:, :hl], wb[:, 0, kc, :], hT[:, kc, h0:h0 + hl], kc == 0, kc == 7, (wbk, hTk),
                                 (PS(ia),), False)
                        for kc in range(8):
                            S.mm(ps[ib][:, :hl], wb[:, 1, kc, :], hT[:, kc, h0:h0 + hl], kc == 0, kc == 7, (wbk, hTk),
                                 (PS(ib),), kc == 7)
                        S.act(sa[u % 2][:, :hl], ps[ia][:, :hl], AF.Silu, (PS(ia),), (("sa", u % 2),))
                        S.tt("dve", gT[:, j, h0:h0 + hl], sa[u % 2][:, :hl], ps[ib][:, :hl], ALU.mult,
                             (("sa", u % 2), PS(ib)), (("gT", j),))
                    if nxt is not None and j >= 10:
                        next(nxt, None)
                    if j == 0 and pending_final:
                        pending_final.pop(0)()
                    itn["n"] += 1
                    if (l, f) == (0, 0):
                        if itn["n"] > 5 and ada_state["k"] < 72:
                            ada_group()
                            if 10 <= itn["n"] < 18:
                                pump_conv(1)
                        elif ada_state["k"] >= 72:
                            pump_conv(1)
                    elif itn["n"] % 4 == 0:
                        pump_conv(1)
                if nxt is not None:
                    run_all(nxt)
                v = 0
                for dc in range(8):
                    wb, wbk = wdb[dc % NWD], ("wdb", dc % NWD)
                    S.dma("sp", wb[:], WD[l][f][dc].rearrange("p (j m) -> p j m", j=NFF),
                          (("WD", l, f, dc),), (wbk,))
                    for (h0, hl) in halves(T):
                        io = 4 + v % 2
                        v += 1
                        for j in range(NFF):
                            S.mm(ps[io][:, :hl], wb[:, j, :], gT[:, j, h0:h0 + hl], j == 0, j == NFF - 1, (wbk, ("gT", j)),
                                 (PS(io),), j == NFF - 1)
                        S.stt(xb[:, dc, h0:h0 + hl], ps[io][:, :hl], Gate[l][:, s, dc, col:col + 1], xb[:, dc, h0:h0 + hl],
                              ALU.mult, ALU.add, (PS(io), ("Gate", l, s), (xbk[0], xbk[1], dc)), ((xbk[0], xbk[1], dc),))
                    if not fuse_final:
                        S.dma("pool", Xv[:, dc, t0:t0 + T], xb[:, dc, :T], ((xbk[0], xbk[1], dc),), ())
                if fuse_final:
                    def final_norm(xb=xb, xbk=xbk, t0=t0, T=T):
                        xall = tuple((xbk[0], xbk[1], kc) for kc in range(8))
                        for (h0, hl) in halves(T):
                            S.act(sq[:, :, :hl], xb[:, :, h0:h0 + hl], AF.Square, xall, ("sq",))
                            for kc in range(8):
                                S.mm(ps[6][:, :hl], ones_b[:], sq[:, kc, :hl], kc == 0, kc == 7, ("ones_b", "sq"), (PS(6),), kc == 7)
                            S.act(rs[:, h0:h0 + hl], ps[6][:, :hl], AF.Sqrt, (PS(6), "eps_s"), ("rs",), bias=eps_s[:, 0:1], scale=1.0 / D)
                        S.recip(rs[:, :T], rs[:, :T], ("rs",), ("rs",))
                        for kc in range(8):
                            S.stt(xb[:, kc, :T], xb[:, kc, :T], finalg_s[:, kc:kc + 1], rs[:, :T], ALU.mult, ALU.mult,
                                  ((xbk[0], xbk[1], kc), "finalg", "rs"), ((xbk[0], xbk[1], kc),))
                            S.dma("pool", outv[:, kc, t0:t0 + T], xb[:, kc, :T], ((xbk[0], xbk[1], kc),), ())
                    if bi + 1 < len(blocks):
                        pending_final.append(final_norm)
                    else:
                        final_norm()
            if (l, f) == (0, 0):
                while ada_group():
                    pass
        return finish_phase(f"ffn{l}{f}")

    res = {}

    def load_out_weights(l, stack):
        res["woa"] = sb(stack, "woa", [128, 3, 1024], BF16); res["wob"] = sb(stack, "wob", [128, 3, 1024], BF16)
        res["woc"] = sb(stack, "woc", [128, 2, 1024], BF16); res["wo"] = sb(stack, "wo", [128, 8, 1024], BF16)

    def out_weight_loads(l):
        woa, wob, woc, wo = res["woa"], res["wob"], res["woc"], res["wo"]
        todo = []
        for k in range(3):
            todo.append((woa[:, k, :], woa_in[l][:, k * 1024:(k + 1) * 1024], "woa"))
            todo.append((wob[:, k, :], wob_in[l][:, k * 1024:(k + 1) * 1024], "wob"))
        for k in range(2):
            todo.append((woc[:, k, :], woc_in[l][:, k * 1024:(k + 1) * 1024], "woc"))
        for k in range(8):
            todo.append((wo[:, k, :], wo_in[l][:, k * 1024:(k + 1) * 1024], "wo"))
        return [lambda d=d, s_=s_, k_=k_: S.dma("pool", d, s_, (), (k_,), max_dma_last_dim=4096) for (d, s_, k_) in todo]

    def proj_phase(l):
        pump_until(("WINV", l))
        with contextlib.ExitStack() as st:
            xbs = [sb(st, f"xb{i}", [128, 8, 1024], F32) for i in range(2)]
            hTs = [sb(st, f"hT{i}", [128, 8, 1024], BF16) for i in range(2)]
            sq = sb(st, "sq", [128, 8, 512], BF16)
            rs = sb(st, "rs", [128, 1024], F32)
            tmp = [sb(st, f"tmp{i}", [128, 1024], F32) for i in range(2)]
            wv = sb(st, "wv", [128, 8, 512], BF16)
            wfb = [sb(st, f"wfb{i}", [128, 8, 128], BF16) for i in range(6)]
            ob = [sb(st, f"ob{i}", [128, 1024], BF16) for i in range(4)]
            cosb = sb(st, "cosb", [128, 1024], F32)
            sinb = sb(st, "sinb", [128, 1024], F32)
            t1 = [sb(st, f"t1{i}", [128, 512], F32) for i in range(2)]
            t2 = [sb(st, f"t2{i}", [128, 512], F32) for i in range(2)]
            vst = [sb(st, f"vst{i}", [128, 512], BF16) for i in range(2)]
            cnt = {"w": 0, "o": 0, "p": 0, "t": 0}
            blocks = BLOCKS

            def load_x(bi, kcs=range(8)):
                t0, T, col = blocks[bi]
                for kc in kcs:
                    S.dma("sp", xbs[bi % 2][:, kc, :T], Xv[:, kc, t0:t0 + T], (), (("xb", bi % 2, kc),))

            def norm_gen(bi):
                t0, T, col = blocks[bi]
                return norm_steps(l, 1, col, xbs[bi % 2], ("xb", bi % 2), T, hTs[bi % 2], ("hT", bi % 2), sq, rs, tmp)

            def load_w(ci):
                wi = cnt["w"] % 6
                cnt["w"] += 1
                wb, wbk = wfb[wi], ("wfb", wi)
                S.dma("sp", wb[:], WINFM[l][ci].rearrange("p (k m) -> p k m", k=8), (("WINFM", l, ci),), (wbk,))
                return wb, wbk

            def new_ob():
                oi = cnt["o"] % 4
                cnt["o"] += 1
                return ob[oi], ("ob", oi)

            load_x(0)
            run_all(norm_gen(0))
            owl = []
            for bi, (t0, T, col) in enumerate(blocks):
                hT, hTk = hTs[bi % 2], ("hT", bi % 2)
                nxt = None
                latent = col == 0
                if bi == 1:
                    owl = out_weight_loads(l)
                if latent:
                    S.dma("sp", cosb[:, :T], costab[:, t0:t0 + T], (), ("cosb",))
                    S.dma("sp", sinb[:, :T], sintab[:, t0:t0 + T], (), ("sinb",))

                def fm(w, h0, hl):
                    wb, wbk = w
                    pi = cnt["p"] % 4
                    cnt["p"] += 1
                    for kc in range(8):
                        S.mm(ps[pi][:, :hl], wb[:, kc, :], hT[:, kc, h0:h0 + hl], kc == 0, kc == 7, (wbk, hTk), (PS(pi),), kc == 7)
                    return pi


                ci = 0
                for c in range(3):
                    w = load_w(ci); ci += 1
                    o, ok = new_ob()
                    for (h0, hl) in halves(T):
                        pi = fm(w, h0, hl)
                        S.act(o[:, h0:h0 + hl], ps[pi][:, :hl], AF.Copy, (PS(pi),), (ok,), scale=0.125)
                    S.dma("pool", QA[c * 128:(c + 1) * 128, t0:t0 + T], o[:, :T], (ok,), ())
                for c in range(3):
                    w = load_w(ci); ci += 1
                    o, ok = new_ob()
                    for (h0, hl) in halves(T):
                        pi = fm(w, h0, hl)
                        S.copy("dve", o[:, h0:h0 + hl], ps[pi][:, :hl], (PS(pi),), (ok,))
                    S.dma("pool", KA[c * 128:(c + 1) * 128, t0:t0 + T], o[:, :T], (ok,), ())
                for c in range(4):
                    dst = QB[c * 128:(c + 1) * 128, t0:t0 + T] if c < 3 else KB[:, t0:t0 + T]
                    w = load_w(ci); ci += 1
                    wsw = load_w(ci) if latent else None
                    ci += 1
                    o, ok = new_ob()
                    for (h0, hl) in halves(T):
                        pq = fm(w, h0, hl)
                        if latent:
                            psw = fm(wsw, h0, hl)
                            ti = cnt["t"] % 2
                            cnt["t"] += 1
                            S.tt("dve", t1[ti][:, :hl], ps[pq][:, :hl], cosb[:, h0:h0 + hl], ALU.mult, (PS(pq), "cosb"), (("t1", ti),))
                            S.tt("dve", t2[ti][:, :hl], ps[psw][:, :hl], sinb[:, h0:h0 + hl], ALU.mult, (PS(psw), "sinb"), (("t2", ti),))
                            S.tt("pool", o[:, h0:h0 + hl], t1[ti][:, :hl], t2[ti][:, :hl], ALU.add, (("t1", ti), ("t2", ti)), (ok,))
                        else:
                            S.copy("dve", o[:, h0:h0 + hl], ps[pq][:, :hl], (PS(pq),), (ok,))
                    S.dma("pool", dst, o[:, :T], (ok,), ())
                for c in range(2):
                    wa_ = load_w(ci); ci += 1
                    wg_ = load_w(ci); ci += 1
                    o, ok = new_ob()
                    for (h0, hl) in halves(T):
                        pa_ = fm(wa_, h0, hl)
                        pg_ = fm(wg_, h0, hl)
                        ti = cnt["t"] % 2
                        cnt["t"] += 1
                        S.act(t1[ti][:, :hl], ps[pg_][:, :hl], AF.Sigmoid, (PS(pg_),), (("t1", ti),))
                        S.tt("dve", o[:, h0:h0 + hl], t1[ti][:, :hl], ps[pa_][:, :hl], ALU.mult, (("t1", ti), PS(pa_)), (ok,))
                    S.dma("pool", HC[c * 128:(c + 1) * 128, t0:t0 + T], o[:, :T], (ok,), ())
                for gi in range(24):
                    if gi == 0 and bi == 0:
                        S.dma("sp", wv[:], WINV[l].rearrange("p (k m) -> p k m", k=8), (("WINV", l),), ("wv",))
                    if gi < 8 and bi + 1 < len(blocks):
                        load_x(bi + 1, [gi])
                        if gi == 7:
                            nxt = norm_gen(bi + 1)
                    if gi % 4 == 0:
                        pump_conv(1)
                    if gi % 4 == 2 and owl:
                        owl.pop(0)()
                    w = load_w(ci); ci += 1
                    o, ok = new_ob()
                    for (h0, hl) in halves(T):
                        pi = fm(w, h0, hl)
                        S.act(o[:, h0:h0 + hl], ps[pi][:, :hl], AF.Sigmoid, (PS(pi), "bgate"), (ok,), bias=bgate_s[:, l, gi:gi + 1])
                    S.dma("pool", G[gi * 128:(gi + 1) * 128, t0:t0 + T], o[:, :T], (ok,), ())
                    if gi >= 8 and nxt is not None:
                        next(nxt, None)
                assert ci == N_FM
                for tt_ in range(T // 128):
                    pi = 4 + tt_ % 2
                    for kc in range(8):
                        S.mm(ps[pi][:, :], hT[:, kc, tt_ * 128:(tt_ + 1) * 128], wv[:, kc, :], kc == 0, kc == 7,
                             (hTk, "wv"), (PS(pi),), kc == 7)
                    v, vk = vst[tt_ % 2], ("vst", tt_ % 2)
                    S.copy("dve", v[:], ps[pi][:, :], (PS(pi),), (vk,))
                    r0 = t0 + tt_ * 128
                    S.dma("pool", VA[r0:r0 + 128, :], v[:, 0:384], (vk,), ())
                    S.dma("pool", VB[r0:r0 + 128, :], v[:, 384:512], (vk,), ())
                    if nxt is not None:
                        next(nxt, None)
                if nxt is not None:
                    run_all(nxt)
            while owl:
                owl.pop(0)()
        return finish_phase(f"proj{l}")

    def mix_phase(l, last):
        pump_until(("NATB", l, 2, 5))
        with contextlib.ExitStack() as st:
            woa, wob, woc, wo = res["woa"], res["wob"], res["woc"], res["wo"]
            nat = sb(st, "nat", [128, 6, NA_STRIP], BF16)
            swm = sb(st, "swm", [128, 384], BF16)
            esk = sb(st, "esk", [128, 3], F32)
            cw = sb(st, "cw", [128, 2, 31], F32); cb = sb(st, "cb", [128, 2], F32)
            lg = sb(st, "lg", [128, 2], F32); lb = sb(st, "lb", [128, 2], F32)
            diag = sb(st, "diag", [128, 2, 31, 128], BF16)
            kac = sb(st, "kac", [64, 6, 256], BF16); vac = sb(st, "vac", [128, 2, 384], BF16)
            kbc = sb(st, "kbc", [64, 2, 256], BF16); vbc = sb(st, "vbc", [128, 2, 128], BF16)
            kat = sb(st, "kat", [64, 6, 1024], BF16); qat = sb(st, "qat", [64, 6, 512], BF16)
            vat = sb(st, "vat", [128, 8, 384], BF16)
            kbt = sb(st, "kbt", [64, 2, 768], BF16); qbt = sb(st, "qbt", [64, 6, 512], BF16)
            vbt = sb(st, "vbt", [128, 6, 128], BF16)
            pT = [sb(st, f"pT{i}", [128, 512], BF16) for i in range(8)]
            rD = [sb(st, f"rD{i}", [128, 512], F32) for i in range(2)]
            yaT = sb(st, "yaT", [128, 3, 512], BF16); ybT = sb(st, "ybT", [128, 3, 512], BF16)
            ycT = sb(st, "ycT", [128, 2, 512], BF16)
            hcb = sb(st, "hcb", [128, 2, 544], BF16)
            hv = sb(st, "hv", [128, 2, 512], F32); hq = sb(st, "hq", [128, 2, 512], F32)
            mu = sb(st, "mu", [128, 512], F32); var = sb(st, "var", [128, 512], F32)
            gt = [sb(st, f"gt{i}", [128, 3, 512], BF16) for i in range(2)]
            m1 = [sb(st, f"m1{i}", [128, 512], F32) for i in range(2)]
            m2 = [sb(st, f"m2{i}", [128, 512], F32) for i in range(2)]
            m3 = [sb(st, f"m3{i}", [128, 512], F32) for i in range(2)]
            mT = sb(st, "mT", [128, 8, 512], BF16)
            xb = sb(st, "xbm", [128, 8, 512], F32)

            S.dma("pool", swm[:], swmask[:, :], (), ("swm",), max_dma_last_dim=4096)
            S.dma("sp", esk[:], sinkp[l], (), ("esk",))
            S.act(esk[:], esk[:], AF.Exp, ("esk",), ("esk",))
            S.dma("sp", cw[:], convw[l], (), ("cw",)); S.dma("sp", cb[:], convb[l], (), ("cb",))
            S.dma("sp", lg[:], lng[l], (), ("lg",)); S.dma("sp", lb[:], lnb[l], (), ("lb",))
            for c in range(2):
                for w in range(31):
                    S.ts("dve", diag[:, c, w, :], ident_f[:], cw[:, c, w:w + 1], ALU.mult, ("ident_f", "cw"), ("diag",))
            S.dma("sp", kac[:], KA[:, SEQ:NT].rearrange("(h d) t -> d h t", d=64), (), ("kac",))
            S.dma("sp", kbc[:], KB[:, SEQ:NT].rearrange("(h d) t -> d h t", d=64), (), ("kbc",))
            S.dma("sp", vac[:], VA[SEQ:NT, :].rearrange("(c p) f -> p c f", p=128), (), ("vac",))
            S.dma("sp", vbc[:], VB[SEQ:NT, :].rearrange("(c p) f -> p c f", p=128), (), ("vbc",))

            att = {"i": 0, "queue": [], "pend": [], "pair": 0, "cfg": None, "bg": None}
            GRP = 3
            SBANKS = (0, 1, 2, 7)

            def emit_pv_group(items):
                for (p, pk, v_ap, vk, qa, qb, hp, first, last, obank, dbank, fin) in items:
                    nq = qb - qa
                    lo = hp * 64
                    S.mm(ps[obank][lo:lo + 64, qa:qb], v_ap, p[:, :nq], first, last, (vk, pk), (PS(obank),), False)
                    S.mm(ps[dbank][lo:lo + 64, qa:qb], ones_b[:, 0:64], p[:, :nq], first, last, ("ones_b", pk), (PS(dbank),), True)
                    if fin is not None:
                        fin()

            def flush_group():
                grp = att["pend"]
                att["pend"] = []
                if not grp:
                    return
                steps = []
                for (q_ap, qk, chunk, hp, exp_scale, first, last, obank, dbank, fin) in grp:
                    i = att["i"]
                    att["i"] += 1
                    steps.append((SBANKS[i % 4], pT[i % 8], ("pT", i % 8)))
                for (sbk, p, pk), (q_ap, qk, chunk, hp, exp_scale, first, last, obank, dbank, fin) in zip(steps, grp):
                    (k_ap, kk, v_ap, vk, qa, qb, b_ap, bk) = chunk
                    nq = qb - qa
                    S.mm(ps[sbk][:, :nq], k_ap, q_ap[:, qa:qb], True, b_ap is None, (kk, qk), (PS(sbk),), b_ap is None)
                for (sbk, p, pk), (q_ap, qk, chunk, hp, exp_scale, first, last, obank, dbank, fin) in zip(steps, grp):
                    (k_ap, kk, v_ap, vk, qa, qb, b_ap, bk) = chunk
                    nq = qb - qa
                    if b_ap is not None:
                        S.mm(ps[sbk][:, :nq], ident_b[:], b_ap, False, True, ("ident_b", bk), (PS(sbk),), True)
                items = []
                for (sbk, p, pk), (q_ap, qk, chunk, hp, exp_scale, first, last, obank, dbank, fin) in zip(steps, grp):
                    (k_ap, kk, v_ap, vk, qa, qb, b_ap, bk) = chunk
                    nq = qb - qa
                    S.act(p[:, :nq], ps[sbk][:, :nq], AF.Exp, (PS(sbk),), (pk,), scale=exp_scale)
                    items.append((p, pk, v_ap, vk, qa, qb, hp, first, last, obank, dbank, fin))
                if att["queue"]:
                    emit_pv_group(att["queue"])
                att["queue"] = items
                if att["i"] % 12 == 0:
                    pump_conv(1)
                if att["bg"] is not None:
                    if next(att["bg"], "done") == "done":
                        att["bg"] = None

            def att_step(q_ap, qk, chunk, hp, exp_scale, first, last, obank, dbank, fin):
                att["pend"].append((q_ap, qk, chunk, hp, exp_scale, first, last, obank, dbank, fin))
                if len(att["pend"]) >= GRP:
                    flush_group()

            def att_drain():
                flush_group()
                if att["queue"]:
                    emit_pv_group(att["queue"])
                    att["queue"] = []

            def att_pair(T, q_tile, qk, c, chunks_of, exp_scale, dst, dstk, sink_ap):
                pr = att["pair"] % 2
                att["pair"] += 1
                obank, dbank = (3, 4) if pr == 0 else (5, 6)
                r, rk = rD[pr], ("rD", pr)

                def fin():
                    if sink_ap is not None:
                        S.act(r[:, :T], ps[dbank][:, :T], AF.Ln, (PS(dbank), "esk"), (rk,), bias=sink_ap)
                    else:
                        S.act(r[:, :T], ps[dbank][:, :T], AF.Ln, (PS(dbank),), (rk,))
                    S.act(r[:, :T], r[:, :T], AF.Exp, (rk,), (rk,), scale=-1.0)
                    S.tt("dve", dst[:, c, :T], ps[obank][:, :T], r[:, :T], ALU.mult, (PS(obank), rk), (dstk,))

                for hp in range(2):
                    h = 2 * c + hp
                    chunks = chunks_of(h)
                    assert chunks[0][4] == 0 and chunks[0][5] == T
                    for ci, ch in enumerate(chunks):
                        lastc = ci == len(chunks) - 1
                        att_step(q_tile[:, h, :], qk, ch, hp, exp_scale, ci == 0, lastc, obank, dbank,
                                 fin if (lastc and hp == 1) else None)

            def ctx_chunks_a(h, T):
                return [(kac[:, h, cc_ * 128:(cc_ + 1) * 128], "kac", vac[:, cc_, h * 64:(h + 1) * 64], "vac", 0, T, None, None)
                        for cc_ in range(2)]

            def ctx_chunks_b(kv, T):
                return [(kbc[:, kv, cc_ * 128:(cc_ + 1) * 128], "kbc", vbc[:, cc_, kv * 64:(kv + 1) * 64], "vbc", 0, T, None, None)
                        for cc_ in range(2)]

            def load_hcb(T, t0, lim_lo, lim_hi):
                a0 = max(t0 - 15, lim_lo)
                a1 = min(t0 + T + 15, lim_hi)
                if a0 > t0 - 15:
                    S.memset("pool", hcb[:, :, 0:15], 0.0, ("hcb",))
                if a1 < t0 + T + 15:
                    S.memset("pool", hcb[:, :, T + 15:T + 30], 0.0, ("hcb",))
                o0 = a0 - (t0 - 15)
                S.dma("sp", hcb[:, :, o0:o0 + (a1 - a0)], HC[:, a0:a1].rearrange("(c p) t -> p c t", p=128), (), ("hcb",))

            def conv_ln(T):
                for c in range(2):
                    bank = (2, 7)[c]
                    for w in range(31):
                        S.mm(ps[bank][:, :T], diag[:, c, w, :], hcb[:, c, w:w + T], w == 0, w == 30, ("diag", "hcb"),
                             (PS(bank),), w == 30)
                    S.act(hv[:, c, :T], ps[bank][:, :T], AF.Identity, (PS(bank), "cb"), ("hv",), bias=cb[:, c:c + 1])
                S.act(hq[:, :, :T], hv[:, :, :T], AF.Square, ("hv",), ("hq",))
                for c in range(2):
                    S.mm(ps[2][:, :T], ones_f[:], hv[:, c, :T], c == 0, c == 1, ("ones_f", "hv"), (PS(2),), c == 1)
                for c in range(2):
                    S.mm(ps[7][:, :T], ones_f[:], hq[:, c, :T], c == 0, c == 1, ("ones_f", "hq"), (PS(7),), c == 1)
                S.act(mu[:, :T], ps[2][:, :T], AF.Copy, (PS(2),), ("mu",), scale=1.0 / 256)
                S.act(hq[:, 0, :T], ps[7][:, :T], AF.Copy, (PS(7), "hq"), ("hq",), scale=1.0 / 256)
                yield
                S.tt("dve", var[:, :T], mu[:, :T], mu[:, :T], ALU.mult, ("mu",), ("var",))
                S.tt("dve", var[:, :T], hq[:, 0, :T], var[:, :T], ALU.subtract, ("hq", "var"), ("var",))
                S.ts("dve", var[:, :T], var[:, :T], 0.0, ALU.max, ("var",), ("var",))
                yield
                S.act(var[:, :T], var[:, :T], AF.Ln, ("var", "eps_s"), ("var",), bias=eps_s[:, 0:1], scale=1.0)
                yield
                S.act(var[:, :T], var[:, :T], AF.Exp, ("var",), ("var",), scale=-0.5)
                yield
                for c in range(2):
                    S.tt("dve", hv[:, c, :T], hv[:, c, :T], mu[:, :T], ALU.subtract, ("hv", "mu"), ("hv",))
                    S.tt("dve", hv[:, c, :T], hv[:, c, :T], var[:, :T], ALU.mult, ("hv", "var"), ("hv",))
                    yield

            def conv_silu(T):
                for c in range(2):
                    S.act(ycT[:, c, :T], hv[:, c, :T], AF.Silu, ("hv", "lg", "lb"), ("ycT",),
                          bias=lb[:, c:c + 1], scale=lg[:, c:c + 1])

            def merge(T, t0, col):
                Gv = G.rearrange("(b c p) t -> p b c t", b=3, p=128)
                for dc in range(8):
                    g_, gk = gt[dc % 2], ("gt", dc % 2)
                    S.dma("sp", g_[:, :, :T], Gv[:, :, dc, t0:t0 + T], (), (gk,))
                    for k in range(3):
                        S.mm(ps[0][:, :T], woa[:, k, dc * 128:(dc + 1) * 128], yaT[:, k, :T], k == 0, k == 2,
                             ("woa", "yaT"), (PS(0),), k == 2)
                    for k in range(3):
                        S.mm(ps[1][:, :T], wob[:, k, dc * 128:(dc + 1) * 128], ybT[:, k, :T], k == 0, k == 2,
                             ("wob", "ybT"), (PS(1),), k == 2)
                    for k in range(2):
                        S.mm(ps[2][:, :T], woc[:, k, dc * 128:(dc + 1) * 128], ycT[:, k, :T], k == 0, k == 1,
                             ("woc", "ycT"), (PS(2),), k == 1)
                    i = dc % 2
                    S.tt("dve", m1[i][:, :T], ps[0][:, :T], g_[:, 0, :T], ALU.mult, (PS(0), gk), (("m1", i),))
                    S.tt("dve", m2[i][:, :T], ps[1][:, :T], g_[:, 1, :T], ALU.mult, (PS(1), gk), (("m2", i),))
                    S.tt("dve", m3[i][:, :T], ps[2][:, :T], g_[:, 2, :T], ALU.mult, (PS(2), gk), (("m3", i),))
                    S.tt("pool", m1[i][:, :T], m1[i][:, :T], m2[i][:, :T], ALU.add, (("m1", i), ("m2", i)), (("m1", i),))
                    S.tt("pool", mT[:, dc, :T], m1[i][:, :T], m3[i][:, :T], ALU.add, (("m1", i), ("m3", i)), ("mT",))
                for dc in range(8):
                    bank = (7, 3)[dc % 2]
                    for k in range(8):
                        S.mm(ps[bank][:, :T], wo[:, k, dc * 128:(dc + 1) * 128], mT[:, k, :T], k == 0, k == 7,
                             ("wo", "mT"), (PS(bank),), k == 7)
                    S.stt(xb[:, dc, :T], ps[bank][:, :T], Gate[l][:, 1, dc, col:col + 1], xb[:, dc, :T],
                          ALU.mult, ALU.add, (PS(bank), ("Gate", l, 1), "xbm"), ("xbm",))
                S.dma("pool", Xv[:, :, t0:t0 + T], xb[:, :, :T], ("xbm",), ())

            def load_na(tb):
                plan = na_block_plan(tb)
                mlo, mhi = plan[0][0], plan[-1][0]
                nk = (mhi - mlo + 1) * 128
                assert nk <= 1024
                t0 = tb * 512
                if att["cfg"] != na_cfg(tb):
                    att["cfg"] = na_cfg(tb)
                    natv = NATB[l][att["cfg"]].rearrange("p (h q) -> p h q", h=6)
                    S.dma("sp", nat[:], natv, tuple(("NATB", l, att["cfg"], h) for h in range(6)), ("nat",))
                S.dma("sp", kat[:, :, 0:nk], KA[:, mlo * 128:mlo * 128 + nk].rearrange("(h d) t -> d h t", d=64), (), ("kat",))
                S.dma("sp", qat[:], QA[:, t0:t0 + 512].rearrange("(h d) t -> d h t", d=64), (), ("qat",))
                S.dma("sp", vat[:, 0:nk // 128, :], VA[mlo * 128:mlo * 128 + nk, :].rearrange("(c p) f -> p c f", p=128),
                      (), ("vat",))

            def load_sw(tb):
                t0 = tb * 512
                k0 = max(t0 - 128, 0)
                k1 = min(t0 + 640, SEQ)
                S.dma("sp", kbt[:, :, 0:k1 - k0], KB[:, k0:k1].rearrange("(h d) t -> d h t", d=64), (), ("kbt",))
                S.dma("sp", qbt[:], QB[:, t0:t0 + 512].rearrange("(h d) t -> d h t", d=64), (), ("qbt",))
                S.dma("sp", vbt[:, 0:(k1 - k0) // 128, :], VB[k0:k1, :].rearrange("(c p) f -> p c f", p=128), (), ("vbt",))

            for tb in range(8):
                t0 = tb * 512
                plan = na_block_plan(tb)
                mlo, mhi = plan[0][0], plan[-1][0]
                k0 = max(t0 - 128, 0)
                k1 = min(t0 + 640, SEQ)
                if tb == 0:
                    load_na(0)
                    load_sw(0)
                load_hcb(512, t0, 0, SEQ)
                S.dma("sp", xb[:, :, :512], Xv[:, :, t0:t0 + 512], (), ("xbm",))

                def na_chunks_of(h):
                    chunks = ctx_chunks_a(h, 512)
                    for (m, nf, nl, off) in plan:
                        qa_, qb_ = (nf - 4 * tb) * 128, (nl - 4 * tb + 1) * 128
                        chunks.append((kat[:, h, (m - mlo) * 128:(m - mlo + 1) * 128], "kat",
                                       vat[:, m - mlo, h * 64:(h + 1) * 64], "vat", qa_, qb_,
                                       nat[:, h, off:off + (qb_ - qa_)], "nat"))
                    return chunks

                def sw_chunks_of(h):
                    kv = h // 3
                    chunks = ctx_chunks_b(kv, 512)
                    for kb_ in range(max(4 * tb - 1, 0), min(4 * tb + 4, 31) + 1):
                        nlo, nhi = max(kb_ - 1, 4 * tb), min(kb_ + 1, 4 * tb + 3)
                        qa_, qb_ = (nlo - 4 * tb) * 128, (nhi - 4 * tb + 1) * 128
                        off = kb_ * 128 - k0
                        mo = (nlo - (kb_ - 1)) * 128
                        chunks.append((kbt[:, kv, off:off + 128], "kbt", vbt[:, off // 128, kv * 64:(kv + 1) * 64], "vbt",
                                       qa_, qb_, swm[:, mo:mo + (qb_ - qa_)], "swm"))
                    return chunks

                for c in range(3):
                    att_pair(512, qat, "qat", c, na_chunks_of, 1.0, yaT, "yaT", None)
                att_drain()
                if tb + 1 < 8:
                    load_na(tb + 1)
                elif not last:
                    S.dma("sp", qat[:, :, 0:256], QA[:, SEQ:NT].rearrange("(h d) t -> d h t", d=64), (), ("qat",))
                att["bg"] = conv_ln(512)
                next(att["bg"])
                for c in range(3):
                    att_pair(512, qbt, "qbt", c, sw_chunks_of, 0.125, ybT, "ybT", esk[:, c:c + 1])
                att_drain()
                if att["bg"] is not None:
                    run_all(att["bg"])
                    att["bg"] = None
                conv_silu(512)
                if tb + 1 < 8:
                    load_sw(tb + 1)
                elif not last:
                    S.dma("sp", qbt[:, :, 0:256], QB[:, SEQ:NT].rearrange("(h d) t -> d h t", d=64), (), ("qbt",))
                merge(512, t0, 0)

            if not last:
                load_hcb(256, SEQ, SEQ, NT)
                S.dma("sp", xb[:, :, :256], Xv[:, :, SEQ:NT], (), ("xbm",))
                for c in range(3):
                    att_pair(256, qat, "qat", c, lambda h: ctx_chunks_a(h, 256), 1.0, yaT, "yaT", None)
                att_drain()
                att["bg"] = conv_ln(256)
                next(att["bg"])
                for c in range(3):
                    att_pair(256, qbt, "qbt", c, lambda h: ctx_chunks_b(h // 3, 256), 0.125, ybT, "ybT", esk[:, c:c + 1])
                att_drain()
                if att["bg"] is not None:
                    run_all(att["bg"])
                    att["bg"] = None
                conv_silu(256)
                merge(256, SEQ, 1)
        return finish_phase(f"mix{l}")

    def final_phase():
        with contextlib.ExitStack() as st:
            xbs = [sb(st, f"xb{i}", [128, 8, 512], F32) for i in range(2)]
            sq = sb(st, "sq", [128, 8, 512], BF16)
            rs = sb(st, "rs", [128, 512], F32)
            obf = [sb(st, f"obf{i}", [128, 8, 512], F32) for i in range(2)]
            S.dma("sp", xbs[0][:], Xv[:, :, 0:512], (), (("xb", 0),))
            for bi in range(8):
                t0 = bi * 512
                xb, xbk = xbs[bi % 2], ("xb", bi % 2)
                if bi + 1 < 8:
                    S.dma("sp", xbs[(bi + 1) % 2][:], Xv[:, :, t0 + 512:t0 + 1024], (), (("xb", (bi + 1) % 2),))
                S.act(sq[:], xb[:], AF.Square, (xbk,), ("sq",))
                for kc in range(8):
                    S.mm(ps[6][:, :], ones_b[:], sq[:, kc, :], kc == 0, kc == 7, ("ones_b", "sq"), (PS(6),), kc == 7)
                S.act(rs[:], ps[6][:, :], AF.Sqrt, (PS(6), "eps_s"), ("rs",), bias=eps_s[:, 0:1], scale=1.0 / D)
                S.recip(rs[:], rs[:], ("rs",), ("rs",))
                o, ok = obf[bi % 2], ("obf", bi % 2)
                for kc in range(8):
                    S.stt(o[:, kc, :], xb[:, kc, :], finalg_s[:, kc:kc + 1], rs[:], ALU.mult, ALU.mult,
                          (xbk, "finalg", "rs"), (ok,))
                S.dma("pool", outv[:, :, t0:t0 + 512], o[:], (ok,), ())
        S.barrier()

    def program():
        if done["flag"]:
            return
        for l in range(DEPTH):
            last = l == DEPTH - 1
            if ffn_phase(l, 0, (BLOCKS[1:] + BLOCKS[:1]) if l == 0 else BLOCKS):
                return
            if l == 0:
                ada_stack.close()
                ada_state["closed"] = True
            with contextlib.ExitStack() as lst:
                load_out_weights(l, lst)
                if proj_phase(l):
                    return
                if mix_phase(l, last):
                    return
            if ffn_phase(l, 1, ([(0, 256, 0), (256, 768, 0)] + BLOCKS[2:]) if last else BLOCKS, fuse_final=last):
                return
        pass

    program()
    S.barrier()
    S.emit()
    if not ada_state.get("closed"):
        ada_stack.close()
    stack0.close()
    return nc


_CACHE = {}


def kernel(**inputs):
    shared = prep_shared(inputs)
    if "nc" not in _CACHE:
        _CACHE["nc"] = build()
    nc = _CACHE["nc"]
    in_maps = []
    for b in range(8):
        m = dict(shared)
        m.update(prep_core(inputs, b))
        in_maps.append(m)
    res = run_bass_kernel_spmd(nc, in_maps, core_ids=list(range(8)))
    out = np.stack([np.ascontiguousarray(np.asarray(r["outT"], dtype=np.float32).T) for r in res.results])
    return out
```
